# Optimizing a Trainium2 kernel written in Bass

```python
import math
import jax, jax.numpy as jnp
from jax import lax
import numpy as np

D_MODEL = 1024
BATCH = 8
SEQ = 2048
DEPTH = 1
DEC_BATCH = 128
DEC_SEQ = 8
PAST_LEN = 16384
PAGE_SIZE = 128

D_MIX = D_MODEL
A_WIDTH = D_MIX // 2
A_HEAD = 64
A_HEADS = A_WIDTH // A_HEAD
A_DECAY_LORA = 64
A_AAA_LORA = 64
A_GATE_LORA = 128
A_SHIFT_W = 3 * A_WIDTH + A_DECAY_LORA + A_AAA_LORA + A_GATE_LORA
B_WIDTH = D_MIX - A_WIDTH
B_HEAD = 128
B_HEADS = B_WIDTH // B_HEAD
B_CONV = 4
B_QKV = 3 * B_WIDTH
B_CHUNK = 64
B_PROJ_W = B_QKV + B_WIDTH + 2 * B_HEADS
IN_W = A_SHIFT_W + B_PROJ_W
D_FF = 2816
FFN_CONV = 3
PLE_DIM = 256
DN_ALPHA = (2.0 * DEPTH) ** 0.25
DN_BETA = (8.0 * DEPTH) ** -0.25
LN_EPS = 1e-5
GN_EPS = 64e-5
RMS_EPS = 1e-6
L2_EPS = 1e-12

kernel_name = 'hybrid_rwkv7_gdn_convffn_step'

F32 = jnp.float32


def _split_at(t, sizes):
    idx = np.cumsum(sizes)[:-1].tolist()
    return jnp.split(t, idx, axis=-1)


def _heads(t, n_heads):
    return t.reshape(t.shape[:-1] + (n_heads, t.shape[-1] // n_heads))


def _layer_norm(x, g, b):
    xf = x.astype(F32)
    mu = xf.mean(-1, keepdims=True)
    var = jnp.square(xf - mu).mean(-1, keepdims=True)
    return ((xf - mu) * lax.rsqrt(var + LN_EPS) * g + b).astype(x.dtype)


def _rms_norm(x, g):
    xf = x.astype(F32)
    return (xf * lax.rsqrt(jnp.square(xf).mean(-1, keepdims=True) + RMS_EPS) * g).astype(x.dtype)


def _l2norm(x):
    xf = x.astype(F32)
    return xf * lax.rsqrt(jnp.square(xf).sum(-1, keepdims=True) + L2_EPS)


def _causal_dwconv(buf, x, w):
    K = w.shape[0]
    T = x.shape[1]
    full = jnp.concatenate([buf.astype(x.dtype), x], axis=1)
    y = sum(full[:, j:j + T] * w[j] for j in range(K))
    return y, full[:, full.shape[1] - (K - 1):]


def _rwkv7_mix(u, shift_buf, wkv0, mu, w0, w_w2, a0, w_a2, w_g2, k_k, k_a, r_k, gn_g, gn_b):
    bsz, T, _ = u.shape
    prev = jnp.concatenate([shift_buf[:, None, :].astype(u.dtype), u[:, :-1]], axis=1)
    xs = u + (prev - u) * mu
    r, k, v, wd, ad, gd = _split_at(xs, (A_WIDTH, A_WIDTH, A_WIDTH, A_DECAY_LORA, A_AAA_LORA, A_GATE_LORA))
    w = -jax.nn.softplus(-(w0 + jnp.tanh(wd) @ w_w2)) - 0.5
    decay = jnp.exp(-jnp.exp(w.astype(F32)))
    a = jax.nn.sigmoid(a0 + ad @ w_a2)
    g = jax.nn.sigmoid(gd) @ w_g2
    kk = _l2norm(_heads(k * k_k, A_HEADS))
    k = k * (1.0 + (a - 1.0) * k_a)
    r_h, k_h, v_h, a_h, w_h = (_heads(t, A_HEADS).astype(F32) for t in (r, k, v, a, decay))

    def step(S, inp):
        r_t, w_t, k_t, v_t, kk_t, a_t = inp
        sa = jnp.einsum('bhvk,bhk->bhv', S, -kk_t)
        S = S * w_t[:, :, None, :] + sa[..., None] * (kk_t * a_t)[:, :, None, :] + v_t[..., None] * k_t[:, :, None, :]
        return S, jnp.einsum('bhvk,bhk->bhv', S, r_t)

    seq = tuple(jnp.swapaxes(t, 0, 1) for t in (r_h, w_h, k_h, v_h, kk, a_h))
    S_fin, o = lax.scan(step, wkv0.astype(F32), seq)
    o = jnp.swapaxes(o, 0, 1)
    mean = o.mean(-1, keepdims=True)
    var = jnp.square(o - mean).mean(-1, keepdims=True)
    o = (o - mean) * lax.rsqrt(var + GN_EPS) * gn_g.reshape(A_HEADS, A_HEAD) + gn_b.reshape(A_HEADS, A_HEAD)
    bonus = (r_h * k_h * r_k.reshape(A_HEADS, A_HEAD)).sum(-1, keepdims=True) * v_h
    o = (o + bonus).reshape(bsz, T, A_WIDTH) * g
    return o.astype(u.dtype), u[:, -1].astype(shift_buf.dtype), S_fin.astype(wkv0.dtype)


def _chunk_gated_delta(q, k, v, beta, g, S0):
    bsz, T, H, Dk = q.shape
    Dv = v.shape[-1]
    C = min(B_CHUNK, T)
    n = -(-T // C)
    pad = n * C - T

    def prep(t):
        t = jnp.pad(t.astype(F32), [(0, 0), (0, pad)] + [(0, 0)] * (t.ndim - 2))
        t = t.reshape((bsz, n, C) + t.shape[2:])
        return jnp.moveaxis(jnp.swapaxes(t, 2, 3), 1, 0)

    q, k, v, beta, g = (prep(t) for t in (q, k, v, beta, g))
    G = jnp.cumsum(g, axis=-1)
    diff = G[..., :, None] - G[..., None, :]
    incl = jnp.tril(jnp.ones((C, C), bool))
    strict = jnp.tril(jnp.ones((C, C), bool), -1)
    dec_incl = jnp.exp(jnp.where(incl, diff, -jnp.inf))
    dec_strict = jnp.where(strict, dec_incl, 0.0)
    kb = k * beta[..., None]
    Lmat = jnp.einsum('nbhid,nbhjd->nbhij', kb, k) * dec_strict
    eye = jnp.eye(C, dtype=F32)
    rhs = jnp.concatenate([v * beta[..., None], kb * jnp.exp(G)[..., None]], axis=-1)
    sol = lax.linalg.triangular_solve(eye + Lmat, rhs, left_side=True, lower=True, unit_diagonal=True)
    u_val, w_k = sol[..., :Dv], sol[..., Dv:]
    attn_in = jnp.einsum('nbhid,nbhjd->nbhij', q, k) * dec_incl
    q_dec = q * jnp.exp(G)[..., None]
    k_dec = k * jnp.exp(G[..., -1:] - G)[..., None]
    g_last = jnp.exp(G[..., -1])

    def step(S, inp):
        u_c, w_c, a_c, q_c, k_c, gl = inp
        v_new = u_c - jnp.einsum('bhcd,bhdv->bhcv', w_c, S)
        o = jnp.einsum('bhcd,bhdv->bhcv', q_c, S) + jnp.einsum('bhij,bhjv->bhiv', a_c, v_new)
        S = S * gl[..., None, None] + jnp.einsum('bhcd,bhcv->bhdv', k_c, v_new)
        return S, o

    S_fin, o = lax.scan(step, S0.astype(F32), (u_val, w_k, attn_in, q_dec, k_dec, g_last))
    o = jnp.swapaxes(jnp.moveaxis(o, 0, 1), 2, 3).reshape(bsz, n * C, H, Dv)[:, :T]
    return o, S_fin


def _gated_delta_mix(proj, conv_buf, S0, conv_w, a_log, dt_bias, norm_g):
    bsz, T, _ = proj.shape
    qkv, z, b, a = _split_at(proj, (B_QKV, B_WIDTH, B_HEADS, B_HEADS))
    qkv_c, new_buf = _causal_dwconv(conv_buf, qkv, conv_w)
    qkv_c = jax.nn.silu(qkv_c)
    q, k, v = (_heads(t, B_HEADS) for t in jnp.split(qkv_c, 3, axis=-1))
    q = _l2norm(q) * (B_HEAD ** -0.5)
    k = _l2norm(k)
    beta = jax.nn.sigmoid(b.astype(F32))
    g = -jnp.exp(a_log) * jax.nn.softplus(a.astype(F32) + dt_bias)
    o, S_fin = _chunk_gated_delta(q, k, v, beta, g, S0)
    o = _rms_norm(o, norm_g) * jax.nn.silu(_heads(z, B_HEADS).astype(F32))
    return o.reshape(bsz, T, B_WIDTH).astype(proj.dtype), new_buf.astype(conv_buf.dtype), S_fin.astype(S0.dtype)


def _conv_ffn(x, buf, w_up, conv_w, conv_b, w_down):
    gate, up = jnp.split(x @ w_up, 2, axis=-1)
    gate_c, new_buf = _causal_dwconv(buf, gate, conv_w)
    y = (jax.nn.silu(gate_c + conv_b) * up) @ w_down
    return y, new_buf.astype(buf.dtype)


def _layer(x, p, st, lw):
    (w_in, a_mu, a_w0, a_w_w2, a_a0, a_w_a2, a_w_g2, a_k_k, a_k_a, a_r_k, a_gn_g, a_gn_b,
     b_conv_w, b_a_log, b_dt_bias, b_norm_g, w_o, ln1_g, ln1_b, w_up, f_conv_w, f_conv_b,
     w_down, ln2_g, ln2_b, w_ple, ple_g, w_ple_gate) = lw
    a_wkv, a_shift, b_ssm, b_conv, f_conv = st
    proj = x @ w_in
    o_a, a_shift_new, a_wkv_new = _rwkv7_mix(proj[..., :A_SHIFT_W], a_shift, a_wkv, a_mu, a_w0, a_w_w2,
                                             a_a0, a_w_a2, a_w_g2, a_k_k, a_k_a, a_r_k, a_gn_g, a_gn_b)
    o_b, b_conv_new, b_ssm_new = _gated_delta_mix(proj[..., A_SHIFT_W:], b_conv, b_ssm, b_conv_w,
                                                  b_a_log, b_dt_bias, b_norm_g)
    mix = jnp.concatenate([o_a, o_b], axis=-1) @ w_o
    x = _layer_norm(DN_ALPHA * x + mix, ln1_g, ln1_b)
    ffn, f_conv_new = _conv_ffn(x, f_conv, w_up, f_conv_w, f_conv_b, w_down)
    x = _layer_norm(DN_ALPHA * x + ffn, ln2_g, ln2_b)
    e = _rms_norm(p @ w_ple, ple_g)
    x = x + jax.nn.sigmoid(x @ w_ple_gate) * e
    return x, (a_wkv_new, a_shift_new, b_ssm_new, b_conv_new, f_conv_new)


def _trunk(x, p, states, weights):
    new = []
    for i in range(DEPTH):
        x, st = _layer(x, p[i], tuple(s[i] for s in states), tuple(w[i] for w in weights))
        new.append(st)
    stacked = tuple(jnp.stack([st[j] for st in new]) for j in range(len(states)))
    return x, stacked


def setup_inputs(seed: int = 0) -> dict:
    key = jax.random.key(seed)
    ks = iter(jax.random.split(key, 64))
    nrm = lambda shape, scale: scale * jax.random.normal(next(ks), shape, F32)
    uni = lambda shape, lo, hi: jax.random.uniform(next(ks), shape, F32, lo, hi)
    L = DEPTH
    dt = jnp.exp(uni((L, B_HEADS), math.log(1e-3), math.log(1e-1)))
    return {
        'x_prompt': nrm((BATCH, SEQ, D_MODEL), 1.0),
        'x_sample': nrm((DEC_BATCH, DEC_SEQ, D_MODEL), 1.0),
        'p_prompt': nrm((L, BATCH, SEQ, PLE_DIM), 1.0),
        'p_sample': nrm((L, DEC_BATCH, DEC_SEQ, PLE_DIM), 1.0),
        'state_a_wkv': nrm((L, DEC_BATCH, A_HEADS, A_HEAD, A_HEAD), 0.3),
        'state_a_shift': nrm((L, DEC_BATCH, A_SHIFT_W), 1.0),
        'state_b_ssm': nrm((L, DEC_BATCH, B_HEADS, B_HEAD, B_HEAD), 0.1),
        'state_b_conv': nrm((L, DEC_BATCH, B_CONV - 1, B_QKV), 1.0),
        'state_ffn_conv': nrm((L, DEC_BATCH, FFN_CONV - 1, D_FF), 1.0),
        'w_in': nrm((L, D_MODEL, IN_W), D_MODEL ** -0.5),
        'a_mu': uni((L, A_SHIFT_W), 0.0, 1.0),
        'a_w0': uni((L, A_WIDTH), -6.0, -1.0),
        'a_w_w2': nrm((L, A_DECAY_LORA, A_WIDTH), 0.1),
        'a_a0': nrm((L, A_WIDTH), 0.1),
        'a_w_a2': nrm((L, A_AAA_LORA, A_WIDTH), 0.5 * A_AAA_LORA ** -0.5),
        'a_w_g2': nrm((L, A_GATE_LORA, A_WIDTH), A_GATE_LORA ** -0.5),
        'a_k_k': 0.85 + nrm((L, A_WIDTH), 0.02),
        'a_k_a': 1.0 + nrm((L, A_WIDTH), 0.02),
        'a_r_k': nrm((L, A_WIDTH), 0.1),
        'a_gn_g': 1.0 + nrm((L, A_WIDTH), 0.02),
        'a_gn_b': nrm((L, A_WIDTH), 0.02),
        'b_conv_w': nrm((L, B_CONV, B_QKV), 0.5),
        'b_a_log': jnp.log(uni((L, B_HEADS), 1.0, 16.0)),
        'b_dt_bias': dt + jnp.log(-jnp.expm1(-dt)),
        'b_norm_g': 1.0 + nrm((L, B_HEAD), 0.02),
        'w_o': nrm((L, D_MIX, D_MODEL), DN_BETA * D_MIX ** -0.5),
        'ln1_g': 1.0 + nrm((L, D_MODEL), 0.02),
        'ln1_b': nrm((L, D_MODEL), 0.02),
        'w_up': nrm((L, D_MODEL, 2 * D_FF), D_MODEL ** -0.5),
        'f_conv_w': nrm((L, FFN_CONV, D_FF), FFN_CONV ** -0.5),
        'f_conv_b': nrm((L, D_FF), 0.02),
        'w_down': nrm((L, D_FF, D_MODEL), DN_BETA * D_FF ** -0.5),
        'ln2_g': 1.0 + nrm((L, D_MODEL), 0.02),
        'ln2_b': nrm((L, D_MODEL), 0.02),
        'w_ple': nrm((L, PLE_DIM, D_MODEL), PLE_DIM ** -0.5),
        'ple_g': 1.0 + nrm((L, D_MODEL), 0.02),
        'w_ple_gate': nrm((L, D_MODEL, D_MODEL), D_MODEL ** -0.5),
    }


def reference(x_prompt, x_sample, p_prompt, p_sample, state_a_wkv, state_a_shift, state_b_ssm,
              state_b_conv, state_ffn_conv, w_in, a_mu, a_w0, a_w_w2, a_a0, a_w_a2, a_w_g2, a_k_k,
              a_k_a, a_r_k, a_gn_g, a_gn_b, b_conv_w, b_a_log, b_dt_bias, b_norm_g, w_o, ln1_g,
              ln1_b, w_up, f_conv_w, f_conv_b, w_down, ln2_g, ln2_b, w_ple, ple_g, w_ple_gate):
    weights = (w_in, a_mu, a_w0, a_w_w2, a_a0, a_w_a2, a_w_g2, a_k_k, a_k_a, a_r_k, a_gn_g, a_gn_b,
               b_conv_w, b_a_log, b_dt_bias, b_norm_g, w_o, ln1_g, ln1_b, w_up, f_conv_w, f_conv_b,
               w_down, ln2_g, ln2_b, w_ple, ple_g, w_ple_gate)
    bp = x_prompt.shape[0]
    zeros = lambda *s: jnp.zeros((DEPTH, bp) + s, x_prompt.dtype)
    prompt_init = (zeros(A_HEADS, A_HEAD, A_HEAD), zeros(A_SHIFT_W), zeros(B_HEADS, B_HEAD, B_HEAD),
                   zeros(B_CONV - 1, B_QKV), zeros(FFN_CONV - 1, D_FF))
    y_prompt, (pa_wkv, pa_shift, pb_ssm, pb_conv, pf_conv) = _trunk(x_prompt, p_prompt, prompt_init, weights)
    sample_init = (state_a_wkv, state_a_shift, state_b_ssm, state_b_conv, state_ffn_conv)
    y_sample, (sa_wkv, sa_shift, sb_ssm, sb_conv, sf_conv) = _trunk(x_sample, p_sample, sample_init, weights)
    return (y_prompt, y_sample, pa_wkv, pa_shift, pb_ssm, pb_conv, pf_conv,
            sa_wkv, sa_shift, sb_ssm, sb_conv, sf_conv)
```

```python
import numpy as np
import concourse.bass as bass
import concourse.mybir as mybir

F32 = mybir.dt.float32
BF16 = mybir.dt.bfloat16
AF = mybir.ActivationFunctionType
ALU = mybir.AluOpType
AX = mybir.AxisListType

ENGS = ['pe', 'act', 'dve', 'pool', 'sp']
NDMASEM = 40


ARENA_ALLOCS = []


def _box(ap):
    t = ap.tensor
    name = t.name
    pairs = [tuple(x) for x in ap.ap]
    space = str(ap.space)
    sz = mybir.dt.size(ap.dtype)
    if 'DRAM' in space.upper() or 'HBM' in space.upper():
        lo = ap.offset
        hi = lo
        for st, n in pairs:
            hi += abs(st) * (n - 1)
        return (name, 0, 1, lo * sz, (hi + 1) * sz)
    shp = list(t.shape)
    pstride = 1
    for s_ in shp[1:]:
        pstride *= s_
    p0 = ap.offset // pstride
    f0 = ap.offset % pstride
    np_ = pairs[0][1]
    ext = 0
    for st, n in pairs[1:]:
        ext += abs(st) * (n - 1)
    b0 = f0 * sz
    b1 = (f0 + ext + 1) * sz
    if 'PSUM' in space.upper() or type(t).__name__.startswith('PSum'):
        b0 = (b0 // 2048) * 2048
        b1 = ((b1 + 2047) // 2048) * 2048
    if name == 'arena':
        for (lo, hi, key) in ARENA_ALLOCS:
            if lo <= b0 < hi:
                if b1 > hi:
                    raise RuntimeError('access spans allocations: %s' % key)
                return (key, p0, p0 + np_, b0, b1)
        raise RuntimeError('arena access outside allocations')
    return (name, p0, p0 + np_, b0, b1)


def _ovl(a, b):
    return a[1] < b[2] and b[1] < a[2] and a[3] < b[4] and b[3] < a[4]


def _covers(a, b):
    return a[1] <= b[1] and a[2] >= b[2] and a[3] <= b[3] and a[4] >= b[4]


class _Op:
    __slots__ = ('eng', 'idx', 'sem', 'val', 'clock')


class Prog:
    def __init__(self, nc, same_engine_dist=2):
        self.nc = nc
        self.streams = {e: [] for e in ENGS}
        self.nops = {e: 0 for e in ENGS}
        self.know = {e: {} for e in ENGS}
        self.acc = {}
        self.dma_last = [None] * NDMASEM
        self.dma_cnt = [0] * NDMASEM
        self.dma_rr = 0
        self.sed = same_engine_dist
        self.nwaits = 0
        self.last_op = {}
        self.direct = {e: {} for e in ENGS}

    def _deps(self, reads, writes, eng=None):
        deps = []
        self._raw = set()
        for ap in reads:
            b = _box(ap)
            isps = b[0].startswith('ps')
            for (bb, op, w) in self.acc.get(b[0], ()):
                if (w or (isps and op.eng != eng)) and _ovl(b, bb):
                    deps.append(op)
                    self._raw.add(id(op))
        for ap in writes:
            b = _box(ap)
            for (bb, op, w) in self.acc.get(b[0], ()):
                if _ovl(b, bb):
                    deps.append(op)
        return deps

    def _record(self, op, reads, writes):
        for ap in writes:
            b = _box(ap)
            lst = self.acc.setdefault(b[0], [])
            lst[:] = [x for x in lst if not _covers(b, x[0])]
            lst.append((b, op, True))
        for ap in reads:
            b = _box(ap)
            lst = self.acc.setdefault(b[0], [])
            lst[:] = [x for x in lst if not ((not x[2]) and x[1].eng == op.eng and x[1].sem == op.sem and _covers(b, x[0]))]
            lst.append((b, op, False))
            if len(lst) > 400:
                raise RuntimeError('access list too long for ' + b[0])

    def _wait_for(self, eng, deps, my_idx, is_dma=False):
        kn = self.know[eng]
        for d in sorted(deps, key=lambda o: -o.val):
            if d.eng == eng and d.sem == ('E', eng):
                if eng == 'pe':
                    continue
                pass
            if d.sem[0] == 'D':
                if self.direct[eng].get(d.sem, 0) >= d.val:
                    continue
                self.direct[eng][d.sem] = d.val
            elif kn.get(d.sem, 0) >= d.val:
                continue
            self.streams[eng].append(('wait', d.sem, d.val))
            self.nwaits += 1
            for k, v in d.clock.items():
                if kn.get(k, 0) < v:
                    kn[k] = v

    def op(self, eng, fn, reads=(), writes=()):
        deps = self._deps(reads, writes, eng)
        idx = self.nops[eng]
        self._wait_for(eng, deps, idx)
        o = _Op()
        o.eng = eng
        o.idx = idx
        o.sem = ('E', eng)
        o.val = idx + 1
        self.nops[eng] = idx + 1
        o.clock = dict(self.know[eng])
        o.clock[o.sem] = o.val
        self.streams[eng].append(('op', fn, o.sem, 1))
        self.last_op[eng] = o
        self._record(o, reads, writes)
        return o

    def dma(self, eng, out, in_, **kw):
        reads = [in_]
        writes = [out]
        deps = self._deps(reads, writes)
        kn = self.know[eng]
        pick = None
        half = NDMASEM // 2
        base = half if eng == 'pool' else 0
        if not hasattr(self, 'dma_rrs'):
            self.dma_rrs = {0: 0, half: 0}
        rr = self.dma_rrs[base]
        for k in range(half):
            j = base + (rr + k) % half
            last = self.dma_last[j]
            if last is None or kn.get(last.sem, 0) >= last.val:
                pick = j
                break
        if pick is None:
            pick = base + rr % half
            deps = deps + [self.dma_last[pick]]
        self.dma_rrs[base] = (pick - base + 1) % half
        self._wait_for(eng, deps, self.nops[eng], True)
        o = _Op()
        o.eng = eng
        o.idx = -1
        o.sem = ('D', pick)
        self.dma_cnt[pick] += 16
        o.val = self.dma_cnt[pick]
        o.clock = dict(kn)
        o.clock[o.sem] = o.val
        self.dma_last[pick] = o
        self.streams[eng].append(('dma', (out, in_, kw), o.sem, 16))
        self._record(o, reads, writes)
        return o

    def barrier(self):
        deps = [o for o in self.last_op.values()] + [o for o in self.dma_last if o is not None]
        for e in ENGS:
            kn = self.know[e]
            for d in sorted(deps, key=lambda o: -o.val):
                if d.sem == ('E', e):
                    continue
                if kn.get(d.sem, 0) >= d.val:
                    continue
                self.streams[e].append(('wait', d.sem, d.val))
                for k_, v in d.clock.items():
                    if kn.get(k_, 0) < v:
                        kn[k_] = v
        self.acc = {}

    def final_wait_all_dma(self, eng='sp'):
        for j in range(NDMASEM):
            last = self.dma_last[j]
            if last is not None and self.know[eng].get(last.sem, 0) < last.val:
                self.streams[eng].append(('wait', last.sem, last.val))
                self.know[eng][last.sem] = last.val

    def emit(self):
        nc = self.nc
        import contextlib
        with contextlib.ExitStack() as st:
            sems = {}
            for e in ENGS:
                sems[('E', e)] = st.enter_context(nc.semaphore('s_' + e))
            for j in range(NDMASEM):
                if self.dma_last[j] is not None:
                    sems[('D', j)] = st.enter_context(nc.semaphore('d_%d' % j))
            block = st.enter_context(nc.Block())

            def run(engname):
                def body(eng):
                    for item in self.streams[engname]:
                        if item[0] == 'wait':
                            eng.wait_ge(sems[item[1]], item[2])
                        elif item[0] == 'op':
                            item[1](eng).then_inc(sems[item[2]], item[3])
                        else:
                            out, in_, kw = item[1]
                            eng.dma_start(out=out, in_=in_, **kw).then_inc(sems[item[2]], item[3])
                return body

            block.tensor(run('pe'))
            block.scalar(run('act'))
            block.vector(run('dve'))
            block.gpsimd(run('pool'))
            block.sync(run('sp'))


import contextlib

DN_ALPHA = 2.0 ** 0.25
LN_EPS = 1e-5
GN_EPS = 64e-5
RMS_EPS = 1e-6
L2_EPS = 1e-12
NT = 17
TOK = 2176


class KB:
    def __init__(self, nc, P, st):
        self.nc, self.P, self.st = nc, P, st
        self.d = {}

    def din(self, name, shape, dt=F32):
        self.d[name] = self.nc.dram_tensor(name, list(shape), dt, kind='ExternalInput').ap()
        return self.d[name]

    def dout(self, name, shape, dt=F32):
        self.d[name] = self.nc.dram_tensor(name, list(shape), dt, kind='ExternalOutput').ap()
        return self.d[name]

    def dint(self, name, shape, dt=F32):
        self.d[name] = self.nc.dram_tensor(name, list(shape), dt, kind='Internal').ap()
        return self.d[name]

    def init_arena(self, nfp32=53200):
        self.arena = self.st.enter_context(self.nc.sbuf_tensor('arena', [128, nfp32], F32))
        self.acap = nfp32
        self.acur = 0
        ARENA_ALLOCS.clear()

    def arena_reset(self, cur):
        ARENA_ALLOCS[:] = [a for a in ARENA_ALLOCS if a[1] <= cur * 4]
        self.acur = cur

    def sb(self, name, shape, dt=F32):
        shape = list(shape)
        n = 1
        for s_ in shape[1:]:
            n *= s_
        nb = n * mybir.dt.size(dt)
        nf = (nb + 3) // 4
        if self.acur + nf > self.acap:
            raise RuntimeError('arena overflow at %s: need %d have %d' % (name, nf, self.acap - self.acur))
        v = self.arena[:, self.acur:self.acur + nf]
        ARENA_ALLOCS.append((self.acur * 4, (self.acur + nf) * 4, name))
        self.acur += nf
        if dt != F32:
            v = v.bitcast(dt)
            v = v[:, 0:n]
        if len(shape) == 3:
            v = v.rearrange('p (a b) -> p a b', a=shape[1], b=shape[2])
        elif len(shape) == 4:
            v = v.rearrange('p (a b c) -> p a b c', a=shape[1], b=shape[2], c=shape[3])
        return v

    def ps(self, name, shape, dt=F32):
        return self.st.enter_context(self.nc.psum_tensor(name, list(shape), dt))

    def mm(self, out, lhsT, rhs, start=True, stop=True):
        self.P.op('pe', lambda e: e.matmul(out, lhsT, rhs, start=start, stop=stop), reads=[lhsT, rhs], writes=[out])

    def tt(self, eng, out, a, b, op):
        self.P.op(eng, lambda e: e.tensor_tensor(out=out, in0=a, in1=b, op=op), reads=[a, b], writes=[out])

    def ts(self, eng, out, a, s1, op0, s2=None, op1=None):
        rd = [a] + [s for s in (s1, s2) if not isinstance(s, (int, float, type(None)))]
        if op1 is None:
            self.P.op(eng, lambda e: e.tensor_scalar(out=out, in0=a, scalar1=s1, scalar2=None, op0=op0), reads=rd, writes=[out])
        else:
            self.P.op(eng, lambda e: e.tensor_scalar(out=out, in0=a, scalar1=s1, scalar2=s2, op0=op0, op1=op1), reads=rd, writes=[out])

    def stt(self, eng, out, in0, scalar, in1, op0, op1):
        eng = 'dve'
        rd = [in0, in1] + ([] if isinstance(scalar, (int, float)) else [scalar])
        self.P.op(eng, lambda e: e.scalar_tensor_tensor(out=out, in0=in0, scalar=scalar, in1=in1, op0=op0, op1=op1), reads=rd, writes=[out])

    def act(self, out, in_, func, bias=0.0, scale=1.0):
        rd = [in_] + [s for s in (bias, scale) if not isinstance(s, (int, float))]
        self.P.op('act', lambda e: e.activation(out=out, in_=in_, func=func, bias=bias, scale=scale), reads=rd, writes=[out])

    def cp(self, eng, out, in_):
        if eng == 'dve':
            eng = 'act'
        if eng == 'act':
            self.P.op('act', lambda e: e.copy(out=out, in_=in_), reads=[in_], writes=[out])
        else:
            self.P.op(eng, lambda e: e.tensor_copy(out=out, in_=in_), reads=[in_], writes=[out])

    def memset(self, eng, ap, v):
        self.P.op(eng, lambda e: e.memset(ap, v), reads=[], writes=[ap])

    def dma(self, eng, out, in_, **kw):
        self.P.dma(eng, out, in_, **kw)

    def rsqrt(self, out, in_, eps):
        self.act(out, in_, AF.Ln, bias=eps)
        self.act(out, out, AF.Exp, scale=-0.5)

    def rownorm_stats(self, src, mv, stats):
        for c in range(2):
            o = stats[:, c, :]
            i = src[:, c * 512:(c + 1) * 512]
            self.P.op('dve', lambda e, o=o, i=i: e.bn_stats(out=o, in_=i), reads=[i], writes=[o])
        sv = stats[:, :, :]
        self.P.op('dve', lambda e: e.bn_aggr(out=mv, in_=sv), reads=[sv], writes=[mv])


def phase_B(k, cfg):
    d = k.d
    wup = k.sb('wup', [128, 8, 5632], BF16)
    wdn = k.sb('wdn', [128, 22, 1024], BF16)
    wpg = k.sb('wpg', [128, 8, 1024], BF16)
    wpl = k.sb('wpl', [128, 2, 1024], BF16)
    for kc in range(8):
        k.dma('pool', wup[:, kc, :], d['w_up'][kc])
    for kc in range(22):
        k.dma('pool', wdn[:, kc, :], d['w_down'][kc])
    for kc in range(8):
        k.dma('pool', wpg[:, kc, :], d['w_ple_gate'][kc])
    for kc in range(2):
        k.dma('pool', wpl[:, kc, :], d['w_ple'][kc])
    cw = k.sb('f_cw', [128, 22, 3])
    cb = k.sb('f_cb', [128, 22])
    k.dma('sp', cw[:], d['f_conv_w'])
    k.dma('sp', cb[:], d['f_conv_b'])
    l2g = k.sb('l2g', [128, 1024]); l2b = k.sb('l2b', [128, 1024]); plg = k.sb('plg', [128, 1024])
    k.dma('sp', l2g[:], d['ln2_g'].partition_broadcast(128))
    k.dma('sp', l2b[:], d['ln2_b'].partition_broadcast(128))
    k.dma('sp', plg[:], d['ple_g'].partition_broadcast(128))
    identF = cfg['identF']
    prev2 = k.sb('prev2', [128, 22, 2])
    k.memset('pool', prev2[:], 0.0)
    sfc_in = k.sb('sfc_in', [128, 22, 16, 2])
    k.dma('sp', sfc_in[:], d['s_fconv'])
    sfc_out = sfc_in
    x1ts = [k.sb('b_x1t%d' % j, [128, 1024]) for j in range(2)]
    x1Ts = [k.sb('b_x1T%d' % j, [128, 8, 128], BF16) for j in range(2)]
    pTt = k.sb('b_pT', [128, 2, 128], BF16)
    gb = [k.sb('b_gb%d' % i, [128, 4, 130]) for i in range(2)]
    gs1 = k.sb('b_gs1', [128, 4, 128]); gs2 = k.sb('b_gs2', [128, 4, 128])
    tmpc = [k.sb('b_tmpc%d' % i, [128, 4, 128]) for i in range(2)]
    hhT = k.sb('b_hhT', [128, 22, 128], BF16)
    z2 = k.sb('b_z2', [128, 1024])
    ebuf = k.sb('b_e', [128, 1024])
    stats = k.sb('b_stats', [128, 2, 6]); mv = k.sb('b_mv', [128, 2]); rstd = k.sb('b_rstd', [128, 1])
    stats2 = k.sb('b_stats2', [128, 2, 6]); mv2 = k.sb('b_mv2', [128, 2]); rstd2 = k.sb('b_rstd2', [128, 1])
    psA, psB, psC, psD = cfg['psA'], cfg['psB'], cfg['psC'], cfg['psD']

    tl = list(cfg.get('tiles', range(NT)))

    def head(n):
        r = tl[n] * 128
        xt_, xT_ = x1ts[n % 2], x1Ts[n % 2]
        k.dma('sp', xt_[:], d['x1s'][r:r + 128, :])
        for kc in range(8):
            k.mm(psA[:, kc * 128:(kc + 1) * 128], xt_[:, kc * 128:(kc + 1) * 128], identF[:])
        k.cp('act', xT_[:].rearrange('p a b -> p (a b)'), psA[:, :])

    for i in tl:
        samp = (i == NT - 1)
        r0 = i * 128
        n_ = tl.index(i)
        x1t = x1ts[n_ % 2]
        x1T = x1Ts[n_ % 2]
        x2T = x1T
        sg = x1t
        if n_ == 0:
            head(0)
        for kc in range(2):
            k.dma('pool', pTt[:, kc, :], d['pT'][kc, :, r0:r0 + 128])
        for grp in range(6):
            ng = min(4, 22 - 4 * grp)
            bank = grp % 2
            pg = psB[:, bank * 512: bank * 512 + ng * 128]
            pu = psC[:, bank * 512: bank * 512 + ng * 128]
            for j in range(ng):
                f = 4 * grp + j
                for kc in range(8):
                    k.mm(psB[:, bank * 512 + j * 128: bank * 512 + (j + 1) * 128], wup[:, kc, f * 128:(f + 1) * 128], x1T[:, kc, :], start=(kc == 0), stop=(kc == 7))
            for j in range(ng):
                f = 22 + 4 * grp + j
                for kc in range(8):
                    k.mm(psC[:, bank * 512 + j * 128: bank * 512 + (j + 1) * 128], wup[:, kc, f * 128:(f + 1) * 128], x1T[:, kc, :], start=(kc == 0), stop=(kc == 7))
            g = gb[grp % 2]
            tc_ = tmpc[grp % 2]
            k.cp('act', g[:, 0:ng, 2:130], pg.rearrange('p (a b) -> p a b', b=128))
            if not samp:
                k.cp('pool', g[:, 0:ng, 0:2], prev2[:, 4 * grp:4 * grp + ng, :])
            else:
                g4 = g[:, 0:ng, 2:130].rearrange('p a (s t) -> p a s t', t=8)
                s1 = gs1[:, 0:ng, :].rearrange('p a (s t) -> p a s t', t=8)
                s2 = gs2[:, 0:ng, :].rearrange('p a (s t) -> p a s t', t=8)
                for j in range(ng):
                    f = 4 * grp + j
                    k.cp('pool', s1[:, j, :, 1:8], g4[:, j, :, 0:7])
                    k.cp('pool', s1[:, j, :, 0:1], sfc_in[:, f, :, 1:2])
                    k.cp('pool', s2[:, j, :, 2:8], g4[:, j, :, 0:6])
                    k.cp('pool', s2[:, j, :, 0:2], sfc_in[:, f, :, 0:2])
                    k.cp('pool', sfc_out[:, f, :, :], g4[:, j, :, 6:8])
            for j in range(ng):
                f = 4 * grp + j
                if not samp:
                    G0, G1, G2 = g[:, j, 2:130], g[:, j, 1:129], g[:, j, 0:128]
                else:
                    G0, G1, G2 = g[:, j, 2:130], gs1[:, j, :], gs2[:, j, :]
                t = tc_[:, j, :]
                k.ts('dve', t, G2, cw[:, f, 0:1], ALU.mult)
                k.stt('dve', t, G1, cw[:, f, 1:2], t, ALU.mult, ALU.add)
                k.stt('dve', t, G0, cw[:, f, 2:3], t, ALU.mult, ALU.add)
                k.act(t, t, AF.Silu, bias=cb[:, f:f + 1])
                k.tt('dve', hhT[:, f, :], t, psC[:, bank * 512 + j * 128: bank * 512 + (j + 1) * 128], ALU.mult)
            if not samp:
                k.cp('pool', prev2[:, 4 * grp:4 * grp + ng, :], g[:, 0:ng, 128:130])
        if i == NT - 2:
            k.dma('sp', d['o_pfconv'], prev2[:])
        if samp:
            k.dma('sp', d['o_sfconv'], sfc_out[:])
        for half in range(2):
            for f in range(22):
                k.mm(psD[:, half * 512:(half + 1) * 512], hhT[:, f, :], wdn[:, f, half * 512:(half + 1) * 512], start=(f == 0), stop=(f == 21))
        k.stt('dve', z2[:], x1t[:], DN_ALPHA, psD[:, :], ALU.mult, ALU.add)
        if n_ + 1 < len(tl):
            head(n_ + 1)
        k.rownorm_stats(z2, mv[:], stats)
        k.rsqrt(rstd[:], mv[:, 1:2], LN_EPS)
        k.ts('dve', z2[:], z2[:], mv[:, 0:1], ALU.subtract, rstd[:, 0:1], ALU.mult)
        k.tt('dve', z2[:], z2[:], l2g[:], ALU.mult)
        k.tt('dve', z2[:], z2[:], l2b[:], ALU.add)
        for kc in range(8):
            k.mm(psA[:, kc * 128:(kc + 1) * 128], z2[:, kc * 128:(kc + 1) * 128], identF[:])
        k.cp('act', x2T[:].rearrange('p a b -> p (a b)'), psA[:, :])
        for half in range(2):
            for kc in range(8):
                k.mm(psB[:, half * 512:(half + 1) * 512], x2T[:, kc, :], wpg[:, kc, half * 512:(half + 1) * 512], start=(kc == 0), stop=(kc == 7))
        for half in range(2):
            for kc in range(2):
                k.mm(psC[:, half * 512:(half + 1) * 512], pTt[:, kc, :], wpl[:, kc, half * 512:(half + 1) * 512], start=(kc == 0), stop=(kc == 1))
        k.rownorm_stats(psC, mv2[:], stats2)
        k.stt('dve', rstd2[:], mv2[:, 0:1], mv2[:, 0:1], mv2[:, 1:2], ALU.mult, ALU.add)
        k.rsqrt(rstd2[:], rstd2[:], RMS_EPS)
        k.stt('dve', ebuf[:], psC[:, :], rstd2[:, 0:1], plg[:], ALU.mult, ALU.mult)
        k.act(sg[:], psB[:, :], AF.Sigmoid)
        k.tt('dve', ebuf[:], ebuf[:], sg[:], ALU.mult)
        k.tt('dve', ebuf[:], ebuf[:], z2[:], ALU.add)
        k.dma('sp', d['y'][r0:r0 + 128, :], ebuf[:])


class _Rec:
    def __init__(self):
        self.calls = []

    def op(self, *a, **kw):
        self.calls.append(('op', a, kw))

    def dma(self, *a, **kw):
        self.calls.append(('dma', a, kw))


def record_calls(k, fn):
    real = k.P
    rec = _Rec()
    k.P = rec
    try:
        fn()
    finally:
        k.P = real
    return rec.calls


def replay_interleaved(k, lists):
    idx = [0] * len(lists)
    while any(idx[j] < len(lists[j]) for j in range(len(lists))):
        for j, l in enumerate(lists):
            if idx[j] < len(l):
                kind, a, kw = l[idx[j]]
                idx[j] += 1
                getattr(k.P, kind)(*a, **kw)


def phase_A(k, cfg):
    fl = lambda a: a.rearrange('p a b -> p (a b)')
    d = k.d
    P = k.P
    identF = cfg['identF']
    banks = cfg['banks']
    bstate = [0]

    def bank():
        b = banks[bstate[0] % 8]
        bstate[0] += 1
        return b

    mark0 = k.acur
    wstage = k.sb('wstage', [128, 8, 3848], BF16)
    for kc in range(8):
        k.dma('pool', wstage[:, kc, :], d['w_in'][kc])
    for kc in range(8):
        k.dma('sp', d['w_in_bf'][kc], wstage[:, kc, :])
    P.barrier()
    k.arena_reset(mark0)
    winbf = d['w_in_bf'].rearrange('k p f -> p k f')
    wbuf = [k.sb('wbuf%d' % j, [128, 8, 512], BF16) for j in range(2)]
    wo = k.sb('wo', [128, 8, 1024], BF16)
    for kc in range(8):
        k.dma('pool', wo[:, kc, :], d['w_o'][kc])
    identB = k.sb('identB', [128, 128], BF16); k.dma('pool', identB, d['identF'])
    MS = [k.sb('MS%d' % i, [128, 128]) for i in range(2)]
    MI = [k.sb('MI%d' % i, [128, 128]) for i in range(2)]
    MST = [k.sb('MST%d' % i, [128, 128]) for i in range(2)]
    rmask = [k.sb('rmask%d' % i, [128, 512]) for i in range(2)]
    for i in range(2):
        k.dma('sp', MS[i], d['MS'][i]); k.dma('sp', MI[i], d['MI'][i]); k.dma('sp', MST[i], d['MST'][i]); k.dma('sp', rmask[i], d['rmask'][i])
    Bones = k.sb('Bones', [128, 128]); k.dma('sp', Bones, d['Bones'])
    onesF = k.sb('onesF', [128, 128]); k.dma('sp', onesF, d['onesF'])
    segm = k.sb('segm', [128, 16]); k.dma('sp', segm, d['segm'])
    mu = k.sb('a_mu', [128, 14]); k.dma('sp', mu, d['a_mu'])
    pv = k.sb('a_pv', [128, 7, 4]); k.dma('sp', pv, d['a_pv'])
    nw0 = k.sb('a_nw0', [128, 4]); k.ts('dve', nw0, pv[:, 0, :], -1.0, ALU.mult)
    omka = k.sb('a_omka', [128, 4]); k.ts('dve', omka, pv[:, 3, :], -1.0, ALU.mult, 1.0, ALU.add)
    wlora = k.sb('wlora', [128, 512], BF16); k.dma('pool', wlora, d['a_wlora'])
    wg2 = k.sb('wg2', [128, 512], BF16); k.dma('pool', wg2, d['a_w_g2'])
    cwb = k.sb('b_cw', [128, 12, 4]); k.dma('sp', cwb, d['b_conv_w'])
    alog = k.sb('b_alog', [128, 4]); k.dma('sp', alog, d['b_a_log'].partition_broadcast(128))
    dtb = k.sb('b_dtb', [128, 4]); k.dma('sp', dtb, d['b_dt_bias'].partition_broadcast(128))
    negA = k.sb('b_negA', [128, 4]); k.act(negA, alog, AF.Exp); k.ts('dve', negA, negA, -1.0, ALU.mult)
    ng = k.sb('b_ng', [128, 1]); k.dma('sp', ng, d['b_norm_g'])
    l1g = k.sb('l1g', [128, 1024]); l1b = k.sb('l1b', [128, 1024])
    k.dma('sp', l1g, d['ln1_g'].partition_broadcast(128)); k.dma('sp', l1b, d['ln1_b'].partition_broadcast(128))
    onecol = onesF[:, 0:1]
    ash_in = k.sb('ash_in', [128, 14, 16]); k.dma('sp', ash_in, d['s_ashift'])
    bcv_in = k.sb('bcv_in', [128, 12, 16, 3]); k.dma('sp', bcv_in, d['s_bconv'])
    S = [k.sb('S%d' % v, [128, 128]) for v in range(8)]
    for v in range(8):
        k.memset('pool', S[v], 0.0)
    xT = k.sb('xT', [128, 8, 128], BF16)
    xt = k.sb('xt', [128, 1024])
    uT = k.sb('uT', [128, 14, 129]); k.memset('pool', uT[:, :, 128:129], 0.0)
    xs = k.sb('xs', [128, 14, 128])
    qkvp = k.sb('qkvp', [128, 12, 131]); k.memset('pool', qkvp[:, :, 128:131], 0.0)
    big = k.sb('big', [128, 4608])
    gsh = [big[:, j * 1536:(j + 1) * 1536].rearrange('p (a b) -> p a b', a=12, b=128) for j in range(3)]
    prevS = big[:, 0:1792].rearrange('p (a b) -> p a b', a=14, b=128)
    KdM = big[:, 0:1024].bitcast(BF16).rearrange('p (a b) -> p a b', a=16, b=128)
    KtM = big[:, 1024:2048].bitcast(BF16).rearrange('p (a b) -> p a b', a=16, b=128)
    Sin = big[:, 2048:4096].rearrange('p (a b) -> p a b', a=16, b=128)
    Sout = Sin
    rwtmp = k.sb('rwtmp', [128, 3584])
    cv = k.sb('cv', [128, 12, 128])
    zT = k.sb('zT', [128, 4, 128])
    batok = k.sb('batok', [128, 8])
    F4 = lambda n: k.sb(n, [128, 4, 128])
    B4 = lambda n: k.sb(n, [128, 4, 128], BF16)
    t_lw, t_L, t_a, t_kk, t_kp, t_b, t_tmp = [rwtmp[:, j * 512:(j + 1) * 512].rearrange('p (a b) -> p a b', a=4, b=128) for j in range(7)]
    t_eL, t_g, t_bon = [F4('t_%d' % j) for j in range(3)]
    rt, kkt, bt, kt, vb = [B4('r_%d' % j) for j in range(5)]
    lorain = k.sb('lorain', [128, 128], BF16); sgd = k.sb('sgd', [128, 128], BF16)
    KKtok, Btok, Ktok, Vtok = [k.sb('tok%d' % j, [128, 512], BF16) for j in range(4)]
    tmp8 = k.sb('tmp8', [128, 8, 128])
    qh = F4('qh'); qhb = B4('qhb'); khb = B4('khb'); vgb = B4('vgb')
    Win = B4('Win'); Yin = B4('Yin'); Kdg = B4('Kdg')
    beta = k.sb('beta', [128, 4]); gtk = k.sb('gtk', [128, 4]); Gtok = k.sb('Gtok', [128, 4]); eGtok = k.sb('eGtok', [128, 4])
    eGs = k.sb('eGs', [128, 4]); beG = k.sb('beG', [128, 4])
    eGbc = [k.sb('eGbc%d' % h, [128, 128]) for h in range(4)]
    QTg = F4('QTg')

    def mkset(alloc_f, alloc_b):
        B = {}
        B['gd'] = [alloc_f() for _ in range(5)]
        B['inv'] = [alloc_f() for _ in range(8)]
        B['TTb'] = [alloc_b() for _ in range(2)]
        B['ATb'] = [alloc_b() for _ in range(2)]
        B['A2Tb'] = [alloc_b() for _ in range(2)]
        B['AakT'] = alloc_b()
        B['XYin'] = alloc_b(256)
        B['nwv'] = alloc_b(); B['uv'] = alloc_b()
        for n in ('ZT', 'OcT', 'OT', 'PhiT', 'Psi', 'Idg'):
            B[n] = alloc_f()
        B['gtmp'] = [alloc_f() for _ in range(3)]
        return B
    cnt = [0]

    def a0f():
        cnt[0] += 1
        return k.sb('s0f%d' % cnt[0], [128, 128])

    def a0b(n=128):
        cnt[0] += 1
        return k.sb('s0b%d' % cnt[0], [128, n], BF16)
    bigcur = [0]

    def a1f():
        v = big[:, bigcur[0]:bigcur[0] + 128]
        bigcur[0] += 128
        return v

    def a1b(n=128):
        v = big[:, bigcur[0]:bigcur[0] + n // 2].bitcast(BF16)
        bigcur[0] += n // 2
        return v
    BS = [mkset(a0f, a0b), mkset(a1f, a1b), mkset(a0f, a0b), mkset(a0f, a0b)]
    assert bigcur[0] <= 4608, bigcur[0]
    oT = k.sb('oT', [128, 8, 128], BF16)
    z1 = xt
    stats = k.sb('a_stats', [128, 2, 6]); mv = k.sb('a_mv', [128, 2]); rstd = k.sb('a_rstd', [128, 1])
    ashs = k.sb('ashs', [128, 14, 16]); bcvs = k.sb('bcvs', [128, 12, 16, 3])
    pash = k.sb('pash', [128, 14, 1])

    bst = [0, 0, 0, 0]

    def mkbank(l):
        def f():
            b = banks[2 * l + bst[l] % 2]
            bst[l] += 1
            return b
        return f
    lbank = [mkbank(l) for l in range(4)]
    pst = [0, 0]

    def mkpbank(l):
        def f():
            b = banks[4 * l + pst[l] % 4]
            pst[l] += 1
            return b
        return f
    pbank = [mkpbank(0), mkpbank(1)]

    tlA = list(cfg.get('tiles', range(NT)))
    for i in tlA:
        samp = (i == NT - 1)
        mi = 1 if samp else 0
        blk = 8 if samp else 64
        nseg = 128 // blk
        nlev = 2 if samp else 5
        if cfg.get('nlev') is not None:
            nlev = cfg['nlev']
        r0 = i * 128
        if cfg.get('tilebar'):
            P.barrier()
        nA = tlA.index(i)
        if nA == 0:
            for kc in range(8):
                k.dma('pool', xT[:, kc, :], d['xT'][kc, :, r0:r0 + 128])
        k.dma('sp', xt, d['x'][r0:r0 + 128, :])
        if not samp:
            k.cp('pool', uT[:, :, 0:1], uT[:, :, 128:129])
            k.cp('pool', qkvp[:, :, 0:3], qkvp[:, :, 128:131])
        for grp in range(cfg.get('pgrp', 8)):
            nchunk = 4 if grp < 7 else 2
            b = bank()
            wb = wbuf[grp % 2]
            ncols = 512 if grp < 7 else 264
            if not (grp < 2 and nA > 0):
                k.dma('sp', wb[:, :, 0:ncols], winbf[:, :, grp * 512:grp * 512 + ncols])
            for j in range(nchunk):
                c = grp * 4 + j
                for kc in range(8):
                    k.mm(b[:, j * 128:(j + 1) * 128], wb[:, kc, j * 128:(j + 1) * 128], xT[:, kc, :], start=(kc == 0), stop=(kc == 7))
            for j in range(nchunk):
                c = grp * 4 + j
                src = b[:, j * 128:(j + 1) * 128]
                if c < 14:
                    k.cp('act', uT[:, c, 1:129], src)
                elif c < 26:
                    k.cp('act', qkvp[:, c - 14, 3:131], src)
                else:
                    k.act(zT[:, c - 26, :], src, AF.Silu)
        b = bank()
        for kc in range(8):
            k.mm(b[:, 0:8], xT[:, kc, :], wbuf[1][:, kc, 256:264], start=(kc == 0), stop=(kc == 7))
        k.cp('dve', batok, b[:, 0:8])
        if nA + 1 < len(tlA):
            rn = tlA[nA + 1] * 128
            for kc in range(8):
                k.dma('pool', xT[:, kc, :], d['xT'][kc, :, rn:rn + 128])
            for g_ in range(2):
                k.dma('sp', wbuf[g_][:, :, 0:512], winbf[:, :, g_ * 512:g_ * 512 + 512])
        def pre_rwkv(bank):
            cur = uT[:, :, 1:129]
            if not samp:
                prev = uT[:, :, 0:128]
            else:
                p4 = prevS.rearrange('p a (s t) -> p a s t', t=8)
                c4 = cur.rearrange('p a (s t) -> p a s t', t=8)
                for c in range(14):
                    k.cp('pool', p4[:, c, :, 1:8], c4[:, c, :, 0:7])
                    k.cp('pool', p4[:, c, :, 0:1], ash_in[:, c, :].rearrange('p (s o) -> p s o', o=1))
                    k.cp('pool', ashs[:, c, :].rearrange('p (s o) -> p s o', o=1), c4[:, c, :, 7:8])
                prev = prevS
                k.dma('sp', d['o_sashift'], ashs)
            if i == NT - 2:
                k.cp('pool', pash, uT[:, :, 128:129])
                k.dma('sp', d['o_pashift'], pash)
            for c in range(14):
                k.tt('pool', xs[:, c, :], prev[:, c, :], cur[:, c, :], ALU.subtract)
                k.stt('dve' if c % 2 else 'pool', xs[:, c, :], xs[:, c, :], mu[:, c:c + 1], cur[:, c, :], ALU.mult, ALU.add)
            r_, kx, v_ = xs[:, 0:4, :], xs[:, 4:8, :], xs[:, 8:12, :]
            k.act(lorain[0:64, :], xs[0:64, 12, :], AF.Tanh)
            k.cp('dve', lorain[64:128, :], xs[64:128, 12, :])
            k.act(sgd, xs[:, 13, :], AF.Sigmoid)
            bw, ba_, bg = bank(), bank(), bank()
            for c in range(4):
                cs = slice(c * 128, (c + 1) * 128)
                k.mm(bw[:, cs], wlora[0:64, cs], lorain[0:64, :])
                k.mm(ba_[:, cs], wlora[64:128, cs], lorain[64:128, :])
                k.mm(bg[:, cs], wg2[:, cs], sgd)
            for c in range(4):
                cs = slice(c * 128, (c + 1) * 128)
                k.act(t_lw[:, c, :], bw[:, cs], AF.Exp, bias=nw0[:, c:c + 1], scale=-1.0)
                k.act(t_a[:, c, :], ba_[:, cs], AF.Sigmoid, bias=pv[:, 1, c:c + 1])
            k.cp('dve', t_g.rearrange('p a b -> p (a b)'), bg[:, :])
            fl = lambda a: a.rearrange('p a b -> p (a b)')
            k.act(fl(t_lw), fl(t_lw), AF.Ln, bias=1.0)
            k.act(fl(t_lw), fl(t_lw), AF.Exp, bias=-0.5, scale=-1.0)
            k.ts('dve', fl(t_lw), fl(t_lw), -1.0, ALU.mult)
            for c in range(4):
                k.ts('dve', t_kk[:, c, :], kx[:, c, :], pv[:, 2, c:c + 1], ALU.mult)
            k.tt('pool', t_tmp, t_kk, t_kk, ALU.mult)
            b = bank()
            for c in range(4):
                k.mm(b[:, c * 128:(c + 1) * 128], Bones, t_tmp[:, c, :])
            k.rsqrt(fl(t_tmp), b[:, :], L2_EPS)
            k.tt('dve', t_kk, t_kk, t_tmp, ALU.mult)
            for c in range(4):
                k.ts('dve', t_tmp[:, c, :], t_a[:, c, :], pv[:, 3, c:c + 1], ALU.mult, omka[:, c:c + 1], ALU.add)
            k.tt('dve', t_kp, kx, t_tmp, ALU.mult)
            k.tt('pool', t_b, t_kk, t_a, ALU.mult)
            k.P.op('dve', lambda e, rm=rmask[mi]: e.tensor_tensor_scan(out=fl(t_L), data0=rm, data1=fl(t_lw), initial=0.0, op0=ALU.mult, op1=ALU.add),
                 reads=[rmask[mi], fl(t_lw)], writes=[fl(t_L)])
            k.tt('dve', t_lw, t_L, t_lw, ALU.subtract)
            k.act(fl(t_eL), fl(t_L), AF.Exp)
            k.act(fl(t_lw), fl(t_lw), AF.Exp)
            k.act(fl(t_L), fl(t_L), AF.Exp, scale=-1.0)
            k.tt('dve', rt, r_, t_eL, ALU.mult)
            k.tt('dve', kkt, t_kk, t_lw, ALU.mult)
            k.tt('pool', bt, t_b, t_L, ALU.mult)
            k.tt('pool', kt, t_kp, t_L, ALU.mult)
            k.cp('pool', vb, v_)
            k.tt('dve', t_tmp, r_, t_kp, ALU.mult)
            for c in range(4):
                k.ts('dve', t_tmp[:, c, :], t_tmp[:, c, :], pv[:, 4, c:c + 1], ALU.mult)
            b = bank()
            for c in range(4):
                k.mm(b[:, c * 128:(c + 1) * 128], Bones, t_tmp[:, c, :])
            k.tt('dve', fl(t_bon), b[:, :], fl(v_), ALU.mult)
            for src, dst in ((kkt, KKtok), (bt, Btok), (kt, Ktok), (vb, Vtok)):
                b = bank()
                for c in range(4):
                    k.mm(b[:, c * 128:(c + 1) * 128], src[:, c, :], identB)
                k.cp('act', dst, b[:, :])
        def pre_gdn(bank):
            if samp:
                q4 = qkvp[:, :, 3:131].rearrange('p a (s t) -> p a s t', t=8)
                for j in range(3):
                    g4 = gsh[j].rearrange('p a (s t) -> p a s t', t=8)
                    for c in range(12):
                        k.cp('pool', g4[:, c, :, j + 1:8], q4[:, c, :, 0:7 - j])
                        k.cp('pool', g4[:, c, :, 0:j + 1], bcv_in[:, c, :, 2 - j:3])
                for c in range(12):
                    k.cp('pool', bcvs[:, c, :, :], q4[:, c, :, 5:8])
                k.dma('sp', d['o_sbconv'], bcvs)
            if i == NT - 2:
                k.dma('sp', d['o_pbconv'], qkvp[:, :, 128:131])
            for c in range(12):
                if samp:
                    X0, X1, X2, X3 = qkvp[:, c, 3:131], gsh[0][:, c, :], gsh[1][:, c, :], gsh[2][:, c, :]
                else:
                    X0, X1, X2, X3 = qkvp[:, c, 3:131], qkvp[:, c, 2:130], qkvp[:, c, 1:129], qkvp[:, c, 0:128]
                e = 'dve' if c % 2 else 'pool'
                k.ts(e, cv[:, c, :], X3, cwb[:, c, 0:1], ALU.mult)
                k.stt(e, cv[:, c, :], X2, cwb[:, c, 1:2], cv[:, c, :], ALU.mult, ALU.add)
                k.stt(e, cv[:, c, :], X1, cwb[:, c, 2:3], cv[:, c, :], ALU.mult, ALU.add)
                k.stt(e, cv[:, c, :], X0, cwb[:, c, 3:4], cv[:, c, :], ALU.mult, ALU.add)
            k.act(fl(cv), fl(cv), AF.Silu)
            k.tt('pool', tmp8, cv[:, 0:8, :], cv[:, 0:8, :], ALU.mult)
            b1, b2 = bank(), bank()
            for c in range(8):
                bb = b1 if c < 4 else b2
                k.mm(bb[:, (c % 4) * 128:(c % 4 + 1) * 128], onesF, tmp8[:, c, :])
            k.rsqrt(fl(tmp8[:, 0:4, :]), b1[:, :], L2_EPS)
            k.rsqrt(fl(tmp8[:, 4:8, :]), b2[:, :], L2_EPS)
            k.stt('dve', qh, cv[:, 0:4, :], 128.0 ** -0.5, tmp8[:, 0:4, :], ALU.mult, ALU.mult)
            k.cp('pool', qhb, qh)
            k.tt('dve', khb, cv[:, 4:8, :], tmp8[:, 4:8, :], ALU.mult)
            k.cp('pool', vgb, cv[:, 8:12, :])
            k.act(beta, batok[:, 0:4], AF.Sigmoid)
            k.tt('dve', gtk, batok[:, 4:8], dtb, ALU.add)
            k.act(gtk, gtk, AF.Exp)
            k.act(gtk, gtk, AF.Ln, bias=1.0)
            k.tt('dve', gtk, gtk, negA, ALU.mult)
            b = bank()
            k.mm(b[:, 0:4], MI[mi], gtk)
            k.mm(b[:, 4:8], MST[mi], gtk)
            k.cp('dve', Gtok, b[:, 0:4])
            k.act(eGtok, b[:, 0:4], AF.Exp)
            k.act(eGs, b[:, 4:8], AF.Exp)
            k.tt('dve', beG, beta, eGtok, ALU.mult)
            bk_, bv_ = bank(), bank()
            for h in range(4):
                k.mm(bk_[:, h * 128:(h + 1) * 128], khb[:, h, :], identB)
                k.mm(bv_[:, h * 128:(h + 1) * 128], vgb[:, h, :], identB)
            for h in range(4):
                hs = slice(h * 128, (h + 1) * 128)
                k.ts('dve', Win[:, h, :], bk_[:, hs], beG[:, h:h + 1], ALU.mult)
                k.ts('dve', Kdg[:, h, :], bk_[:, hs], eGs[:, h:h + 1], ALU.mult)
                k.ts('dve', Yin[:, h, :], bv_[:, hs], beta[:, h:h + 1], ALU.mult)

        if samp or cfg.get('nopre'):
            pre_rwkv(bank); pre_gdn(bank)
        else:
            lists = [record_calls(k, lambda: pre_rwkv(pbank[0])), record_calls(k, lambda: pre_gdn(pbank[1]))]
            replay_interleaved(k, lists)
        def vh_gen(vh, B, bank):
            Um, Lm, Ua, La, Ub, Lb, Qm, Qn = B['inv']
            TTb, ATb, A2Tb = B['TTb'], B['ATb'], B['A2Tb']
            AakT, XYin, nwv, uv = B['AakT'], B['XYin'], B['nwv'], B['uv']
            ZT, OcT, OT, PhiT, Psi, Idg = B['ZT'], B['OcT'], B['OT'], B['PhiT'], B['Psi'], B['Idg']
            gtmp = B['gtmp']
            gB, bB, tmpD, E1, E2 = B['gd']

            def invert(TTout):
                k.tt('dve', Qm, identF, Um, ALU.subtract)
                Uk, Lk, Un, Ln_ = Um, Lm, Ua, La
                Q, Q2 = Qm, Qn
                for lev in range(nlev):
                    b = bank()
                    k.mm(b[:, 0:128], Uk, Lk)
                    if lev < nlev - 1:
                        k.mm(b[:, 128:256], Lk, Uk)
                    yield
                    k.cp('act', Ln_, b[:, 0:128])
                    if lev < nlev - 1:
                        k.cp('act', Un, b[:, 128:256])
                    yield
                    b2 = bank()
                    k.mm(b2[:, 0:128], Ln_, Q)
                    yield
                    k.tt('dve', Q2, b2[:, 0:128], Q, ALU.add)
                    Q, Q2 = Q2, Q
                    if Un is Ua:
                        Uk, Lk, Un, Ln_ = Ua, La, Ub, Lb
                    else:
                        Uk, Lk, Un, Ln_ = Ub, Lb, Ua, La
                    yield
                k.cp('act', TTout, Q)

            rw = vh < 4
            if rw:
                hp = vh
                subs = []
                for h2 in range(2):
                    rs = slice(h2 * 64, (h2 + 1) * 64)
                    hc = slice(hp * 128 + h2 * 64, hp * 128 + (h2 + 1) * 64)
                    b = bank()
                    k.mm(b[:, 0:128], bt[rs, hp, :], kkt[rs, hp, :])
                    k.mm(b[:, 128:256], kkt[rs, hp, :], bt[rs, hp, :])
                    k.mm(b[:, 256:384], kt[rs, hp, :], kkt[rs, hp, :])
                    yield
                    k.tt('dve', Um, b[:, 0:128], MS[mi], ALU.mult)
                    k.tt('dve', Lm, b[:, 128:256], MST[mi], ALU.mult)
                    k.tt('dve', AakT, b[:, 256:384], MS[mi], ALU.mult)
                    b = bank()
                    k.mm(b[:, 0:128], bt[rs, hp, :], rt[rs, hp, :])
                    k.mm(b[:, 128:256], kt[rs, hp, :], rt[rs, hp, :])
                    yield
                    k.tt('dve', ATb[h2], b[:, 0:128], MI[mi], ALU.mult)
                    k.tt('dve', A2Tb[h2], b[:, 128:256], MI[mi], ALU.mult)
                    b = bank()
                    k.mm(b[:, 0:64], AakT, Vtok[:, hc])
                    k.cp('pool', XYin[:, 0:64], KKtok[:, hc])
                    yield
                    k.act(XYin[:, 64:128], b[:, 0:64], AF.Copy, scale=-1.0)
                    yield from invert(TTb[h2])
                    yield
                    b = bank()
                    k.mm(b[:, 0:128], TTb[h2], XYin[:, 0:128])
                    yield
                    k.act(nwv[:, rs], b[:, 0:64], AF.Copy, scale=-1.0)
                    k.cp('act', uv[:, rs], b[:, 64:128])
                    subs.append((h2, rs))
                    yield
                Kd = Btok[:, hp * 128:(hp + 1) * 128]
                QT = rt[:, hp, :]
                BD = Bones
            else:
                h = vh - 4
                k.ts('dve', gB, onesF, gtk[:, h:h + 1], ALU.mult)
                k.ts('dve', bB, onesF, beta[:, h:h + 1], ALU.mult)
                bG = bank()
                k.mm(bG[:, 0:128], gB, MI[mi])
                k.mm(bG[:, 128:256], bB, identF)
                bKK = bank()
                k.mm(bKK[:, 0:128], khb[:, h, :], khb[:, h, :])
                k.mm(bKK[:, 128:256], khb[:, h, :], qhb[:, h, :])
                yield
                k.act(eGbc[h], bG[:, 0:128], AF.Exp)
                k.ts('dve', tmpD, bG[:, 0:128], Gtok[:, h:h + 1], ALU.subtract)
                k.ts('dve', E1, tmpD, 0.0, ALU.min)
                k.ts('dve', E2, tmpD, -1.0, ALU.mult, 0.0, ALU.min)
                yield
                k.act(E1, E1, AF.Exp)
                k.act(E2, E2, AF.Exp)
                k.tt('dve', QTg[:, h, :], qh[:, h, :], eGbc[h], ALU.mult)
                yield
                k.tt('dve', Um, E1, MS[mi], ALU.mult)
                k.tt('dve', Um, Um, bG[:, 128:256], ALU.mult)
                k.tt('dve', Um, Um, bKK[:, 0:128], ALU.mult)
                yield
                k.tt('dve', tmpD, E1, MI[mi], ALU.mult)
                k.tt('dve', ATb[0], tmpD, bKK[:, 128:256], ALU.mult)
                k.stt('dve', Lm, E2, beta[:, h:h + 1], MST[mi], ALU.mult, ALU.mult)
                k.tt('dve', Lm, Lm, bKK[:, 0:128], ALU.mult)
                yield
                yield from invert(TTb[0])
                k.cp('pool', XYin[:, 0:128], Win[:, h, :])
                k.cp('pool', XYin[:, 128:256], Yin[:, h, :])
                yield
                b = bank()
                k.mm(b[:, 0:256], TTb[0], XYin[:, 0:256])
                yield
                k.act(nwv, b[:, 0:128], AF.Copy, scale=-1.0)
                k.cp('act', uv, b[:, 128:256])
                subs = [(0, slice(0, 128))]
                Kd = Kdg[:, h, :]
                QT = QTg[:, h, :]
                BD = None
                yield
            bZ, bO = bank(), bank()
            for (h2, rs) in subs:
                k.mm(bZ[rs, 0:128], nwv[:, rs], ATb[h2])
                k.mm(bO[rs, 0:128], uv[:, rs], ATb[h2], start=True, stop=(not rw))
                if rw:
                    hc = slice(vh * 128 + h2 * 64, vh * 128 + (h2 + 1) * 64)
                    k.mm(bO[rs, 0:128], Vtok[:, hc], A2Tb[h2], start=False, stop=True)
            yield
            k.tt('dve', ZT, bZ[:, 0:128], QT, ALU.add)
            k.cp('act', OcT, bO[:, 0:128])
            yield
            if samp:
                k.dma('sp', Sin, d['s_state'][vh])
                for j in range(16):
                    k.ts('dve', KdM[:, j, :], Kd, segm[:, j:j + 1], ALU.mult)
                    if rw:
                        k.ts('dve', KtM[:, j, :], Ktok[:, vh * 128:(vh + 1) * 128], segm[:, j:j + 1], ALU.mult)
            for sg in range(nseg):
                cols = slice(sg * blk, (sg + 1) * blk)
                last = sg * blk + blk - 1
                Scur = Sin[:, sg, :] if samp else S[vh]
                Snew = Sout[:, sg, :] if samp else S[vh]
                if rw:
                    cP = t_eL[:, vh, last:last + 1]
                    IdgX = identF
                else:
                    cP = onecol
                    k.ts('dve', Idg, identF, eGbc[vh - 4][:, last:last + 1], ALU.mult)
                    IdgX = Idg
                if samp:
                    nwS, KdS, uS = nwv, KdM[:, sg, :], uv
                else:
                    rows = slice(sg * 64, (sg + 1) * 64)
                    nwS, KdS, uS = nwv[rows, :], Kd[rows, :], uv[rows, :]
                bP = bank()
                k.mm(bP[:, 0:128], nwS, KdS)
                k.mm(bP[:, 128:256], KdS, uS, start=True, stop=(not rw))
                if rw:
                    if samp:
                        k.mm(bP[:, 128:256], KtM[:, sg, :], Vtok[:, vh * 128:(vh + 1) * 128], start=False, stop=True)
                    else:
                        k.mm(bP[:, 128:256], Ktok[rows, vh * 128:(vh + 1) * 128], Vtok[rows, vh * 128:(vh + 1) * 128], start=False, stop=True)
                yield
                if BD is not None:
                    k.tt('dve', PhiT, bP[:, 0:128], BD, ALU.mult)
                    k.tt('dve', PhiT, PhiT, IdgX, ALU.add)
                    k.stt('dve', Psi, bP[:, 128:256], cP, BD, ALU.mult, ALU.mult)
                else:
                    k.tt('dve', PhiT, bP[:, 0:128], IdgX, ALU.add)
                    k.cp('act', Psi, bP[:, 128:256])
                yield
                bS = bank()
                k.mm(bS[:, 0:blk], Scur, ZT[:, cols])
                k.mm(bS[:, 128:256], PhiT, Scur)
                yield
                k.tt('dve', OT[:, cols], bS[:, 0:blk], OcT[:, cols], ALU.add)
                k.stt('dve', Snew, bS[:, 128:256], cP, Psi, ALU.mult, ALU.add)
                yield
            if samp:
                k.dma('sp', d['o_sstate'][vh], Sout)
            if rw:
                k.tt('pool', gtmp[0], OT, OT, ALU.mult)
                b = bank()
                k.mm(b[:, 0:128], Bones, OT)
                k.mm(b[:, 128:256], Bones, gtmp[0])
                yield
                k.act(gtmp[1], b[:, 0:128], AF.Copy, scale=1.0 / 64)
                k.tt('dve', gtmp[2], gtmp[1], gtmp[1], ALU.mult)
                k.stt('dve', gtmp[2], b[:, 128:256], 1.0 / 64, gtmp[2], ALU.mult, ALU.subtract)
                yield
                k.rsqrt(gtmp[2], gtmp[2], GN_EPS)
                k.tt('dve', gtmp[1], OT, gtmp[1], ALU.subtract)
                yield
                k.tt('dve', gtmp[1], gtmp[1], gtmp[2], ALU.mult)
                k.ts('dve', gtmp[1], gtmp[1], pv[:, 5, vh:vh + 1], ALU.mult, pv[:, 6, vh:vh + 1], ALU.add)
                yield
                k.tt('pool', gtmp[1], gtmp[1], t_bon[:, vh, :], ALU.add)
                k.tt('pool', oT[:, vh, :], gtmp[1], t_g[:, vh, :], ALU.mult)
            else:
                h = vh - 4
                k.tt('pool', gtmp[0], OT, OT, ALU.mult)
                b = bank()
                k.mm(b[:, 0:128], onesF, gtmp[0])
                yield
                k.act(gtmp[2], b[:, 0:128], AF.Copy, scale=1.0 / 128)
                k.rsqrt(gtmp[2], gtmp[2], RMS_EPS)
                yield
                k.tt('dve', gtmp[1], OT, gtmp[2], ALU.mult)
                k.stt('dve', oT[:, vh, :], gtmp[1], ng[:, 0:1], zT[:, h, :], ALU.mult, ALU.mult)
            yield

        def lane(vhs, B, bankf):
            for vh in vhs:
                yield from vh_gen(vh, B, bankf)

        if samp or cfg.get('nolanes'):
            lanes = [lane(range(8), BS[0], bank)]
        else:
            lanes = [lane([l, 4 + l], BS[l], lbank[l]) for l in range(4)]
        while lanes:
            for g in list(lanes):
                try:
                    next(g)
                except StopIteration:
                    lanes.remove(g)
        b1, b2 = bank(), bank()
        for half, bb in ((0, b1), (1, b2)):
            for kc in range(8):
                k.mm(bb[:, :], oT[:, kc, :], wo[:, kc, half * 512:(half + 1) * 512], start=(kc == 0), stop=(kc == 7))
        k.stt('dve', z1[:, 0:512], xt[:, 0:512], DN_ALPHA, b1[:, :], ALU.mult, ALU.add)
        k.stt('dve', z1[:, 512:1024], xt[:, 512:1024], DN_ALPHA, b2[:, :], ALU.mult, ALU.add)
        k.rownorm_stats(z1, mv, stats)
        k.rsqrt(rstd, mv[:, 1:2], LN_EPS)
        k.ts('dve', z1, z1, mv[:, 0:1], ALU.subtract, rstd[:, 0:1], ALU.mult)
        k.tt('dve', z1, z1, l1g, ALU.mult)
        k.tt('dve', z1, z1, l1b, ALU.add)
        k.dma('sp', d['x1s'][r0:r0 + 128, :], z1)
        if cfg.get('dbg'):
            k.dma('sp', d['dbg_oT'][i], oT)
    for v in range(8):
        k.dma('sp', d['o_pstate'][v], S[v])


def consts():
    c = {}
    c['identF'] = np.eye(128, dtype=np.float32)
    MS = np.zeros((2, 128, 128), np.float32); MI = np.zeros((2, 128, 128), np.float32)
    rmask = np.ones((2, 128, 512), np.float32)
    for i, blk in enumerate((64, 8)):
        s = np.arange(128)[:, None]; t = np.arange(128)[None, :]
        same = (s // blk) == (t // blk)
        MS[i] = (same & (s < t)); MI[i] = (same & (s <= t))
        tt = np.arange(512)
        rmask[i][:, (tt % blk) == 0] = 0.0
    c['MS'] = MS; c['MI'] = MI; c['MST'] = np.ascontiguousarray(MS.transpose(0, 2, 1)); c['rmask'] = rmask
    p = np.arange(128)
    c['Bones'] = ((p[:, None] // 64) == (p[None, :] // 64)).astype(np.float32)
    c['onesF'] = np.ones((128, 128), np.float32)
    c['segm'] = ((p[:, None] // 8) == np.arange(16)[None, :]).astype(np.float32)
    return c


def fm(v, nch):
    return np.ascontiguousarray(v.reshape(nch, 128).T)


def weights(inp):
    w = {}
    w['w_in'] = inp['w_in'][0].reshape(8, 128, 3848)
    w['w_o'] = inp['w_o'][0].reshape(8, 128, 1024)
    w['w_up'] = inp['w_up'][0].reshape(8, 128, 5632)
    w['w_down'] = inp['w_down'][0].reshape(22, 128, 1024)
    w['w_ple_gate'] = inp['w_ple_gate'][0].reshape(8, 128, 1024)
    w['w_ple'] = inp['w_ple'][0].reshape(2, 128, 1024)
    w['f_conv_w'] = np.ascontiguousarray(inp['f_conv_w'][0].reshape(3, 22, 128).transpose(2, 1, 0))
    w['f_conv_b'] = fm(inp['f_conv_b'][0], 22)
    for n in ('ln1_g', 'ln1_b', 'ln2_g', 'ln2_b', 'ple_g'):
        w[n] = np.ascontiguousarray(inp[n][0])
    w['a_mu'] = fm(inp['a_mu'][0], 14)
    pv = np.stack([inp[n][0].reshape(4, 128) for n in ('a_w0', 'a_a0', 'a_k_k', 'a_k_a', 'a_r_k', 'a_gn_g', 'a_gn_b')], 0)
    w['a_pv'] = np.ascontiguousarray(pv.transpose(2, 0, 1))
    w['a_wlora'] = np.ascontiguousarray(np.concatenate([inp['a_w_w2'][0], inp['a_w_a2'][0]], 0))
    w['a_w_g2'] = np.ascontiguousarray(inp['a_w_g2'][0])
    w['b_conv_w'] = np.ascontiguousarray(inp['b_conv_w'][0].reshape(4, 12, 128).transpose(2, 1, 0))
    w['b_a_log'] = np.ascontiguousarray(inp['b_a_log'][0]); w['b_dt_bias'] = np.ascontiguousarray(inp['b_dt_bias'][0])
    w['b_norm_g'] = np.ascontiguousarray(inp['b_norm_g'][0].reshape(128, 1))
    return w


def percore(inp, c):
    m = {}
    sl = slice(16 * c, 16 * c + 16)
    x = np.concatenate([inp['x_prompt'][c], inp['x_sample'][sl].reshape(128, 1024)], 0)
    m['x'] = np.ascontiguousarray(x)
    m['xT'] = np.ascontiguousarray(x.T.reshape(8, 128, TOK))
    p = np.concatenate([inp['p_prompt'][0, c], inp['p_sample'][0, sl].reshape(128, 256)], 0)
    m['pT'] = np.ascontiguousarray(p.T.reshape(2, 128, TOK))
    m['s_fconv'] = np.ascontiguousarray(inp['state_ffn_conv'][0, sl].reshape(16, 2, 22, 128).transpose(3, 2, 0, 1))
    m['s_ashift'] = np.ascontiguousarray(inp['state_a_shift'][0, sl].reshape(16, 14, 128).transpose(2, 1, 0))
    m['s_bconv'] = np.ascontiguousarray(inp['state_b_conv'][0, sl].reshape(16, 3, 12, 128).transpose(3, 2, 0, 1))
    st = np.zeros((8, 128, 16, 128), np.float32)
    wkv = inp['state_a_wkv'][0, sl]
    for hp in range(4):
        for h2 in range(2):
            blkk = wkv[:, hp * 2 + h2].transpose(2, 0, 1)
            st[hp, h2 * 64:(h2 + 1) * 64, :, h2 * 64:(h2 + 1) * 64] = blkk
    ssm = inp['state_b_ssm'][0, sl]
    for h in range(4):
        st[4 + h] = ssm[:, h].transpose(1, 0, 2)
    m['s_state'] = st
    return m


def unpack(r):
    o = {}
    o['y_p'] = r['y'][:2048]; o['y_s'] = r['y'][2048:].reshape(16, 8, 1024)
    ps = r['o_pstate']
    o['pa_wkv'] = np.stack([ps[h // 2][(h % 2) * 64:(h % 2 + 1) * 64, (h % 2) * 64:(h % 2 + 1) * 64].T for h in range(8)], 0)
    o['pb_ssm'] = ps[4:8]
    o['pa_shift'] = r['o_pashift'].reshape(128, 14).T.reshape(1792)
    o['pb_conv'] = r['o_pbconv'].transpose(2, 1, 0).reshape(3, 1536)
    o['pf_conv'] = r['o_pfconv'].transpose(2, 1, 0).reshape(2, 2816)
    ss = r['o_sstate']
    o['sa_wkv'] = np.stack([ss[h // 2][(h % 2) * 64:(h % 2 + 1) * 64, :, (h % 2) * 64:(h % 2 + 1) * 64].transpose(1, 2, 0) for h in range(8)], 1)
    o['sb_ssm'] = ss[4:8].transpose(2, 0, 1, 3)
    o['sa_shift'] = r['o_sashift'].transpose(2, 1, 0).reshape(16, 1792)
    o['sb_conv'] = r['o_sbconv'].transpose(2, 3, 1, 0).reshape(16, 3, 1536)
    o['sf_conv'] = r['o_sfconv'].transpose(2, 3, 1, 0).reshape(16, 2, 2816)
    return o


from concourse.bass_utils import run_bass_kernel_spmd

_CACHE = {}


def build():
    nc = bass.Bass("TRN2", target_bir_lowering=False)
    CS = consts()
    with contextlib.ExitStack() as st:
        P = Prog(nc)
        k = KB(nc, P, st)
        k.din('x', [TOK, 1024]); k.din('xT', [8, 128, TOK]); k.din('pT', [2, 128, TOK])
        k.din('w_in', [8, 128, 3848]); k.din('w_o', [8, 128, 1024])
        k.din('w_up', [8, 128, 5632]); k.din('w_down', [22, 128, 1024]); k.din('w_ple_gate', [8, 128, 1024]); k.din('w_ple', [2, 128, 1024])
        for n, v in CS.items():
            k.din(n, v.shape)
        k.din('a_mu', [128, 14]); k.din('a_pv', [128, 7, 4]); k.din('a_wlora', [128, 512]); k.din('a_w_g2', [128, 512])
        k.din('b_conv_w', [128, 12, 4]); k.din('b_a_log', [4]); k.din('b_dt_bias', [4]); k.din('b_norm_g', [128, 1])
        k.din('ln1_g', [1024]); k.din('ln1_b', [1024]); k.din('ln2_g', [1024]); k.din('ln2_b', [1024]); k.din('ple_g', [1024])
        k.din('f_conv_w', [128, 22, 3]); k.din('f_conv_b', [128, 22])
        k.din('s_ashift', [128, 14, 16]); k.din('s_bconv', [128, 12, 16, 3]); k.din('s_state', [8, 128, 16, 128]); k.din('s_fconv', [128, 22, 16, 2])
        k.dint('x1s', [TOK, 1024])
        k.dint('w_in_bf', [8, 128, 3848], BF16)
        k.dout('y', [TOK, 1024])
        k.dout('o_pstate', [8, 128, 128]); k.dout('o_sstate', [8, 128, 16, 128])
        k.dout('o_pashift', [128, 14, 1]); k.dout('o_sashift', [128, 14, 16]); k.dout('o_pbconv', [128, 12, 3]); k.dout('o_sbconv', [128, 12, 16, 3])
        k.dout('o_pfconv', [128, 22, 2]); k.dout('o_sfconv', [128, 22, 16, 2])
        cfg = {}
        k.init_arena()
        idf = k.sb('identF_sb', [128, 128]); k.dma('sp', idf, k.d['identF']); cfg['identF'] = idf
        mark = k.acur
        pss = [k.ps('ps' + n, [128, 1024]) for n in 'ABCD']
        cfg['banks'] = [p[:, h * 512:(h + 1) * 512] for p in pss for h in range(2)]
        for n, p in zip('ABCD', pss):
            cfg['ps' + n] = p
        phase_A(k, cfg)
        P.barrier()
        k.arena_reset(mark)
        phase_B(k, cfg)
        P.final_wait_all_dma('sp')
        P.emit()
    return nc, CS


def kernel(**inputs):
    inp = {kk: np.asarray(v) for kk, v in inputs.items()}
    if 'nc' not in _CACHE:
        _CACHE['nc'] = build()
    nc, CS = _CACHE['nc']
    W = weights(inp)
    in_maps = []
    for c in range(8):
        m = {}
        m.update(CS)
        m.update(W)
        m.update(percore(inp, c))
        in_maps.append(m)
    res = run_bass_kernel_spmd(nc, in_maps, core_ids=list(range(8)))
    us = [unpack(r) for r in res.results]
    f = np.float32
    out = (
        np.stack([u['y_p'] for u in us], 0).astype(f),
        np.concatenate([u['y_s'] for u in us], 0).astype(f),
        np.stack([u['pa_wkv'] for u in us], 0)[None].astype(f),
        np.stack([u['pa_shift'] for u in us], 0)[None].astype(f),
        np.stack([u['pb_ssm'] for u in us], 0)[None].astype(f),
        np.stack([u['pb_conv'] for u in us], 0)[None].astype(f),
        np.stack([u['pf_conv'] for u in us], 0)[None].astype(f),
        np.concatenate([u['sa_wkv'] for u in us], 0)[None].astype(f),
        np.concatenate([u['sa_shift'] for u in us], 0)[None].astype(f),
        np.concatenate([u['sb_ssm'] for u in us], 0)[None].astype(f),
        np.concatenate([u['sb_conv'] for u in us], 0)[None].astype(f),
        np.concatenate([u['sf_conv'] for u in us], 0)[None].astype(f),
    )
    return tuple(np.ascontiguousarray(o) for o in out)
```

```python
import numpy as np
import concourse.bass as bass
import concourse.mybir as mybir

F32 = mybir.dt.float32
BF16 = mybir.dt.bfloat16
AF = mybir.ActivationFunctionType
ALU = mybir.AluOpType
AX = mybir.AxisListType

ENGS = ['pe', 'act', 'dve', 'pool', 'sp']
NDMASEM = 40


ARENA_ALLOCS = []


def _box(ap):
    t = ap.tensor
    name = t.name
    pairs = [tuple(x) for x in ap.ap]
    space = str(ap.space)
    sz = mybir.dt.size(ap.dtype)
    if 'DRAM' in space.upper() or 'HBM' in space.upper():
        lo = ap.offset
        hi = lo
        for st, n in pairs:
            hi += abs(st) * (n - 1)
        return (name, 0, 1, lo * sz, (hi + 1) * sz)
    shp = list(t.shape)
    pstride = 1
    for s_ in shp[1:]:
        pstride *= s_
    p0 = ap.offset // pstride
    f0 = ap.offset % pstride
    np_ = pairs[0][1]
    ext = 0
    for st, n in pairs[1:]:
        ext += abs(st) * (n - 1)
    b0 = f0 * sz
    b1 = (f0 + ext + 1) * sz
    if 'PSUM' in space.upper() or type(t).__name__.startswith('PSum'):
        b0 = (b0 // 2048) * 2048
        b1 = ((b1 + 2047) // 2048) * 2048
    if name == 'arena':
        for (lo, hi, key) in ARENA_ALLOCS:
            if lo <= b0 < hi:
                if b1 > hi:
                    raise RuntimeError('access spans allocations: %s' % key)
                return (key, p0, p0 + np_, b0, b1)
        raise RuntimeError('arena access outside allocations')
    return (name, p0, p0 + np_, b0, b1)


def _ovl(a, b):
    return a[1] < b[2] and b[1] < a[2] and a[3] < b[4] and b[3] < a[4]


def _covers(a, b):
    return a[1] <= b[1] and a[2] >= b[2] and a[3] <= b[3] and a[4] >= b[4]


class _Op:
    __slots__ = ('eng', 'idx', 'sem', 'val', 'clock')


class Prog:
    def __init__(self, nc, same_engine_dist=2):
        self.nc = nc
        self.streams = {e: [] for e in ENGS}
        self.nops = {e: 0 for e in ENGS}
        self.know = {e: {} for e in ENGS}
        self.acc = {}
        self.dma_last = [None] * NDMASEM
        self.dma_cnt = [0] * NDMASEM
        self.dma_rr = 0
        self.sed = same_engine_dist
        self.nwaits = 0
        self.last_op = {}
        self.direct = {e: {} for e in ENGS}

    def _deps(self, reads, writes, eng=None):
        deps = []
        self._raw = set()
        for ap in reads:
            b = _box(ap)
            isps = b[0].startswith('ps')
            for (bb, op, w) in self.acc.get(b[0], ()):
                if (w or (isps and op.eng != eng)) and _ovl(b, bb):
                    deps.append(op)
                    self._raw.add(id(op))
        for ap in writes:
            b = _box(ap)
            for (bb, op, w) in self.acc.get(b[0], ()):
                if _ovl(b, bb):
                    deps.append(op)
        return deps

    def _record(self, op, reads, writes):
        for ap in writes:
            b = _box(ap)
            lst = self.acc.setdefault(b[0], [])
            lst[:] = [x for x in lst if not _covers(b, x[0])]
            lst.append((b, op, True))
        for ap in reads:
            b = _box(ap)
            lst = self.acc.setdefault(b[0], [])
            lst[:] = [x for x in lst if not ((not x[2]) and x[1].eng == op.eng and x[1].sem == op.sem and _covers(b, x[0]))]
            lst.append((b, op, False))
            if len(lst) > 400:
                raise RuntimeError('access list too long for ' + b[0])

    def _wait_for(self, eng, deps, my_idx, is_dma=False):
        kn = self.know[eng]
        for d in sorted(deps, key=lambda o: -o.val):
            if d.eng == eng and d.sem == ('E', eng):
                if eng == 'pe':
                    continue
                pass
            if d.sem[0] == 'D':
                if self.direct[eng].get(d.sem, 0) >= d.val:
                    continue
                self.direct[eng][d.sem] = d.val
            elif kn.get(d.sem, 0) >= d.val:
                continue
            self.streams[eng].append(('wait', d.sem, d.val))
            self.nwaits += 1
            for k, v in d.clock.items():
                if kn.get(k, 0) < v:
                    kn[k] = v

    def op(self, eng, fn, reads=(), writes=()):
        deps = self._deps(reads, writes, eng)
        idx = self.nops[eng]
        self._wait_for(eng, deps, idx)
        o = _Op()
        o.eng = eng
        o.idx = idx
        o.sem = ('E', eng)
        o.val = idx + 1
        self.nops[eng] = idx + 1
        o.clock = dict(self.know[eng])
        o.clock[o.sem] = o.val
        self.streams[eng].append(('op', fn, o.sem, 1))
        self.last_op[eng] = o
        self._record(o, reads, writes)
        return o

    def dma(self, eng, out, in_, **kw):
        reads = [in_]
        writes = [out]
        deps = self._deps(reads, writes)
        kn = self.know[eng]
        pick = None
        half = NDMASEM // 2
        base = half if eng == 'pool' else 0
        if not hasattr(self, 'dma_rrs'):
            self.dma_rrs = {0: 0, half: 0}
        rr = self.dma_rrs[base]
        for k in range(half):
            j = base + (rr + k) % half
            last = self.dma_last[j]
            if last is None or kn.get(last.sem, 0) >= last.val:
                pick = j
                break
        if pick is None:
            pick = base + rr % half
            deps = deps + [self.dma_last[pick]]
        self.dma_rrs[base] = (pick - base + 1) % half
        self._wait_for(eng, deps, self.nops[eng], True)
        o = _Op()
        o.eng = eng
        o.idx = -1
        o.sem = ('D', pick)
        self.dma_cnt[pick] += 16
        o.val = self.dma_cnt[pick]
        o.clock = dict(kn)
        o.clock[o.sem] = o.val
        self.dma_last[pick] = o
        self.streams[eng].append(('dma', (out, in_, kw), o.sem, 16))
        self._record(o, reads, writes)
        return o

    def barrier(self):
        deps = [o for o in self.last_op.values()] + [o for o in self.dma_last if o is not None]
        for e in ENGS:
            kn = self.know[e]
            for d in sorted(deps, key=lambda o: -o.val):
                if d.sem == ('E', e):
                    continue
                if kn.get(d.sem, 0) >= d.val:
                    continue
                self.streams[e].append(('wait', d.sem, d.val))
                for k_, v in d.clock.items():
                    if kn.get(k_, 0) < v:
                        kn[k_] = v
        self.acc = {}

    def final_wait_all_dma(self, eng='sp'):
        for j in range(NDMASEM):
            last = self.dma_last[j]
            if last is not None and self.know[eng].get(last.sem, 0) < last.val:
                self.streams[eng].append(('wait', last.sem, last.val))
                self.know[eng][last.sem] = last.val

    def emit(self):
        nc = self.nc
        import contextlib
        with contextlib.ExitStack() as st:
            sems = {}
            for e in ENGS:
                sems[('E', e)] = st.enter_context(nc.semaphore('s_' + e))
            for j in range(NDMASEM):
                if self.dma_last[j] is not None:
                    sems[('D', j)] = st.enter_context(nc.semaphore('d_%d' % j))
            block = st.enter_context(nc.Block())

            def run(engname):
                def body(eng):
                    for item in self.streams[engname]:
                        if item[0] == 'wait':
                            eng.wait_ge(sems[item[1]], item[2])
                        elif item[0] == 'op':
                            item[1](eng).then_inc(sems[item[2]], item[3])
                        else:
                            out, in_, kw = item[1]
                            eng.dma_start(out=out, in_=in_, **kw).then_inc(sems[item[2]], item[3])
                return body

            block.tensor(run('pe'))
            block.scalar(run('act'))
            block.vector(run('dve'))
            block.gpsimd(run('pool'))
            block.sync(run('sp'))


import contextlib

DN_ALPHA = 2.0 ** 0.25
LN_EPS = 1e-5
GN_EPS = 64e-5
RMS_EPS = 1e-6
L2_EPS = 1e-12
NT = 17
TOK = 2176


class KB:
    def __init__(self, nc, P, st):
        self.nc, self.P, self.st = nc, P, st
        self.d = {}

    def din(self, name, shape, dt=F32):
        self.d[name] = self.nc.dram_tensor(name, list(shape), dt, kind='ExternalInput').ap()
        return self.d[name]

    def dout(self, name, shape, dt=F32):
        self.d[name] = self.nc.dram_tensor(name, list(shape), dt, kind='ExternalOutput').ap()
        return self.d[name]

    def dint(self, name, shape, dt=F32):
        self.d[name] = self.nc.dram_tensor(name, list(shape), dt, kind='Internal').ap()
        return self.d[name]

    def init_arena(self, nfp32=53200):
        self.arena = self.st.enter_context(self.nc.sbuf_tensor('arena', [128, nfp32], F32))
        self.acap = nfp32
        self.acur = 0
        ARENA_ALLOCS.clear()

    def arena_reset(self, cur):
        ARENA_ALLOCS[:] = [a for a in ARENA_ALLOCS if a[1] <= cur * 4]
        self.acur = cur

    def sb(self, name, shape, dt=F32):
        shape = list(shape)
        n = 1
        for s_ in shape[1:]:
            n *= s_
        nb = n * mybir.dt.size(dt)
        nf = (nb + 3) // 4
        if self.acur + nf > self.acap:
            raise RuntimeError('arena overflow at %s: need %d have %d' % (name, nf, self.acap - self.acur))
        v = self.arena[:, self.acur:self.acur + nf]
        ARENA_ALLOCS.append((self.acur * 4, (self.acur + nf) * 4, name))
        self.acur += nf
        if dt != F32:
            v = v.bitcast(dt)
            v = v[:, 0:n]
        if len(shape) == 3:
            v = v.rearrange('p (a b) -> p a b', a=shape[1], b=shape[2])
        elif len(shape) == 4:
            v = v.rearrange('p (a b c) -> p a b c', a=shape[1], b=shape[2], c=shape[3])
        return v

    def ps(self, name, shape, dt=F32):
        return self.st.enter_context(self.nc.psum_tensor(name, list(shape), dt))

    def mm(self, out, lhsT, rhs, start=True, stop=True):
        self.P.op('pe', lambda e: e.matmul(out, lhsT, rhs, start=start, stop=stop), reads=[lhsT, rhs], writes=[out])

    def tt(self, eng, out, a, b, op):
        self.P.op(eng, lambda e: e.tensor_tensor(out=out, in0=a, in1=b, op=op), reads=[a, b], writes=[out])

    def ts(self, eng, out, a, s1, op0, s2=None, op1=None):
        rd = [a] + [s for s in (s1, s2) if not isinstance(s, (int, float, type(None)))]
        if op1 is None:
            self.P.op(eng, lambda e: e.tensor_scalar(out=out, in0=a, scalar1=s1, scalar2=None, op0=op0), reads=rd, writes=[out])
        else:
            self.P.op(eng, lambda e: e.tensor_scalar(out=out, in0=a, scalar1=s1, scalar2=s2, op0=op0, op1=op1), reads=rd, writes=[out])

    def stt(self, eng, out, in0, scalar, in1, op0, op1):
        eng = 'dve'
        rd = [in0, in1] + ([] if isinstance(scalar, (int, float)) else [scalar])
        self.P.op(eng, lambda e: e.scalar_tensor_tensor(out=out, in0=in0, scalar=scalar, in1=in1, op0=op0, op1=op1), reads=rd, writes=[out])

    def act(self, out, in_, func, bias=0.0, scale=1.0):
        rd = [in_] + [s for s in (bias, scale) if not isinstance(s, (int, float))]
        self.P.op('act', lambda e: e.activation(out=out, in_=in_, func=func, bias=bias, scale=scale), reads=rd, writes=[out])

    def cp(self, eng, out, in_):
        if eng == 'dve':
            eng = 'act'
        if eng == 'act':
            self.P.op('act', lambda e: e.copy(out=out, in_=in_), reads=[in_], writes=[out])
        else:
            self.P.op(eng, lambda e: e.tensor_copy(out=out, in_=in_), reads=[in_], writes=[out])

    def memset(self, eng, ap, v):
        self.P.op(eng, lambda e: e.memset(ap, v), reads=[], writes=[ap])

    def dma(self, eng, out, in_, **kw):
        self.P.dma(eng, out, in_, **kw)

    def rsqrt(self, out, in_, eps):
        self.act(out, in_, AF.Ln, bias=eps)
        self.act(out, out, AF.Exp, scale=-0.5)

    def rownorm_stats(self, src, mv, stats):
        for c in range(2):
            o = stats[:, c, :]
            i = src[:, c * 512:(c + 1) * 512]
            self.P.op('dve', lambda e, o=o, i=i: e.bn_stats(out=o, in_=i), reads=[i], writes=[o])
        sv = stats[:, :, :]
        self.P.op('dve', lambda e: e.bn_aggr(out=mv, in_=sv), reads=[sv], writes=[mv])


def phase_B(k, cfg):
    d = k.d
    wup = k.sb('wup', [128, 8, 5632], BF16)
    wdn = k.sb('wdn', [128, 22, 1024], BF16)
    wpg = k.sb('wpg', [128, 8, 1024], BF16)
    wpl = k.sb('wpl', [128, 2, 1024], BF16)
    for kc in range(8):
        k.dma('pool', wup[:, kc, :], d['w_up'][kc])
    for kc in range(22):
        k.dma('pool', wdn[:, kc, :], d['w_down'][kc])
    for kc in range(8):
        k.dma('pool', wpg[:, kc, :], d['w_ple_gate'][kc])
    for kc in range(2):
        k.dma('pool', wpl[:, kc, :], d['w_ple'][kc])
    cw = k.sb('f_cw', [128, 22, 3])
    cb = k.sb('f_cb', [128, 22])
    k.dma('sp', cw[:], d['f_conv_w'])
    k.dma('sp', cb[:], d['f_conv_b'])
    l2g = k.sb('l2g', [128, 1024]); l2b = k.sb('l2b', [128, 1024]); plg = k.sb('plg', [128, 1024])
    k.dma('sp', l2g[:], d['ln2_g'].partition_broadcast(128))
    k.dma('sp', l2b[:], d['ln2_b'].partition_broadcast(128))
    k.dma('sp', plg[:], d['ple_g'].partition_broadcast(128))
    identF = cfg['identF']
    prev2 = k.sb('prev2', [128, 22, 2])
    k.memset('pool', prev2[:], 0.0)
    sfc_in = k.sb('sfc_in', [128, 22, 16, 2])
    k.dma('sp', sfc_in[:], d['s_fconv'])
    sfc_out = sfc_in
    x1ts = [k.sb('b_x1t%d' % j, [128, 1024]) for j in range(2)]
    x1Ts = [k.sb('b_x1T%d' % j, [128, 8, 128], BF16) for j in range(2)]
    pTt = k.sb('b_pT', [128, 2, 128], BF16)
    gb = [k.sb('b_gb%d' % i, [128, 4, 130]) for i in range(2)]
    gs1 = k.sb('b_gs1', [128, 4, 128]); gs2 = k.sb('b_gs2', [128, 4, 128])
    tmpc = [k.sb('b_tmpc%d' % i, [128, 4, 128]) for i in range(2)]
    hhT = k.sb('b_hhT', [128, 22, 128], BF16)
    z2 = k.sb('b_z2', [128, 1024])
    ebuf = k.sb('b_e', [128, 1024])
    stats = k.sb('b_stats', [128, 2, 6]); mv = k.sb('b_mv', [128, 2]); rstd = k.sb('b_rstd', [128, 1])
    stats2 = k.sb('b_stats2', [128, 2, 6]); mv2 = k.sb('b_mv2', [128, 2]); rstd2 = k.sb('b_rstd2', [128, 1])
    psA, psB, psC, psD = cfg['psA'], cfg['psB'], cfg['psC'], cfg['psD']

    tl = list(cfg.get('tiles', range(NT)))

    def head(n):
        r = tl[n] * 128
        xt_, xT_ = x1ts[n % 2], x1Ts[n % 2]
        k.dma('sp', xt_[:], d['x1s'][r:r + 128, :])
        for kc in range(8):
            k.mm(psA[:, kc * 128:(kc + 1) * 128], xt_[:, kc * 128:(kc + 1) * 128], identF[:])
        k.cp('act', xT_[:].rearrange('p a b -> p (a b)'), psA[:, :])

    for i in tl:
        samp = (i == NT - 1)
        r0 = i * 128
        n_ = tl.index(i)
        x1t = x1ts[n_ % 2]
        x1T = x1Ts[n_ % 2]
        x2T = x1T
        sg = x1t
        if n_ == 0:
            head(0)
        for kc in range(2):
            k.dma('pool', pTt[:, kc, :], d['pT'][kc, :, r0:r0 + 128])
        for grp in range(6):
            ng = min(4, 22 - 4 * grp)
            bank = grp % 2
            pg = psB[:, bank * 512: bank * 512 + ng * 128]
            pu = psC[:, bank * 512: bank * 512 + ng * 128]
            for j in range(ng):
                f = 4 * grp + j
                for kc in range(8):
                    k.mm(psB[:, bank * 512 + j * 128: bank * 512 + (j + 1) * 128], wup[:, kc, f * 128:(f + 1) * 128], x1T[:, kc, :], start=(kc == 0), stop=(kc == 7))
            for j in range(ng):
                f = 22 + 4 * grp + j
                for kc in range(8):
                    k.mm(psC[:, bank * 512 + j * 128: bank * 512 + (j + 1) * 128], wup[:, kc, f * 128:(f + 1) * 128], x1T[:, kc, :], start=(kc == 0), stop=(kc == 7))
            g = gb[grp % 2]
            tc_ = tmpc[grp % 2]
            k.cp('act', g[:, 0:ng, 2:130], pg.rearrange('p (a b) -> p a b', b=128))
            if not samp:
                k.cp('pool', g[:, 0:ng, 0:2], prev2[:, 4 * grp:4 * grp + ng, :])
            else:
                g4 = g[:, 0:ng, 2:130].rearrange('p a (s t) -> p a s t', t=8)
                s1 = gs1[:, 0:ng, :].rearrange('p a (s t) -> p a s t', t=8)
                s2 = gs2[:, 0:ng, :].rearrange('p a (s t) -> p a s t', t=8)
                for j in range(ng):
                    f = 4 * grp + j
                    k.cp('pool', s1[:, j, :, 1:8], g4[:, j, :, 0:7])
                    k.cp('pool', s1[:, j, :, 0:1], sfc_in[:, f, :, 1:2])
                    k.cp('pool', s2[:, j, :, 2:8], g4[:, j, :, 0:6])
                    k.cp('pool', s2[:, j, :, 0:2], sfc_in[:, f, :, 0:2])
                    k.cp('pool', sfc_out[:, f, :, :], g4[:, j, :, 6:8])
            for j in range(ng):
                f = 4 * grp + j
                if not samp:
                    G0, G1, G2 = g[:, j, 2:130], g[:, j, 1:129], g[:, j, 0:128]
                else:
                    G0, G1, G2 = g[:, j, 2:130], gs1[:, j, :], gs2[:, j, :]
                t = tc_[:, j, :]
                k.ts('dve', t, G2, cw[:, f, 0:1], ALU.mult)
                k.stt('dve', t, G1, cw[:, f, 1:2], t, ALU.mult, ALU.add)
                k.stt('dve', t, G0, cw[:, f, 2:3], t, ALU.mult, ALU.add)
                k.act(t, t, AF.Silu, bias=cb[:, f:f + 1])
                k.tt('dve', hhT[:, f, :], t, psC[:, bank * 512 + j * 128: bank * 512 + (j + 1) * 128], ALU.mult)
            if not samp:
                k.cp('pool', prev2[:, 4 * grp:4 * grp + ng, :], g[:, 0:ng, 128:130])
        if i == NT - 2:
            k.dma('sp', d['o_pfconv'], prev2[:])
        if samp:
            k.dma('sp', d['o_sfconv'], sfc_out[:])
        for half in range(2):
            for f in range(22):
                k.mm(psD[:, half * 512:(half + 1) * 512], hhT[:, f, :], wdn[:, f, half * 512:(half + 1) * 512], start=(f == 0), stop=(f == 21))
        k.stt('dve', z2[:], x1t[:], DN_ALPHA, psD[:, :], ALU.mult, ALU.add)
        if n_ + 1 < len(tl):
            head(n_ + 1)
        k.rownorm_stats(z2, mv[:], stats)
        k.rsqrt(rstd[:], mv[:, 1:2], LN_EPS)
        k.ts('dve', z2[:], z2[:], mv[:, 0:1], ALU.subtract, rstd[:, 0:1], ALU.mult)
        k.tt('dve', z2[:], z2[:], l2g[:], ALU.mult)
        k.tt('dve', z2[:], z2[:], l2b[:], ALU.add)
        for kc in range(8):
            k.mm(psA[:, kc * 128:(kc + 1) * 128], z2[:, kc * 128:(kc + 1) * 128], identF[:])
        k.cp('act', x2T[:].rearrange('p a b -> p (a b)'), psA[:, :])
        for half in range(2):
            for kc in range(8):
                k.mm(psB[:, half * 512:(half + 1) * 512], x2T[:, kc, :], wpg[:, kc, half * 512:(half + 1) * 512], start=(kc == 0), stop=(kc == 7))
        for half in range(2):
            for kc in range(2):
                k.mm(psC[:, half * 512:(half + 1) * 512], pTt[:, kc, :], wpl[:, kc, half * 512:(half + 1) * 512], start=(kc == 0), stop=(kc == 1))
        k.rownorm_stats(psC, mv2[:], stats2)
        k.stt('dve', rstd2[:], mv2[:, 0:1], mv2[:, 0:1], mv2[:, 1:2], ALU.mult, ALU.add)
        k.rsqrt(rstd2[:], rstd2[:], RMS_EPS)
        k.stt('dve', ebuf[:], psC[:, :], rstd2[:, 0:1], plg[:], ALU.mult, ALU.mult)
        k.act(sg[:], psB[:, :], AF.Sigmoid)
        k.tt('dve', ebuf[:], ebuf[:], sg[:], ALU.mult)
        k.tt('dve', ebuf[:], ebuf[:], z2[:], ALU.add)
        k.dma('sp', d['y'][r0:r0 + 128, :], ebuf[:])


class _Rec:
    def __init__(self):
        self.calls = []

    def op(self, *a, **kw):
        self.calls.append(('op', a, kw))

    def dma(self, *a, **kw):
        self.calls.append(('dma', a, kw))


def record_calls(k, fn):
    real = k.P
    rec = _Rec()
    k.P = rec
    try:
        fn()
    finally:
        k.P = real
    return rec.calls


def replay_interleaved(k, lists):
    idx = [0] * len(lists)
    while any(idx[j] < len(lists[j]) for j in range(len(lists))):
        for j, l in enumerate(lists):
            if idx[j] < len(l):
                kind, a, kw = l[idx[j]]
                idx[j] += 1
                getattr(k.P, kind)(*a, **kw)


def phase_A(k, cfg):
    fl = lambda a: a.rearrange('p a b -> p (a b)')
    d = k.d
    P = k.P
    identF = cfg['identF']
    banks = cfg['banks']
    bstate = [0]

    def bank():
        b = banks[bstate[0] % 8]
        bstate[0] += 1
        return b

    mark0 = k.acur
    wstage = k.sb('wstage', [128, 8, 3848], BF16)
    for kc in range(8):
        k.dma('pool', wstage[:, kc, :], d['w_in'][kc])
    for kc in range(8):
        k.dma('sp', d['w_in_bf'][kc], wstage[:, kc, :])
    P.barrier()
    k.arena_reset(mark0)
    winbf = d['w_in_bf'].rearrange('k p f -> p k f')
    wbuf = [k.sb('wbuf%d' % j, [128, 8, 512], BF16) for j in range(2)]
    wo = k.sb('wo', [128, 8, 1024], BF16)
    for kc in range(8):
        k.dma('pool', wo[:, kc, :], d['w_o'][kc])
    identB = k.sb('identB', [128, 128], BF16); k.dma('pool', identB, d['identF'])
    MS = [k.sb('MS%d' % i, [128, 128]) for i in range(2)]
    MI = [k.sb('MI%d' % i, [128, 128]) for i in range(2)]
    MST = [k.sb('MST%d' % i, [128, 128]) for i in range(2)]
    rmask = [k.sb('rmask%d' % i, [128, 512]) for i in range(2)]
    for i in range(2):
        k.dma('sp', MS[i], d['MS'][i]); k.dma('sp', MI[i], d['MI'][i]); k.dma('sp', MST[i], d['MST'][i]); k.dma('sp', rmask[i], d['rmask'][i])
    Bones = k.sb('Bones', [128, 128]); k.dma('sp', Bones, d['Bones'])
    onesF = k.sb('onesF', [128, 128]); k.dma('sp', onesF, d['onesF'])
    segm = k.sb('segm', [128, 16]); k.dma('sp', segm, d['segm'])
    mu = k.sb('a_mu', [128, 14]); k.dma('sp', mu, d['a_mu'])
    pv = k.sb('a_pv', [128, 7, 4]); k.dma('sp', pv, d['a_pv'])
    nw0 = k.sb('a_nw0', [128, 4]); k.ts('dve', nw0, pv[:, 0, :], -1.0, ALU.mult)
    omka = k.sb('a_omka', [128, 4]); k.ts('dve', omka, pv[:, 3, :], -1.0, ALU.mult, 1.0, ALU.add)
    wlora = k.sb('wlora', [128, 512], BF16); k.dma('pool', wlora, d['a_wlora'])
    wg2 = k.sb('wg2', [128, 512], BF16); k.dma('pool', wg2, d['a_w_g2'])
    cwb = k.sb('b_cw', [128, 12, 4]); k.dma('sp', cwb, d['b_conv_w'])
    alog = k.sb('b_alog', [128, 4]); k.dma('sp', alog, d['b_a_log'].partition_broadcast(128))
    dtb = k.sb('b_dtb', [128, 4]); k.dma('sp', dtb, d['b_dt_bias'].partition_broadcast(128))
    negA = k.sb('b_negA', [128, 4]); k.act(negA, alog, AF.Exp); k.ts('dve', negA, negA, -1.0, ALU.mult)
    ng = k.sb('b_ng', [128, 1]); k.dma('sp', ng, d['b_norm_g'])
    l1g = k.sb('l1g', [128, 1024]); l1b = k.sb('l1b', [128, 1024])
    k.dma('sp', l1g, d['ln1_g'].partition_broadcast(128)); k.dma('sp', l1b, d['ln1_b'].partition_broadcast(128))
    onecol = onesF[:, 0:1]
    ash_in = k.sb('ash_in', [128, 14, 16]); k.dma('sp', ash_in, d['s_ashift'])
    bcv_in = k.sb('bcv_in', [128, 12, 16, 3]); k.dma('sp', bcv_in, d['s_bconv'])
    S = [k.sb('S%d' % v, [128, 128]) for v in range(8)]
    for v in range(8):
        k.memset('pool', S[v], 0.0)
    xT = k.sb('xT', [128, 8, 128], BF16)
    xt = k.sb('xt', [128, 1024])
    uT = k.sb('uT', [128, 14, 129]); k.memset('pool', uT[:, :, 128:129], 0.0)
    xs = k.sb('xs', [128, 14, 128])
    qkvp = k.sb('qkvp', [128, 12, 131]); k.memset('pool', qkvp[:, :, 128:131], 0.0)
    big = k.sb('big', [128, 4608])
    gsh = [big[:, j * 1536:(j + 1) * 1536].rearrange('p (a b) -> p a b', a=12, b=128) for j in range(3)]
    prevS = big[:, 0:1792].rearrange('p (a b) -> p a b', a=14, b=128)
    KdM = big[:, 0:1024].bitcast(BF16).rearrange('p (a b) -> p a b', a=16, b=128)
    KtM = big[:, 1024:2048].bitcast(BF16).rearrange('p (a b) -> p a b', a=16, b=128)
    Sin = big[:, 2048:4096].rearrange('p (a b) -> p a b', a=16, b=128)
    Sout = Sin
    rwtmp = k.sb('rwtmp', [128, 3584])
    cv = k.sb('cv', [128, 12, 128])
    zT = k.sb('zT', [128, 4, 128])
    batok = k.sb('batok', [128, 8])
    F4 = lambda n: k.sb(n, [128, 4, 128])
    B4 = lambda n: k.sb(n, [128, 4, 128], BF16)
    t_lw, t_L, t_a, t_kk, t_kp, t_b, t_tmp = [rwtmp[:, j * 512:(j + 1) * 512].rearrange('p (a b) -> p a b', a=4, b=128) for j in range(7)]
    t_eL, t_g, t_bon = [F4('t_%d' % j) for j in range(3)]
    rt, kkt, bt, kt, vb = [B4('r_%d' % j) for j in range(5)]
    lorain = k.sb('lorain', [128, 128], BF16); sgd = k.sb('sgd', [128, 128], BF16)
    KKtok, Btok, Ktok, Vtok = [k.sb('tok%d' % j, [128, 512], BF16) for j in range(4)]
    tmp8 = k.sb('tmp8', [128, 8, 128])
    qh = F4('qh'); qhb = B4('qhb'); khb = B4('khb'); vgb = B4('vgb')
    Win = B4('Win'); Yin = B4('Yin'); Kdg = B4('Kdg')
    beta = k.sb('beta', [128, 4]); gtk = k.sb('gtk', [128, 4]); Gtok = k.sb('Gtok', [128, 4]); eGtok = k.sb('eGtok', [128, 4])
    eGs = k.sb('eGs', [128, 4]); beG = k.sb('beG', [128, 4])
    eGbc = [k.sb('eGbc%d' % h, [128, 128]) for h in range(4)]
    QTg = F4('QTg')

    def mkset(alloc_f, alloc_b):
        B = {}
        B['gd'] = [alloc_f() for _ in range(5)]
        B['inv'] = [alloc_f() for _ in range(8)]
        B['TTb'] = [alloc_b() for _ in range(2)]
        B['ATb'] = [alloc_b() for _ in range(2)]
        B['A2Tb'] = [alloc_b() for _ in range(2)]
        B['AakT'] = alloc_b()
        B['XYin'] = alloc_b(256)
        B['nwv'] = alloc_b(); B['uv'] = alloc_b()
        for n in ('ZT', 'OcT', 'OT', 'PhiT', 'Psi', 'Idg'):
            B[n] = alloc_f()
        B['gtmp'] = [alloc_f() for _ in range(3)]
        return B
    cnt = [0]

    def a0f():
        cnt[0] += 1
        return k.sb('s0f%d' % cnt[0], [128, 128])

    def a0b(n=128):
        cnt[0] += 1
        return k.sb('s0b%d' % cnt[0], [128, n], BF16)
    bigcur = [0]

    def a1f():
        v = big[:, bigcur[0]:bigcur[0] + 128]
        bigcur[0] += 128
        return v

    def a1b(n=128):
        v = big[:, bigcur[0]:bigcur[0] + n // 2].bitcast(BF16)
        bigcur[0] += n // 2
        return v
    BS = [mkset(a0f, a0b), mkset(a1f, a1b), mkset(a0f, a0b), mkset(a0f, a0b)]
    assert bigcur[0] <= 4608, bigcur[0]
    oT = k.sb('oT', [128, 8, 128], BF16)
    z1 = xt
    stats = k.sb('a_stats', [128, 2, 6]); mv = k.sb('a_mv', [128, 2]); rstd = k.sb('a_rstd', [128, 1])
    ashs = k.sb('ashs', [128, 14, 16]); bcvs = k.sb('bcvs', [128, 12, 16, 3])
    pash = k.sb('pash', [128, 14, 1])

    bst = [0, 0, 0, 0]

    def mkbank(l):
        def f():
            b = banks[2 * l + bst[l] % 2]
            bst[l] += 1
            return b
        return f
    lbank = [mkbank(l) for l in range(4)]
    pst = [0, 0]

    def mkpbank(l):
        def f():
            b = banks[4 * l + pst[l] % 4]
            pst[l] += 1
            return b
        return f
    pbank = [mkpbank(0), mkpbank(1)]

    tlA = list(cfg.get('tiles', range(NT)))
    for i in tlA:
        samp = (i == NT - 1)
        mi = 1 if samp else 0
        blk = 8 if samp else 64
        nseg = 128 // blk
        nlev = 2 if samp else 5
        if cfg.get('nlev') is not None:
            nlev = cfg['nlev']
        r0 = i * 128
        if cfg.get('tilebar'):
            P.barrier()
        nA = tlA.index(i)
        for kc in range(8):
            k.dma('pool', xT[:, kc, :], d['xT'][kc, :, r0:r0 + 128])
        k.dma('sp', xt, d['x'][r0:r0 + 128, :])
        if not samp:
            k.cp('pool', uT[:, :, 0:1], uT[:, :, 128:129])
            k.cp('pool', qkvp[:, :, 0:3], qkvp[:, :, 128:131])
        for grp in range(cfg.get('pgrp', 8)):
            nchunk = 4 if grp < 7 else 2
            b = bank()
            wb = wbuf[grp % 2]
            ncols = 512 if grp < 7 else 264
            if not (grp < 2 and nA > 0):
                k.dma('sp', wb[:, :, 0:ncols], winbf[:, :, grp * 512:grp * 512 + ncols])
            for j in range(nchunk):
                c = grp * 4 + j
                for kc in range(8):
                    k.mm(b[:, j * 128:(j + 1) * 128], wb[:, kc, j * 128:(j + 1) * 128], xT[:, kc, :], start=(kc == 0), stop=(kc == 7))
            for j in range(nchunk):
                c = grp * 4 + j
                src = b[:, j * 128:(j + 1) * 128]
                if c < 14:
                    k.cp('act', uT[:, c, 1:129], src)
                elif c < 26:
                    k.cp('act', qkvp[:, c - 14, 3:131], src)
                else:
                    k.act(zT[:, c - 26, :], src, AF.Silu)
        b = bank()
        for kc in range(8):
            k.mm(b[:, 0:8], xT[:, kc, :], wbuf[1][:, kc, 256:264], start=(kc == 0), stop=(kc == 7))
        k.cp('dve', batok, b[:, 0:8])
        if nA + 1 < len(tlA):
            for g_ in range(2):
                k.dma('sp', wbuf[g_][:, :, 0:512], winbf[:, :, g_ * 512:g_ * 512 + 512])
        def pre_rwkv(bank):
            cur = uT[:, :, 1:129]
            if not samp:
                prev = uT[:, :, 0:128]
            else:
                p4 = prevS.rearrange('p a (s t) -> p a s t', t=8)
                c4 = cur.rearrange('p a (s t) -> p a s t', t=8)
                for c in range(14):
                    k.cp('pool', p4[:, c, :, 1:8], c4[:, c, :, 0:7])
                    k.cp('pool', p4[:, c, :, 0:1], ash_in[:, c, :].rearrange('p (s o) -> p s o', o=1))
                    k.cp('pool', ashs[:, c, :].rearrange('p (s o) -> p s o', o=1), c4[:, c, :, 7:8])
                prev = prevS
                k.dma('sp', d['o_sashift'], ashs)
            if i == NT - 2:
                k.cp('pool', pash, uT[:, :, 128:129])
                k.dma('sp', d['o_pashift'], pash)
            for c in range(14):
                k.tt('pool', xs[:, c, :], prev[:, c, :], cur[:, c, :], ALU.subtract)
                k.stt('dve' if c % 2 else 'pool', xs[:, c, :], xs[:, c, :], mu[:, c:c + 1], cur[:, c, :], ALU.mult, ALU.add)
            r_, kx, v_ = xs[:, 0:4, :], xs[:, 4:8, :], xs[:, 8:12, :]
            k.act(lorain[0:64, :], xs[0:64, 12, :], AF.Tanh)
            k.cp('dve', lorain[64:128, :], xs[64:128, 12, :])
            k.act(sgd, xs[:, 13, :], AF.Sigmoid)
            bw, ba_, bg = bank(), bank(), bank()
            for c in range(4):
                cs = slice(c * 128, (c + 1) * 128)
                k.mm(bw[:, cs], wlora[0:64, cs], lorain[0:64, :])
                k.mm(ba_[:, cs], wlora[64:128, cs], lorain[64:128, :])
                k.mm(bg[:, cs], wg2[:, cs], sgd)
            for c in range(4):
                cs = slice(c * 128, (c + 1) * 128)
                k.act(t_lw[:, c, :], bw[:, cs], AF.Exp, bias=nw0[:, c:c + 1], scale=-1.0)
                k.act(t_a[:, c, :], ba_[:, cs], AF.Sigmoid, bias=pv[:, 1, c:c + 1])
            k.cp('dve', t_g.rearrange('p a b -> p (a b)'), bg[:, :])
            fl = lambda a: a.rearrange('p a b -> p (a b)')
            k.act(fl(t_lw), fl(t_lw), AF.Ln, bias=1.0)
            k.act(fl(t_lw), fl(t_lw), AF.Exp, bias=-0.5, scale=-1.0)
            k.ts('dve', fl(t_lw), fl(t_lw), -1.0, ALU.mult)
            for c in range(4):
                k.ts('dve', t_kk[:, c, :], kx[:, c, :], pv[:, 2, c:c + 1], ALU.mult)
            k.tt('pool', t_tmp, t_kk, t_kk, ALU.mult)
            b = bank()
            for c in range(4):
                k.mm(b[:, c * 128:(c + 1) * 128], Bones, t_tmp[:, c, :])
            k.rsqrt(fl(t_tmp), b[:, :], L2_EPS)
            k.tt('dve', t_kk, t_kk, t_tmp, ALU.mult)
            for c in range(4):
                k.ts('dve', t_tmp[:, c, :], t_a[:, c, :], pv[:, 3, c:c + 1], ALU.mult, omka[:, c:c + 1], ALU.add)
            k.tt('dve', t_kp, kx, t_tmp, ALU.mult)
            k.tt('pool', t_b, t_kk, t_a, ALU.mult)
            k.P.op('dve', lambda e, rm=rmask[mi]: e.tensor_tensor_scan(out=fl(t_L), data0=rm, data1=fl(t_lw), initial=0.0, op0=ALU.mult, op1=ALU.add),
                 reads=[rmask[mi], fl(t_lw)], writes=[fl(t_L)])
            k.tt('dve', t_lw, t_L, t_lw, ALU.subtract)
            k.act(fl(t_eL), fl(t_L), AF.Exp)
            k.act(fl(t_lw), fl(t_lw), AF.Exp)
            k.act(fl(t_L), fl(t_L), AF.Exp, scale=-1.0)
            k.tt('dve', rt, r_, t_eL, ALU.mult)
            k.tt('dve', kkt, t_kk, t_lw, ALU.mult)
            k.tt('pool', bt, t_b, t_L, ALU.mult)
            k.tt('pool', kt, t_kp, t_L, ALU.mult)
            k.cp('pool', vb, v_)
            k.tt('dve', t_tmp, r_, t_kp, ALU.mult)
            for c in range(4):
                k.ts('dve', t_tmp[:, c, :], t_tmp[:, c, :], pv[:, 4, c:c + 1], ALU.mult)
            b = bank()
            for c in range(4):
                k.mm(b[:, c * 128:(c + 1) * 128], Bones, t_tmp[:, c, :])
            k.tt('dve', fl(t_bon), b[:, :], fl(v_), ALU.mult)
            for src, dst in ((kkt, KKtok), (bt, Btok), (kt, Ktok), (vb, Vtok)):
                b = bank()
                for c in range(4):
                    k.mm(b[:, c * 128:(c + 1) * 128], src[:, c, :], identB)
                k.cp('act', dst, b[:, :])
        def pre_gdn(bank):
            if samp:
                q4 = qkvp[:, :, 3:131].rearrange('p a (s t) -> p a s t', t=8)
                for j in range(3):
                    g4 = gsh[j].rearrange('p a (s t) -> p a s t', t=8)
                    for c in range(12):
                        k.cp('pool', g4[:, c, :, j + 1:8], q4[:, c, :, 0:7 - j])
                        k.cp('pool', g4[:, c, :, 0:j + 1], bcv_in[:, c, :, 2 - j:3])
                for c in range(12):
                    k.cp('pool', bcvs[:, c, :, :], q4[:, c, :, 5:8])
                k.dma('sp', d['o_sbconv'], bcvs)
            if i == NT - 2:
                k.dma('sp', d['o_pbconv'], qkvp[:, :, 128:131])
            for c in range(12):
                if samp:
                    X0, X1, X2, X3 = qkvp[:, c, 3:131], gsh[0][:, c, :], gsh[1][:, c, :], gsh[2][:, c, :]
                else:
                    X0, X1, X2, X3 = qkvp[:, c, 3:131], qkvp[:, c, 2:130], qkvp[:, c, 1:129], qkvp[:, c, 0:128]
                e = 'dve' if c % 2 else 'pool'
                k.ts(e, cv[:, c, :], X3, cwb[:, c, 0:1], ALU.mult)
                k.stt(e, cv[:, c, :], X2, cwb[:, c, 1:2], cv[:, c, :], ALU.mult, ALU.add)
                k.stt(e, cv[:, c, :], X1, cwb[:, c, 2:3], cv[:, c, :], ALU.mult, ALU.add)
                k.stt(e, cv[:, c, :], X0, cwb[:, c, 3:4], cv[:, c, :], ALU.mult, ALU.add)
            k.act(fl(cv), fl(cv), AF.Silu)
            k.tt('pool', tmp8, cv[:, 0:8, :], cv[:, 0:8, :], ALU.mult)
            b1, b2 = bank(), bank()
            for c in range(8):
                bb = b1 if c < 4 else b2
                k.mm(bb[:, (c % 4) * 128:(c % 4 + 1) * 128], onesF, tmp8[:, c, :])
            k.rsqrt(fl(tmp8[:, 0:4, :]), b1[:, :], L2_EPS)
            k.rsqrt(fl(tmp8[:, 4:8, :]), b2[:, :], L2_EPS)
            k.stt('dve', qh, cv[:, 0:4, :], 128.0 ** -0.5, tmp8[:, 0:4, :], ALU.mult, ALU.mult)
            k.cp('pool', qhb, qh)
            k.tt('dve', khb, cv[:, 4:8, :], tmp8[:, 4:8, :], ALU.mult)
            k.cp('pool', vgb, cv[:, 8:12, :])
            k.act(beta, batok[:, 0:4], AF.Sigmoid)
            k.tt('dve', gtk, batok[:, 4:8], dtb, ALU.add)
            k.act(gtk, gtk, AF.Exp)
            k.act(gtk, gtk, AF.Ln, bias=1.0)
            k.tt('dve', gtk, gtk, negA, ALU.mult)
            b = bank()
            k.mm(b[:, 0:4], MI[mi], gtk)
            k.mm(b[:, 4:8], MST[mi], gtk)
            k.cp('dve', Gtok, b[:, 0:4])
            k.act(eGtok, b[:, 0:4], AF.Exp)
            k.act(eGs, b[:, 4:8], AF.Exp)
            k.tt('dve', beG, beta, eGtok, ALU.mult)
            bk_, bv_ = bank(), bank()
            for h in range(4):
                k.mm(bk_[:, h * 128:(h + 1) * 128], khb[:, h, :], identB)
                k.mm(bv_[:, h * 128:(h + 1) * 128], vgb[:, h, :], identB)
            for h in range(4):
                hs = slice(h * 128, (h + 1) * 128)
                k.ts('dve', Win[:, h, :], bk_[:, hs], beG[:, h:h + 1], ALU.mult)
                k.ts('dve', Kdg[:, h, :], bk_[:, hs], eGs[:, h:h + 1], ALU.mult)
                k.ts('dve', Yin[:, h, :], bv_[:, hs], beta[:, h:h + 1], ALU.mult)

        if samp or cfg.get('nopre'):
            pre_rwkv(bank); pre_gdn(bank)
        else:
            lists = [record_calls(k, lambda: pre_rwkv(pbank[0])), record_calls(k, lambda: pre_gdn(pbank[1]))]
            replay_interleaved(k, lists)
        def vh_gen(vh, B, bank):
            Um, Lm, Ua, La, Ub, Lb, Qm, Qn = B['inv']
            TTb, ATb, A2Tb = B['TTb'], B['ATb'], B['A2Tb']
            AakT, XYin, nwv, uv = B['AakT'], B['XYin'], B['nwv'], B['uv']
            ZT, OcT, OT, PhiT, Psi, Idg = B['ZT'], B['OcT'], B['OT'], B['PhiT'], B['Psi'], B['Idg']
            gtmp = B['gtmp']
            gB, bB, tmpD, E1, E2 = B['gd']

            def invert(TTout):
                k.tt('dve', Qm, identF, Um, ALU.subtract)
                Uk, Lk, Un, Ln_ = Um, Lm, Ua, La
                Q, Q2 = Qm, Qn
                for lev in range(nlev):
                    b = bank()
                    k.mm(b[:, 0:128], Uk, Lk)
                    if lev < nlev - 1:
                        k.mm(b[:, 128:256], Lk, Uk)
                    yield
                    k.cp('act', Ln_, b[:, 0:128])
                    if lev < nlev - 1:
                        k.cp('act', Un, b[:, 128:256])
                    yield
                    b2 = bank()
                    k.mm(b2[:, 0:128], Ln_, Q)
                    yield
                    k.tt('dve', Q2, b2[:, 0:128], Q, ALU.add)
                    Q, Q2 = Q2, Q
                    if Un is Ua:
                        Uk, Lk, Un, Ln_ = Ua, La, Ub, Lb
                    else:
                        Uk, Lk, Un, Ln_ = Ub, Lb, Ua, La
                    yield
                k.cp('act', TTout, Q)

            rw = vh < 4
            if rw:
                hp = vh
                subs = []
                for h2 in range(2):
                    rs = slice(h2 * 64, (h2 + 1) * 64)
                    hc = slice(hp * 128 + h2 * 64, hp * 128 + (h2 + 1) * 64)
                    b = bank()
                    k.mm(b[:, 0:128], bt[rs, hp, :], kkt[rs, hp, :])
                    k.mm(b[:, 128:256], kkt[rs, hp, :], bt[rs, hp, :])
                    k.mm(b[:, 256:384], kt[rs, hp, :], kkt[rs, hp, :])
                    yield
                    k.tt('dve', Um, b[:, 0:128], MS[mi], ALU.mult)
                    k.tt('dve', Lm, b[:, 128:256], MST[mi], ALU.mult)
                    k.tt('dve', AakT, b[:, 256:384], MS[mi], ALU.mult)
                    b = bank()
                    k.mm(b[:, 0:128], bt[rs, hp, :], rt[rs, hp, :])
                    k.mm(b[:, 128:256], kt[rs, hp, :], rt[rs, hp, :])
                    yield
                    k.tt('dve', ATb[h2], b[:, 0:128], MI[mi], ALU.mult)
                    k.tt('dve', A2Tb[h2], b[:, 128:256], MI[mi], ALU.mult)
                    b = bank()
                    k.mm(b[:, 0:64], AakT, Vtok[:, hc])
                    k.cp('pool', XYin[:, 0:64], KKtok[:, hc])
                    yield
                    k.act(XYin[:, 64:128], b[:, 0:64], AF.Copy, scale=-1.0)
                    yield from invert(TTb[h2])
                    yield
                    b = bank()
                    k.mm(b[:, 0:128], TTb[h2], XYin[:, 0:128])
                    yield
                    k.act(nwv[:, rs], b[:, 0:64], AF.Copy, scale=-1.0)
                    k.cp('act', uv[:, rs], b[:, 64:128])
                    subs.append((h2, rs))
                    yield
                Kd = Btok[:, hp * 128:(hp + 1) * 128]
                QT = rt[:, hp, :]
                BD = Bones
            else:
                h = vh - 4
                k.ts('dve', gB, onesF, gtk[:, h:h + 1], ALU.mult)
                k.ts('dve', bB, onesF, beta[:, h:h + 1], ALU.mult)
                bG = bank()
                k.mm(bG[:, 0:128], gB, MI[mi])
                k.mm(bG[:, 128:256], bB, identF)
                bKK = bank()
                k.mm(bKK[:, 0:128], khb[:, h, :], khb[:, h, :])
                k.mm(bKK[:, 128:256], khb[:, h, :], qhb[:, h, :])
                yield
                k.act(eGbc[h], bG[:, 0:128], AF.Exp)
                k.ts('dve', tmpD, bG[:, 0:128], Gtok[:, h:h + 1], ALU.subtract)
                k.ts('dve', E1, tmpD, 0.0, ALU.min)
                k.ts('dve', E2, tmpD, -1.0, ALU.mult, 0.0, ALU.min)
                yield
                k.act(E1, E1, AF.Exp)
                k.act(E2, E2, AF.Exp)
                k.tt('dve', QTg[:, h, :], qh[:, h, :], eGbc[h], ALU.mult)
                yield
                k.tt('dve', Um, E1, MS[mi], ALU.mult)
                k.tt('dve', Um, Um, bG[:, 128:256], ALU.mult)
                k.tt('dve', Um, Um, bKK[:, 0:128], ALU.mult)
                yield
                k.tt('dve', tmpD, E1, MI[mi], ALU.mult)
                k.tt('dve', ATb[0], tmpD, bKK[:, 128:256], ALU.mult)
                k.stt('dve', Lm, E2, beta[:, h:h + 1], MST[mi], ALU.mult, ALU.mult)
                k.tt('dve', Lm, Lm, bKK[:, 0:128], ALU.mult)
                yield
                yield from invert(TTb[0])
                k.cp('pool', XYin[:, 0:128], Win[:, h, :])
                k.cp('pool', XYin[:, 128:256], Yin[:, h, :])
                yield
                b = bank()
                k.mm(b[:, 0:256], TTb[0], XYin[:, 0:256])
                yield
                k.act(nwv, b[:, 0:128], AF.Copy, scale=-1.0)
                k.cp('act', uv, b[:, 128:256])
                subs = [(0, slice(0, 128))]
                Kd = Kdg[:, h, :]
                QT = QTg[:, h, :]
                BD = None
                yield
            bZ, bO = bank(), bank()
            for (h2, rs) in subs:
                k.mm(bZ[rs, 0:128], nwv[:, rs], ATb[h2])
                k.mm(bO[rs, 0:128], uv[:, rs], ATb[h2], start=True, stop=(not rw))
                if rw:
                    hc = slice(vh * 128 + h2 * 64, vh * 128 + (h2 + 1) * 64)
                    k.mm(bO[rs, 0:128], Vtok[:, hc], A2Tb[h2], start=False, stop=True)
            yield
            k.tt('dve', ZT, bZ[:, 0:128], QT, ALU.add)
            k.cp('act', OcT, bO[:, 0:128])
            yield
            if samp:
                k.dma('sp', Sin, d['s_state'][vh])
                for j in range(16):
                    k.ts('dve', KdM[:, j, :], Kd, segm[:, j:j + 1], ALU.mult)
                    if rw:
                        k.ts('dve', KtM[:, j, :], Ktok[:, vh * 128:(vh + 1) * 128], segm[:, j:j + 1], ALU.mult)
            for sg in range(nseg):
                cols = slice(sg * blk, (sg + 1) * blk)
                last = sg * blk + blk - 1
                Scur = Sin[:, sg, :] if samp else S[vh]
                Snew = Sout[:, sg, :] if samp else S[vh]
                if rw:
                    cP = t_eL[:, vh, last:last + 1]
                    IdgX = identF
                else:
                    cP = onecol
                    k.ts('dve', Idg, identF, eGbc[vh - 4][:, last:last + 1], ALU.mult)
                    IdgX = Idg
                if samp:
                    nwS, KdS, uS = nwv, KdM[:, sg, :], uv
                else:
                    rows = slice(sg * 64, (sg + 1) * 64)
                    nwS, KdS, uS = nwv[rows, :], Kd[rows, :], uv[rows, :]
                bP = bank()
                k.mm(bP[:, 0:128], nwS, KdS)
                k.mm(bP[:, 128:256], KdS, uS, start=True, stop=(not rw))
                if rw:
                    if samp:
                        k.mm(bP[:, 128:256], KtM[:, sg, :], Vtok[:, vh * 128:(vh + 1) * 128], start=False, stop=True)
                    else:
                        k.mm(bP[:, 128:256], Ktok[rows, vh * 128:(vh + 1) * 128], Vtok[rows, vh * 128:(vh + 1) * 128], start=False, stop=True)
                yield
                if BD is not None:
                    k.tt('dve', PhiT, bP[:, 0:128], BD, ALU.mult)
                    k.tt('dve', PhiT, PhiT, IdgX, ALU.add)
                    k.stt('dve', Psi, bP[:, 128:256], cP, BD, ALU.mult, ALU.mult)
                else:
                    k.tt('dve', PhiT, bP[:, 0:128], IdgX, ALU.add)
                    k.cp('act', Psi, bP[:, 128:256])
                yield
                bS = bank()
                k.mm(bS[:, 0:blk], Scur, ZT[:, cols])
                k.mm(bS[:, 128:256], PhiT, Scur)
                yield
                k.tt('dve', OT[:, cols], bS[:, 0:blk], OcT[:, cols], ALU.add)
                k.stt('dve', Snew, bS[:, 128:256], cP, Psi, ALU.mult, ALU.add)
                yield
            if samp:
                k.dma('sp', d['o_sstate'][vh], Sout)
            if rw:
                k.tt('pool', gtmp[0], OT, OT, ALU.mult)
                b = bank()
                k.mm(b[:, 0:128], Bones, OT)
                k.mm(b[:, 128:256], Bones, gtmp[0])
                yield
                k.act(gtmp[1], b[:, 0:128], AF.Copy, scale=1.0 / 64)
                k.tt('dve', gtmp[2], gtmp[1], gtmp[1], ALU.mult)
                k.stt('dve', gtmp[2], b[:, 128:256], 1.0 / 64, gtmp[2], ALU.mult, ALU.subtract)
                yield
                k.rsqrt(gtmp[2], gtmp[2], GN_EPS)
                k.tt('dve', gtmp[1], OT, gtmp[1], ALU.subtract)
                yield
                k.tt('dve', gtmp[1], gtmp[1], gtmp[2], ALU.mult)
                k.ts('dve', gtmp[1], gtmp[1], pv[:, 5, vh:vh + 1], ALU.mult, pv[:, 6, vh:vh + 1], ALU.add)
                yield
                k.tt('pool', gtmp[1], gtmp[1], t_bon[:, vh, :], ALU.add)
                k.tt('pool', oT[:, vh, :], gtmp[1], t_g[:, vh, :], ALU.mult)
            else:
                h = vh - 4
                k.tt('pool', gtmp[0], OT, OT, ALU.mult)
                b = bank()
                k.mm(b[:, 0:128], onesF, gtmp[0])
                yield
                k.act(gtmp[2], b[:, 0:128], AF.Copy, scale=1.0 / 128)
                k.rsqrt(gtmp[2], gtmp[2], RMS_EPS)
                yield
                k.tt('dve', gtmp[1], OT, gtmp[2], ALU.mult)
                k.stt('dve', oT[:, vh, :], gtmp[1], ng[:, 0:1], zT[:, h, :], ALU.mult, ALU.mult)
            yield

        def lane(vhs, B, bankf):
            for vh in vhs:
                yield from vh_gen(vh, B, bankf)

        if samp or cfg.get('nolanes'):
            lanes = [lane(range(8), BS[0], bank)]
        else:
            lanes = [lane([l, 4 + l], BS[l], lbank[l]) for l in range(4)]
        while lanes:
            for g in list(lanes):
                try:
                    next(g)
                except StopIteration:
                    lanes.remove(g)
        b1, b2 = bank(), bank()
        for half, bb in ((0, b1), (1, b2)):
            for kc in range(8):
                k.mm(bb[:, :], oT[:, kc, :], wo[:, kc, half * 512:(half + 1) * 512], start=(kc == 0), stop=(kc == 7))
        k.stt('dve', z1[:, 0:512], xt[:, 0:512], DN_ALPHA, b1[:, :], ALU.mult, ALU.add)
        k.stt('dve', z1[:, 512:1024], xt[:, 512:1024], DN_ALPHA, b2[:, :], ALU.mult, ALU.add)
        k.rownorm_stats(z1, mv, stats)
        k.rsqrt(rstd, mv[:, 1:2], LN_EPS)
        k.ts('dve', z1, z1, mv[:, 0:1], ALU.subtract, rstd[:, 0:1], ALU.mult)
        k.tt('dve', z1, z1, l1g, ALU.mult)
        k.tt('dve', z1, z1, l1b, ALU.add)
        k.dma('sp', d['x1s'][r0:r0 + 128, :], z1)
        if cfg.get('dbg'):
            k.dma('sp', d['dbg_oT'][i], oT)
    for v in range(8):
        k.dma('sp', d['o_pstate'][v], S[v])


def consts():
    c = {}
    c['identF'] = np.eye(128, dtype=np.float32)
    MS = np.zeros((2, 128, 128), np.float32); MI = np.zeros((2, 128, 128), np.float32)
    rmask = np.ones((2, 128, 512), np.float32)
    for i, blk in enumerate((64, 8)):
        s = np.arange(128)[:, None]; t = np.arange(128)[None, :]
        same = (s // blk) == (t // blk)
        MS[i] = (same & (s < t)); MI[i] = (same & (s <= t))
        tt = np.arange(512)
        rmask[i][:, (tt % blk) == 0] = 0.0
    c['MS'] = MS; c['MI'] = MI; c['MST'] = np.ascontiguousarray(MS.transpose(0, 2, 1)); c['rmask'] = rmask
    p = np.arange(128)
    c['Bones'] = ((p[:, None] // 64) == (p[None, :] // 64)).astype(np.float32)
    c['onesF'] = np.ones((128, 128), np.float32)
    c['segm'] = ((p[:, None] // 8) == np.arange(16)[None, :]).astype(np.float32)
    return c


def fm(v, nch):
    return np.ascontiguousarray(v.reshape(nch, 128).T)


def weights(inp):
    w = {}
    w['w_in'] = inp['w_in'][0].reshape(8, 128, 3848)
    w['w_o'] = inp['w_o'][0].reshape(8, 128, 1024)
    w['w_up'] = inp['w_up'][0].reshape(8, 128, 5632)
    w['w_down'] = inp['w_down'][0].reshape(22, 128, 1024)
    w['w_ple_gate'] = inp['w_ple_gate'][0].reshape(8, 128, 1024)
    w['w_ple'] = inp['w_ple'][0].reshape(2, 128, 1024)
    w['f_conv_w'] = np.ascontiguousarray(inp['f_conv_w'][0].reshape(3, 22, 128).transpose(2, 1, 0))
    w['f_conv_b'] = fm(inp['f_conv_b'][0], 22)
    for n in ('ln1_g', 'ln1_b', 'ln2_g', 'ln2_b', 'ple_g'):
        w[n] = np.ascontiguousarray(inp[n][0])
    w['a_mu'] = fm(inp['a_mu'][0], 14)
    pv = np.stack([inp[n][0].reshape(4, 128) for n in ('a_w0', 'a_a0', 'a_k_k', 'a_k_a', 'a_r_k', 'a_gn_g', 'a_gn_b')], 0)
    w['a_pv'] = np.ascontiguousarray(pv.transpose(2, 0, 1))
    w['a_wlora'] = np.ascontiguousarray(np.concatenate([inp['a_w_w2'][0], inp['a_w_a2'][0]], 0))
    w['a_w_g2'] = np.ascontiguousarray(inp['a_w_g2'][0])
    w['b_conv_w'] = np.ascontiguousarray(inp['b_conv_w'][0].reshape(4, 12, 128).transpose(2, 1, 0))
    w['b_a_log'] = np.ascontiguousarray(inp['b_a_log'][0]); w['b_dt_bias'] = np.ascontiguousarray(inp['b_dt_bias'][0])
    w['b_norm_g'] = np.ascontiguousarray(inp['b_norm_g'][0].reshape(128, 1))
    return w


def percore(inp, c):
    m = {}
    sl = slice(16 * c, 16 * c + 16)
    x = np.concatenate([inp['x_prompt'][c], inp['x_sample'][sl].reshape(128, 1024)], 0)
    m['x'] = np.ascontiguousarray(x)
    m['xT'] = np.ascontiguousarray(x.T.reshape(8, 128, TOK))
    p = np.concatenate([inp['p_prompt'][0, c], inp['p_sample'][0, sl].reshape(128, 256)], 0)
    m['pT'] = np.ascontiguousarray(p.T.reshape(2, 128, TOK))
    m['s_fconv'] = np.ascontiguousarray(inp['state_ffn_conv'][0, sl].reshape(16, 2, 22, 128).transpose(3, 2, 0, 1))
    m['s_ashift'] = np.ascontiguousarray(inp['state_a_shift'][0, sl].reshape(16, 14, 128).transpose(2, 1, 0))
    m['s_bconv'] = np.ascontiguousarray(inp['state_b_conv'][0, sl].reshape(16, 3, 12, 128).transpose(3, 2, 0, 1))
    st = np.zeros((8, 128, 16, 128), np.float32)
    wkv = inp['state_a_wkv'][0, sl]
    for hp in range(4):
        for h2 in range(2):
            blkk = wkv[:, hp * 2 + h2].transpose(2, 0, 1)
            st[hp, h2 * 64:(h2 + 1) * 64, :, h2 * 64:(h2 + 1) * 64] = blkk
    ssm = inp['state_b_ssm'][0, sl]
    for h in range(4):
        st[4 + h] = ssm[:, h].transpose(1, 0, 2)
    m['s_state'] = st
    return m


def unpack(r):
    o = {}
    o['y_p'] = r['y'][:2048]; o['y_s'] = r['y'][2048:].reshape(16, 8, 1024)
    ps = r['o_pstate']
    o['pa_wkv'] = np.stack([ps[h // 2][(h % 2) * 64:(h % 2 + 1) * 64, (h % 2) * 64:(h % 2 + 1) * 64].T for h in range(8)], 0)
    o['pb_ssm'] = ps[4:8]
    o['pa_shift'] = r['o_pashift'].reshape(128, 14).T.reshape(1792)
    o['pb_conv'] = r['o_pbconv'].transpose(2, 1, 0).reshape(3, 1536)
    o['pf_conv'] = r['o_pfconv'].transpose(2, 1, 0).reshape(2, 2816)
    ss = r['o_sstate']
    o['sa_wkv'] = np.stack([ss[h // 2][(h % 2) * 64:(h % 2 + 1) * 64, :, (h % 2) * 64:(h % 2 + 1) * 64].transpose(1, 2, 0) for h in range(8)], 1)
    o['sb_ssm'] = ss[4:8].transpose(2, 0, 1, 3)
    o['sa_shift'] = r['o_sashift'].transpose(2, 1, 0).reshape(16, 1792)
    o['sb_conv'] = r['o_sbconv'].transpose(2, 3, 1, 0).reshape(16, 3, 1536)
    o['sf_conv'] = r['o_sfconv'].transpose(2, 3, 1, 0).reshape(16, 2, 2816)
    return o


from concourse.bass_utils import run_bass_kernel_spmd

_CACHE = {}


def build():
    nc = bass.Bass("TRN2", target_bir_lowering=False)
    CS = consts()
    with contextlib.ExitStack() as st:
        P = Prog(nc)
        k = KB(nc, P, st)
        k.din('x', [TOK, 1024]); k.din('xT', [8, 128, TOK]); k.din('pT', [2, 128, TOK])
        k.din('w_in', [8, 128, 3848]); k.din('w_o', [8, 128, 1024])
        k.din('w_up', [8, 128, 5632]); k.din('w_down', [22, 128, 1024]); k.din('w_ple_gate', [8, 128, 1024]); k.din('w_ple', [2, 128, 1024])
        for n, v in CS.items():
            k.din(n, v.shape)
        k.din('a_mu', [128, 14]); k.din('a_pv', [128, 7, 4]); k.din('a_wlora', [128, 512]); k.din('a_w_g2', [128, 512])
        k.din('b_conv_w', [128, 12, 4]); k.din('b_a_log', [4]); k.din('b_dt_bias', [4]); k.din('b_norm_g', [128, 1])
        k.din('ln1_g', [1024]); k.din('ln1_b', [1024]); k.din('ln2_g', [1024]); k.din('ln2_b', [1024]); k.din('ple_g', [1024])
        k.din('f_conv_w', [128, 22, 3]); k.din('f_conv_b', [128, 22])
        k.din('s_ashift', [128, 14, 16]); k.din('s_bconv', [128, 12, 16, 3]); k.din('s_state', [8, 128, 16, 128]); k.din('s_fconv', [128, 22, 16, 2])
        k.dint('x1s', [TOK, 1024])
        k.dint('w_in_bf', [8, 128, 3848], BF16)
        k.dout('y', [TOK, 1024])
        k.dout('o_pstate', [8, 128, 128]); k.dout('o_sstate', [8, 128, 16, 128])
        k.dout('o_pashift', [128, 14, 1]); k.dout('o_sashift', [128, 14, 16]); k.dout('o_pbconv', [128, 12, 3]); k.dout('o_sbconv', [128, 12, 16, 3])
        k.dout('o_pfconv', [128, 22, 2]); k.dout('o_sfconv', [128, 22, 16, 2])
        cfg = {}
        k.init_arena()
        idf = k.sb('identF_sb', [128, 128]); k.dma('sp', idf, k.d['identF']); cfg['identF'] = idf
        mark = k.acur
        pss = [k.ps('ps' + n, [128, 1024]) for n in 'ABCD']
        cfg['banks'] = [p[:, h * 512:(h + 1) * 512] for p in pss for h in range(2)]
        for n, p in zip('ABCD', pss):
            cfg['ps' + n] = p
        phase_A(k, cfg)
        P.barrier()
        k.arena_reset(mark)
        phase_B(k, cfg)
        P.final_wait_all_dma('sp')
        P.emit()
    return nc, CS


def kernel(**inputs):
    inp = {kk: np.asarray(v) for kk, v in inputs.items()}
    if 'nc' not in _CACHE:
        _CACHE['nc'] = build()
    nc, CS = _CACHE['nc']
    W = weights(inp)
    in_maps = []
    for c in range(8):
        m = {}
        m.update(CS)
        m.update(W)
        m.update(percore(inp, c))
        in_maps.append(m)
    res = run_bass_kernel_spmd(nc, in_maps, core_ids=list(range(8)))
    us = [unpack(r) for r in res.results]
    f = np.float32
    out = (
        np.stack([u['y_p'] for u in us], 0).astype(f),
        np.concatenate([u['y_s'] for u in us], 0).astype(f),
        np.stack([u['pa_wkv'] for u in us], 0)[None].astype(f),
        np.stack([u['pa_shift'] for u in us], 0)[None].astype(f),
        np.stack([u['pb_ssm'] for u in us], 0)[None].astype(f),
        np.stack([u['pb_conv'] for u in us], 0)[None].astype(f),
        np.stack([u['pf_conv'] for u in us], 0)[None].astype(f),
        np.concatenate([u['sa_wkv'] for u in us], 0)[None].astype(f),
        np.concatenate([u['sa_shift'] for u in us], 0)[None].astype(f),
        np.concatenate([u['sb_ssm'] for u in us], 0)[None].astype(f),
        np.concatenate([u['sb_conv'] for u in us], 0)[None].astype(f),
        np.concatenate([u['sf_conv'] for u in us], 0)[None].astype(f),
    )
    return tuple(np.ascontiguousarray(o) for o in out)
```

```python
import numpy as np
import concourse.bass as bass
import concourse.mybir as mybir

F32 = mybir.dt.float32
BF16 = mybir.dt.bfloat16
AF = mybir.ActivationFunctionType
ALU = mybir.AluOpType
AX = mybir.AxisListType

ENGS = ['pe', 'act', 'dve', 'pool', 'sp']
NDMASEM = 40


ARENA_ALLOCS = []


def _box(ap):
    t = ap.tensor
    name = t.name
    pairs = [tuple(x) for x in ap.ap]
    space = str(ap.space)
    sz = mybir.dt.size(ap.dtype)
    if 'DRAM' in space.upper() or 'HBM' in space.upper():
        lo = ap.offset
        hi = lo
        for st, n in pairs:
            hi += abs(st) * (n - 1)
        return (name, 0, 1, lo * sz, (hi + 1) * sz)
    shp = list(t.shape)
    pstride = 1
    for s_ in shp[1:]:
        pstride *= s_
    p0 = ap.offset // pstride
    f0 = ap.offset % pstride
    np_ = pairs[0][1]
    ext = 0
    for st, n in pairs[1:]:
        ext += abs(st) * (n - 1)
    b0 = f0 * sz
    b1 = (f0 + ext + 1) * sz
    if 'PSUM' in space.upper() or type(t).__name__.startswith('PSum'):
        b0 = (b0 // 2048) * 2048
        b1 = ((b1 + 2047) // 2048) * 2048
    if name == 'arena':
        for (lo, hi, key) in ARENA_ALLOCS:
            if lo <= b0 < hi:
                if b1 > hi:
                    raise RuntimeError('access spans allocations: %s' % key)
                return (key, p0, p0 + np_, b0, b1)
        raise RuntimeError('arena access outside allocations')
    return (name, p0, p0 + np_, b0, b1)


def _ovl(a, b):
    return a[1] < b[2] and b[1] < a[2] and a[3] < b[4] and b[3] < a[4]


def _covers(a, b):
    return a[1] <= b[1] and a[2] >= b[2] and a[3] <= b[3] and a[4] >= b[4]


class _Op:
    __slots__ = ('eng', 'idx', 'sem', 'val', 'clock')


class Prog:
    def __init__(self, nc, same_engine_dist=2):
        self.nc = nc
        self.streams = {e: [] for e in ENGS}
        self.nops = {e: 0 for e in ENGS}
        self.know = {e: {} for e in ENGS}
        self.acc = {}
        self.dma_last = [None] * NDMASEM
        self.dma_cnt = [0] * NDMASEM
        self.dma_rr = 0
        self.sed = same_engine_dist
        self.nwaits = 0
        self.last_op = {}
        self.direct = {e: {} for e in ENGS}

    def _deps(self, reads, writes, eng=None):
        deps = []
        self._raw = set()
        for ap in reads:
            b = _box(ap)
            isps = b[0].startswith('ps')
            for (bb, op, w) in self.acc.get(b[0], ()):
                if (w or (isps and op.eng != eng)) and _ovl(b, bb):
                    deps.append(op)
                    self._raw.add(id(op))
        for ap in writes:
            b = _box(ap)
            for (bb, op, w) in self.acc.get(b[0], ()):
                if _ovl(b, bb):
                    deps.append(op)
        return deps

    def _record(self, op, reads, writes):
        for ap in writes:
            b = _box(ap)
            lst = self.acc.setdefault(b[0], [])
            lst[:] = [x for x in lst if not _covers(b, x[0])]
            lst.append((b, op, True))
        for ap in reads:
            b = _box(ap)
            lst = self.acc.setdefault(b[0], [])
            lst[:] = [x for x in lst if not ((not x[2]) and x[1].eng == op.eng and x[1].sem == op.sem and _covers(b, x[0]))]
            lst.append((b, op, False))
            if len(lst) > 400:
                raise RuntimeError('access list too long for ' + b[0])

    def _wait_for(self, eng, deps, my_idx, is_dma=False):
        kn = self.know[eng]
        for d in sorted(deps, key=lambda o: -o.val):
            if d.eng == eng and d.sem == ('E', eng):
                if eng == 'pe':
                    continue
                pass
            if d.sem[0] == 'D':
                if self.direct[eng].get(d.sem, 0) >= d.val:
                    continue
                self.direct[eng][d.sem] = d.val
            elif kn.get(d.sem, 0) >= d.val:
                continue
            self.streams[eng].append(('wait', d.sem, d.val))
            self.nwaits += 1
            for k, v in d.clock.items():
                if kn.get(k, 0) < v:
                    kn[k] = v

    def op(self, eng, fn, reads=(), writes=()):
        deps = self._deps(reads, writes, eng)
        idx = self.nops[eng]
        self._wait_for(eng, deps, idx)
        o = _Op()
        o.eng = eng
        o.idx = idx
        o.sem = ('E', eng)
        o.val = idx + 1
        self.nops[eng] = idx + 1
        o.clock = dict(self.know[eng])
        o.clock[o.sem] = o.val
        self.streams[eng].append(('op', fn, o.sem, 1))
        self.last_op[eng] = o
        self._record(o, reads, writes)
        return o

    def dma(self, eng, out, in_, **kw):
        reads = [in_]
        writes = [out]
        deps = self._deps(reads, writes)
        kn = self.know[eng]
        pick = None
        half = NDMASEM // 2
        base = half if eng == 'pool' else 0
        if not hasattr(self, 'dma_rrs'):
            self.dma_rrs = {0: 0, half: 0}
        rr = self.dma_rrs[base]
        for k in range(half):
            j = base + (rr + k) % half
            last = self.dma_last[j]
            if last is None or kn.get(last.sem, 0) >= last.val:
                pick = j
                break
        if pick is None:
            pick = base + rr % half
            deps = deps + [self.dma_last[pick]]
        self.dma_rrs[base] = (pick - base + 1) % half
        self._wait_for(eng, deps, self.nops[eng], True)
        o = _Op()
        o.eng = eng
        o.idx = -1
        o.sem = ('D', pick)
        self.dma_cnt[pick] += 16
        o.val = self.dma_cnt[pick]
        o.clock = dict(kn)
        o.clock[o.sem] = o.val
        self.dma_last[pick] = o
        self.streams[eng].append(('dma', (out, in_, kw), o.sem, 16))
        self._record(o, reads, writes)
        return o

    def barrier(self):
        deps = [o for o in self.last_op.values()] + [o for o in self.dma_last if o is not None]
        for e in ENGS:
            kn = self.know[e]
            for d in sorted(deps, key=lambda o: -o.val):
                if d.sem == ('E', e):
                    continue
                if kn.get(d.sem, 0) >= d.val:
                    continue
                self.streams[e].append(('wait', d.sem, d.val))
                for k_, v in d.clock.items():
                    if kn.get(k_, 0) < v:
                        kn[k_] = v
        self.acc = {}

    def final_wait_all_dma(self, eng='sp'):
        for j in range(NDMASEM):
            last = self.dma_last[j]
            if last is not None and self.know[eng].get(last.sem, 0) < last.val:
                self.streams[eng].append(('wait', last.sem, last.val))
                self.know[eng][last.sem] = last.val

    def emit(self):
        nc = self.nc
        import contextlib
        with contextlib.ExitStack() as st:
            sems = {}
            for e in ENGS:
                sems[('E', e)] = st.enter_context(nc.semaphore('s_' + e))
            for j in range(NDMASEM):
                if self.dma_last[j] is not None:
                    sems[('D', j)] = st.enter_context(nc.semaphore('d_%d' % j))
            block = st.enter_context(nc.Block())

            def run(engname):
                def body(eng):
                    for item in self.streams[engname]:
                        if item[0] == 'wait':
                            eng.wait_ge(sems[item[1]], item[2])
                        elif item[0] == 'op':
                            item[1](eng).then_inc(sems[item[2]], item[3])
                        else:
                            out, in_, kw = item[1]
                            eng.dma_start(out=out, in_=in_, **kw).then_inc(sems[item[2]], item[3])
                return body

            block.tensor(run('pe'))
            block.scalar(run('act'))
            block.vector(run('dve'))
            block.gpsimd(run('pool'))
            block.sync(run('sp'))


import contextlib

DN_ALPHA = 2.0 ** 0.25
LN_EPS = 1e-5
GN_EPS = 64e-5
RMS_EPS = 1e-6
L2_EPS = 1e-12
NT = 17
TOK = 2176


class KB:
    def __init__(self, nc, P, st):
        self.nc, self.P, self.st = nc, P, st
        self.d = {}

    def din(self, name, shape, dt=F32):
        self.d[name] = self.nc.dram_tensor(name, list(shape), dt, kind='ExternalInput').ap()
        return self.d[name]

    def dout(self, name, shape, dt=F32):
        self.d[name] = self.nc.dram_tensor(name, list(shape), dt, kind='ExternalOutput').ap()
        return self.d[name]

    def dint(self, name, shape, dt=F32):
        self.d[name] = self.nc.dram_tensor(name, list(shape), dt, kind='Internal').ap()
        return self.d[name]

    def init_arena(self, nfp32=53200):
        self.arena = self.st.enter_context(self.nc.sbuf_tensor('arena', [128, nfp32], F32))
        self.acap = nfp32
        self.acur = 0
        ARENA_ALLOCS.clear()

    def arena_reset(self, cur):
        ARENA_ALLOCS[:] = [a for a in ARENA_ALLOCS if a[1] <= cur * 4]
        self.acur = cur

    def sb(self, name, shape, dt=F32):
        shape = list(shape)
        n = 1
        for s_ in shape[1:]:
            n *= s_
        nb = n * mybir.dt.size(dt)
        nf = (nb + 3) // 4
        if self.acur + nf > self.acap:
            raise RuntimeError('arena overflow at %s: need %d have %d' % (name, nf, self.acap - self.acur))
        v = self.arena[:, self.acur:self.acur + nf]
        ARENA_ALLOCS.append((self.acur * 4, (self.acur + nf) * 4, name))
        self.acur += nf
        if dt != F32:
            v = v.bitcast(dt)
            v = v[:, 0:n]
        if len(shape) == 3:
            v = v.rearrange('p (a b) -> p a b', a=shape[1], b=shape[2])
        elif len(shape) == 4:
            v = v.rearrange('p (a b c) -> p a b c', a=shape[1], b=shape[2], c=shape[3])
        return v

    def ps(self, name, shape, dt=F32):
        return self.st.enter_context(self.nc.psum_tensor(name, list(shape), dt))

    def mm(self, out, lhsT, rhs, start=True, stop=True):
        self.P.op('pe', lambda e: e.matmul(out, lhsT, rhs, start=start, stop=stop), reads=[lhsT, rhs], writes=[out])

    def tt(self, eng, out, a, b, op):
        self.P.op(eng, lambda e: e.tensor_tensor(out=out, in0=a, in1=b, op=op), reads=[a, b], writes=[out])

    def ts(self, eng, out, a, s1, op0, s2=None, op1=None):
        rd = [a] + [s for s in (s1, s2) if not isinstance(s, (int, float, type(None)))]
        if op1 is None:
            self.P.op(eng, lambda e: e.tensor_scalar(out=out, in0=a, scalar1=s1, scalar2=None, op0=op0), reads=rd, writes=[out])
        else:
            self.P.op(eng, lambda e: e.tensor_scalar(out=out, in0=a, scalar1=s1, scalar2=s2, op0=op0, op1=op1), reads=rd, writes=[out])

    def stt(self, eng, out, in0, scalar, in1, op0, op1):
        eng = 'dve'
        rd = [in0, in1] + ([] if isinstance(scalar, (int, float)) else [scalar])
        self.P.op(eng, lambda e: e.scalar_tensor_tensor(out=out, in0=in0, scalar=scalar, in1=in1, op0=op0, op1=op1), reads=rd, writes=[out])

    def act(self, out, in_, func, bias=0.0, scale=1.0):
        rd = [in_] + [s for s in (bias, scale) if not isinstance(s, (int, float))]
        self.P.op('act', lambda e: e.activation(out=out, in_=in_, func=func, bias=bias, scale=scale), reads=rd, writes=[out])

    def cp(self, eng, out, in_):
        if eng == 'dve':
            eng = 'act'
        if eng == 'act':
            self.P.op('act', lambda e: e.copy(out=out, in_=in_), reads=[in_], writes=[out])
        else:
            self.P.op(eng, lambda e: e.tensor_copy(out=out, in_=in_), reads=[in_], writes=[out])

    def memset(self, eng, ap, v):
        self.P.op(eng, lambda e: e.memset(ap, v), reads=[], writes=[ap])

    def dma(self, eng, out, in_, **kw):
        self.P.dma(eng, out, in_, **kw)

    def rsqrt(self, out, in_, eps):
        self.act(out, in_, AF.Ln, bias=eps)
        self.act(out, out, AF.Exp, scale=-0.5)

    def rownorm_stats(self, src, mv, stats):
        for c in range(2):
            o = stats[:, c, :]
            i = src[:, c * 512:(c + 1) * 512]
            self.P.op('dve', lambda e, o=o, i=i: e.bn_stats(out=o, in_=i), reads=[i], writes=[o])
        sv = stats[:, :, :]
        self.P.op('dve', lambda e: e.bn_aggr(out=mv, in_=sv), reads=[sv], writes=[mv])


def phase_B(k, cfg):
    d = k.d
    wup = k.sb('wup', [128, 8, 5632], BF16)
    wdn = k.sb('wdn', [128, 22, 1024], BF16)
    wpg = k.sb('wpg', [128, 8, 1024], BF16)
    wpl = k.sb('wpl', [128, 2, 1024], BF16)
    for kc in range(8):
        k.dma('pool', wup[:, kc, :], d['w_up'][kc])
    for kc in range(22):
        k.dma('pool', wdn[:, kc, :], d['w_down'][kc])
    for kc in range(8):
        k.dma('pool', wpg[:, kc, :], d['w_ple_gate'][kc])
    for kc in range(2):
        k.dma('pool', wpl[:, kc, :], d['w_ple'][kc])
    cw = k.sb('f_cw', [128, 22, 3])
    cb = k.sb('f_cb', [128, 22])
    k.dma('sp', cw[:], d['f_conv_w'])
    k.dma('sp', cb[:], d['f_conv_b'])
    l2g = k.sb('l2g', [128, 1024]); l2b = k.sb('l2b', [128, 1024]); plg = k.sb('plg', [128, 1024])
    k.dma('sp', l2g[:], d['ln2_g'].partition_broadcast(128))
    k.dma('sp', l2b[:], d['ln2_b'].partition_broadcast(128))
    k.dma('sp', plg[:], d['ple_g'].partition_broadcast(128))
    identF = cfg['identF']
    prev2 = k.sb('prev2', [128, 22, 2])
    k.memset('pool', prev2[:], 0.0)
    sfc_in = k.sb('sfc_in', [128, 22, 16, 2])
    k.dma('sp', sfc_in[:], d['s_fconv'])
    sfc_out = sfc_in
    x1ts = [k.sb('b_x1t%d' % j, [128, 1024]) for j in range(2)]
    x1Ts = [k.sb('b_x1T%d' % j, [128, 8, 128], BF16) for j in range(2)]
    pTts = [k.sb('b_pT%d' % j, [128, 2, 128], BF16) for j in range(2)]
    gb = [k.sb('b_gb%d' % i, [128, 4, 130]) for i in range(2)]
    gs1 = k.sb('b_gs1', [128, 4, 128]); gs2 = k.sb('b_gs2', [128, 4, 128])
    tmpc = [k.sb('b_tmpc%d' % i, [128, 4, 128]) for i in range(2)]
    hhT = k.sb('b_hhT', [128, 22, 128], BF16)
    z2 = k.sb('b_z2', [128, 1024])
    ebuf = k.sb('b_e', [128, 1024])
    stats = k.sb('b_stats', [128, 2, 6]); mv = k.sb('b_mv', [128, 2]); rstd = k.sb('b_rstd', [128, 1])
    stats2 = k.sb('b_stats2', [128, 2, 6]); mv2 = k.sb('b_mv2', [128, 2]); rstd2 = k.sb('b_rstd2', [128, 1])
    psA, psB, psC, psD = cfg['psA'], cfg['psB'], cfg['psC'], cfg['psD']

    tl = list(cfg.get('tiles', range(NT)))

    def head(n):
        r = tl[n] * 128
        xt_, xT_ = x1ts[n % 2], x1Ts[n % 2]
        k.dma('sp', xt_[:], d['x1s'][r:r + 128, :])
        for kc in range(8):
            k.mm(psA[:, kc * 128:(kc + 1) * 128], xt_[:, kc * 128:(kc + 1) * 128], identF[:])
        k.cp('act', xT_[:].rearrange('p a b -> p (a b)'), psA[:, :])

    for i in tl:
        samp = (i == NT - 1)
        r0 = i * 128
        n_ = tl.index(i)
        x1t = x1ts[n_ % 2]
        x1T = x1Ts[n_ % 2]
        x2T = x1T
        pTt = pTts[n_ % 2]
        sg = x1t
        if n_ == 0:
            head(0)
        for kc in range(2):
            k.dma('pool', pTt[:, kc, :], d['pT'][kc, :, r0:r0 + 128])
        for grp in range(6):
            ng = min(4, 22 - 4 * grp)
            bank = grp % 2
            pg = psB[:, bank * 512: bank * 512 + ng * 128]
            pu = psC[:, bank * 512: bank * 512 + ng * 128]
            for j in range(ng):
                f = 4 * grp + j
                for kc in range(8):
                    k.mm(psB[:, bank * 512 + j * 128: bank * 512 + (j + 1) * 128], wup[:, kc, f * 128:(f + 1) * 128], x1T[:, kc, :], start=(kc == 0), stop=(kc == 7))
            for j in range(ng):
                f = 22 + 4 * grp + j
                for kc in range(8):
                    k.mm(psC[:, bank * 512 + j * 128: bank * 512 + (j + 1) * 128], wup[:, kc, f * 128:(f + 1) * 128], x1T[:, kc, :], start=(kc == 0), stop=(kc == 7))
            g = gb[grp % 2]
            tc_ = tmpc[grp % 2]
            k.cp('act', g[:, 0:ng, 2:130], pg.rearrange('p (a b) -> p a b', b=128))
            if not samp:
                k.cp('pool', g[:, 0:ng, 0:2], prev2[:, 4 * grp:4 * grp + ng, :])
            else:
                g4 = g[:, 0:ng, 2:130].rearrange('p a (s t) -> p a s t', t=8)
                s1 = gs1[:, 0:ng, :].rearrange('p a (s t) -> p a s t', t=8)
                s2 = gs2[:, 0:ng, :].rearrange('p a (s t) -> p a s t', t=8)
                for j in range(ng):
                    f = 4 * grp + j
                    k.cp('pool', s1[:, j, :, 1:8], g4[:, j, :, 0:7])
                    k.cp('pool', s1[:, j, :, 0:1], sfc_in[:, f, :, 1:2])
                    k.cp('pool', s2[:, j, :, 2:8], g4[:, j, :, 0:6])
                    k.cp('pool', s2[:, j, :, 0:2], sfc_in[:, f, :, 0:2])
                    k.cp('pool', sfc_out[:, f, :, :], g4[:, j, :, 6:8])
            for j in range(ng):
                f = 4 * grp + j
                if not samp:
                    G0, G1, G2 = g[:, j, 2:130], g[:, j, 1:129], g[:, j, 0:128]
                else:
                    G0, G1, G2 = g[:, j, 2:130], gs1[:, j, :], gs2[:, j, :]
                t = tc_[:, j, :]
                k.ts('dve', t, G2, cw[:, f, 0:1], ALU.mult)
                k.stt('dve', t, G1, cw[:, f, 1:2], t, ALU.mult, ALU.add)
                k.stt('dve', t, G0, cw[:, f, 2:3], t, ALU.mult, ALU.add)
                k.act(t, t, AF.Silu, bias=cb[:, f:f + 1])
                k.tt('dve', hhT[:, f, :], t, psC[:, bank * 512 + j * 128: bank * 512 + (j + 1) * 128], ALU.mult)
            if not samp:
                k.cp('pool', prev2[:, 4 * grp:4 * grp + ng, :], g[:, 0:ng, 128:130])
        if i == NT - 2:
            k.dma('sp', d['o_pfconv'], prev2[:])
        if samp:
            k.dma('sp', d['o_sfconv'], sfc_out[:])
        if n_ + 1 < len(tl):
            head(n_ + 1)
        for half in range(2):
            for f in range(22):
                k.mm(psD[:, half * 512:(half + 1) * 512], hhT[:, f, :], wdn[:, f, half * 512:(half + 1) * 512], start=(f == 0), stop=(f == 21))
        k.stt('dve', z2[:], x1t[:], DN_ALPHA, psD[:, :], ALU.mult, ALU.add)
        k.rownorm_stats(z2, mv[:], stats)
        k.rsqrt(rstd[:], mv[:, 1:2], LN_EPS)
        k.ts('dve', z2[:], z2[:], mv[:, 0:1], ALU.subtract, rstd[:, 0:1], ALU.mult)
        k.tt('dve', z2[:], z2[:], l2g[:], ALU.mult)
        k.tt('dve', z2[:], z2[:], l2b[:], ALU.add)
        for kc in range(8):
            k.mm(psA[:, kc * 128:(kc + 1) * 128], z2[:, kc * 128:(kc + 1) * 128], identF[:])
        k.cp('act', x2T[:].rearrange('p a b -> p (a b)'), psA[:, :])
        for half in range(2):
            for kc in range(8):
                k.mm(psB[:, half * 512:(half + 1) * 512], x2T[:, kc, :], wpg[:, kc, half * 512:(half + 1) * 512], start=(kc == 0), stop=(kc == 7))
        for half in range(2):
            for kc in range(2):
                k.mm(psC[:, half * 512:(half + 1) * 512], pTt[:, kc, :], wpl[:, kc, half * 512:(half + 1) * 512], start=(kc == 0), stop=(kc == 1))
        k.rownorm_stats(psC, mv2[:], stats2)
        k.stt('dve', rstd2[:], mv2[:, 0:1], mv2[:, 0:1], mv2[:, 1:2], ALU.mult, ALU.add)
        k.rsqrt(rstd2[:], rstd2[:], RMS_EPS)
        k.stt('dve', ebuf[:], psC[:, :], rstd2[:, 0:1], plg[:], ALU.mult, ALU.mult)
        k.act(sg[:], psB[:, :], AF.Sigmoid)
        k.tt('dve', ebuf[:], ebuf[:], sg[:], ALU.mult)
        k.tt('dve', ebuf[:], ebuf[:], z2[:], ALU.add)
        k.dma('sp', d['y'][r0:r0 + 128, :], ebuf[:])


class _Rec:
    def __init__(self):
        self.calls = []

    def op(self, *a, **kw):
        self.calls.append(('op', a, kw))

    def dma(self, *a, **kw):
        self.calls.append(('dma', a, kw))


def record_calls(k, fn):
    real = k.P
    rec = _Rec()
    k.P = rec
    try:
        fn()
    finally:
        k.P = real
    return rec.calls


def replay_interleaved(k, lists):
    idx = [0] * len(lists)
    while any(idx[j] < len(lists[j]) for j in range(len(lists))):
        for j, l in enumerate(lists):
            if idx[j] < len(l):
                kind, a, kw = l[idx[j]]
                idx[j] += 1
                getattr(k.P, kind)(*a, **kw)


def phase_A(k, cfg):
    fl = lambda a: a.rearrange('p a b -> p (a b)')
    d = k.d
    P = k.P
    identF = cfg['identF']
    banks = cfg['banks']
    bstate = [0]

    def bank():
        b = banks[bstate[0] % 8]
        bstate[0] += 1
        return b

    mark0 = k.acur
    wstage = k.sb('wstage', [128, 8, 3848], BF16)
    for kc in range(8):
        k.dma('pool', wstage[:, kc, :], d['w_in'][kc])
    for kc in range(8):
        k.dma('sp', d['w_in_bf'][kc], wstage[:, kc, :])
    P.barrier()
    k.arena_reset(mark0)
    winbf = d['w_in_bf'].rearrange('k p f -> p k f')
    wbuf = [k.sb('wbuf%d' % j, [128, 8, 512], BF16) for j in range(2)]
    wo = k.sb('wo', [128, 8, 1024], BF16)
    for kc in range(8):
        k.dma('pool', wo[:, kc, :], d['w_o'][kc])
    identB = k.sb('identB', [128, 128], BF16); k.dma('pool', identB, d['identF'])
    MS = [k.sb('MS%d' % i, [128, 128]) for i in range(2)]
    MI = [k.sb('MI%d' % i, [128, 128]) for i in range(2)]
    MST = [k.sb('MST%d' % i, [128, 128]) for i in range(2)]
    rmask = [k.sb('rmask%d' % i, [128, 512]) for i in range(2)]
    for i in range(2):
        k.dma('sp', MS[i], d['MS'][i]); k.dma('sp', MI[i], d['MI'][i]); k.dma('sp', MST[i], d['MST'][i]); k.dma('sp', rmask[i], d['rmask'][i])
    Bones = k.sb('Bones', [128, 128]); k.dma('sp', Bones, d['Bones'])
    onesF = k.sb('onesF', [128, 128]); k.dma('sp', onesF, d['onesF'])
    segm = k.sb('segm', [128, 16]); k.dma('sp', segm, d['segm'])
    mu = k.sb('a_mu', [128, 14]); k.dma('sp', mu, d['a_mu'])
    pv = k.sb('a_pv', [128, 7, 4]); k.dma('sp', pv, d['a_pv'])
    nw0 = k.sb('a_nw0', [128, 4]); k.ts('dve', nw0, pv[:, 0, :], -1.0, ALU.mult)
    omka = k.sb('a_omka', [128, 4]); k.ts('dve', omka, pv[:, 3, :], -1.0, ALU.mult, 1.0, ALU.add)
    wlora = k.sb('wlora', [128, 512], BF16); k.dma('pool', wlora, d['a_wlora'])
    wg2 = k.sb('wg2', [128, 512], BF16); k.dma('pool', wg2, d['a_w_g2'])
    cwb = k.sb('b_cw', [128, 12, 4]); k.dma('sp', cwb, d['b_conv_w'])
    alog = k.sb('b_alog', [128, 4]); k.dma('sp', alog, d['b_a_log'].partition_broadcast(128))
    dtb = k.sb('b_dtb', [128, 4]); k.dma('sp', dtb, d['b_dt_bias'].partition_broadcast(128))
    negA = k.sb('b_negA', [128, 4]); k.act(negA, alog, AF.Exp); k.ts('dve', negA, negA, -1.0, ALU.mult)
    ng = k.sb('b_ng', [128, 1]); k.dma('sp', ng, d['b_norm_g'])
    l1g = k.sb('l1g', [128, 1024]); l1b = k.sb('l1b', [128, 1024])
    k.dma('sp', l1g, d['ln1_g'].partition_broadcast(128)); k.dma('sp', l1b, d['ln1_b'].partition_broadcast(128))
    onecol = onesF[:, 0:1]
    ash_in = k.sb('ash_in', [128, 14, 16]); k.dma('sp', ash_in, d['s_ashift'])
    bcv_in = k.sb('bcv_in', [128, 12, 16, 3]); k.dma('sp', bcv_in, d['s_bconv'])
    S = [k.sb('S%d' % v, [128, 128]) for v in range(8)]
    for v in range(8):
        k.memset('pool', S[v], 0.0)
    xT = k.sb('xT', [128, 8, 128], BF16)
    xt = k.sb('xt', [128, 1024])
    uT = k.sb('uT', [128, 14, 129]); k.memset('pool', uT[:, :, 128:129], 0.0)
    xs = k.sb('xs', [128, 14, 128])
    qkvp = k.sb('qkvp', [128, 12, 131]); k.memset('pool', qkvp[:, :, 128:131], 0.0)
    big = k.sb('big', [128, 4608])
    gsh = [big[:, j * 1536:(j + 1) * 1536].rearrange('p (a b) -> p a b', a=12, b=128) for j in range(3)]
    prevS = big[:, 0:1792].rearrange('p (a b) -> p a b', a=14, b=128)
    KdM = big[:, 0:1024].bitcast(BF16).rearrange('p (a b) -> p a b', a=16, b=128)
    KtM = big[:, 1024:2048].bitcast(BF16).rearrange('p (a b) -> p a b', a=16, b=128)
    Sin = big[:, 2048:4096].rearrange('p (a b) -> p a b', a=16, b=128)
    Sout = Sin
    rwtmp = k.sb('rwtmp', [128, 3584])
    cv = k.sb('cv', [128, 12, 128])
    zT = k.sb('zT', [128, 4, 128])
    batok = k.sb('batok', [128, 8])
    F4 = lambda n: k.sb(n, [128, 4, 128])
    B4 = lambda n: k.sb(n, [128, 4, 128], BF16)
    t_lw, t_L, t_a, t_kk, t_kp, t_b, t_tmp = [rwtmp[:, j * 512:(j + 1) * 512].rearrange('p (a b) -> p a b', a=4, b=128) for j in range(7)]
    t_eL, t_g, t_bon = [F4('t_%d' % j) for j in range(3)]
    rt, kkt, bt, kt, vb = [B4('r_%d' % j) for j in range(5)]
    lorain = k.sb('lorain', [128, 128], BF16); sgd = k.sb('sgd', [128, 128], BF16)
    KKtok, Btok, Ktok, Vtok = [k.sb('tok%d' % j, [128, 512], BF16) for j in range(4)]
    tmp8 = k.sb('tmp8', [128, 8, 128])
    qh = F4('qh'); qhb = B4('qhb'); khb = B4('khb'); vgb = B4('vgb')
    Win = B4('Win'); Yin = B4('Yin'); Kdg = B4('Kdg')
    beta = k.sb('beta', [128, 4]); gtk = k.sb('gtk', [128, 4]); Gtok = k.sb('Gtok', [128, 4]); eGtok = k.sb('eGtok', [128, 4])
    eGs = k.sb('eGs', [128, 4]); beG = k.sb('beG', [128, 4])
    eGbc = [k.sb('eGbc%d' % h, [128, 128]) for h in range(4)]
    QTg = F4('QTg')

    def mkset(alloc_f, alloc_b):
        B = {}
        B['gd'] = [alloc_f() for _ in range(5)]
        B['inv'] = [alloc_f() for _ in range(8)]
        B['TTb'] = [alloc_b() for _ in range(2)]
        B['ATb'] = [alloc_b() for _ in range(2)]
        B['A2Tb'] = [alloc_b() for _ in range(2)]
        B['AakT'] = alloc_b()
        B['XYin'] = alloc_b(256)
        B['nwv'] = alloc_b(); B['uv'] = alloc_b()
        for n in ('ZT', 'OcT', 'OT', 'PhiT', 'Psi', 'Idg'):
            B[n] = alloc_f()
        B['gtmp'] = [alloc_f() for _ in range(3)]
        return B
    cnt = [0]

    def a0f():
        cnt[0] += 1
        return k.sb('s0f%d' % cnt[0], [128, 128])

    def a0b(n=128):
        cnt[0] += 1
        return k.sb('s0b%d' % cnt[0], [128, n], BF16)
    bigcur = [0]

    def a1f():
        v = big[:, bigcur[0]:bigcur[0] + 128]
        bigcur[0] += 128
        return v

    def a1b(n=128):
        v = big[:, bigcur[0]:bigcur[0] + n // 2].bitcast(BF16)
        bigcur[0] += n // 2
        return v
    BS = [mkset(a0f, a0b), mkset(a1f, a1b), mkset(a0f, a0b), mkset(a0f, a0b)]
    assert bigcur[0] <= 4608, bigcur[0]
    oT = k.sb('oT', [128, 8, 128], BF16)
    z1 = xt
    stats = k.sb('a_stats', [128, 2, 6]); mv = k.sb('a_mv', [128, 2]); rstd = k.sb('a_rstd', [128, 1])
    ashs = k.sb('ashs', [128, 14, 16]); bcvs = k.sb('bcvs', [128, 12, 16, 3])
    pash = k.sb('pash', [128, 14, 1])

    bst = [0, 0, 0, 0]

    def mkbank(l):
        def f():
            b = banks[2 * l + bst[l] % 2]
            bst[l] += 1
            return b
        return f
    lbank = [mkbank(l) for l in range(4)]
    pst = [0, 0]

    def mkpbank(l):
        def f():
            b = banks[4 * l + pst[l] % 4]
            pst[l] += 1
            return b
        return f
    pbank = [mkpbank(0), mkpbank(1)]

    tlA = list(cfg.get('tiles', range(NT)))
    for i in tlA:
        samp = (i == NT - 1)
        mi = 1 if samp else 0
        blk = 8 if samp else 64
        nseg = 128 // blk
        nlev = 2 if samp else 5
        if cfg.get('nlev') is not None:
            nlev = cfg['nlev']
        r0 = i * 128
        if cfg.get('tilebar'):
            P.barrier()
        nA = tlA.index(i)
        for kc in range(8):
            k.dma('pool', xT[:, kc, :], d['xT'][kc, :, r0:r0 + 128])
        k.dma('sp', xt, d['x'][r0:r0 + 128, :])
        if not samp:
            k.cp('pool', uT[:, :, 0:1], uT[:, :, 128:129])
            k.cp('pool', qkvp[:, :, 0:3], qkvp[:, :, 128:131])
        for grp in range(cfg.get('pgrp', 8)):
            nchunk = 4 if grp < 7 else 2
            b = bank()
            wb = wbuf[grp % 2]
            ncols = 512 if grp < 7 else 264
            if not (grp < 2 and nA > 0):
                k.dma('sp', wb[:, :, 0:ncols], winbf[:, :, grp * 512:grp * 512 + ncols])
            for j in range(nchunk):
                c = grp * 4 + j
                for kc in range(8):
                    k.mm(b[:, j * 128:(j + 1) * 128], wb[:, kc, j * 128:(j + 1) * 128], xT[:, kc, :], start=(kc == 0), stop=(kc == 7))
            for j in range(nchunk):
                c = grp * 4 + j
                src = b[:, j * 128:(j + 1) * 128]
                if c < 14:
                    k.cp('act', uT[:, c, 1:129], src)
                elif c < 26:
                    k.cp('act', qkvp[:, c - 14, 3:131], src)
                else:
                    k.act(zT[:, c - 26, :], src, AF.Silu)
        b = bank()
        for kc in range(8):
            k.mm(b[:, 0:8], xT[:, kc, :], wbuf[1][:, kc, 256:264], start=(kc == 0), stop=(kc == 7))
        k.cp('dve', batok, b[:, 0:8])
        if nA + 1 < len(tlA):
            for g_ in range(2):
                k.dma('sp', wbuf[g_][:, :, 0:512], winbf[:, :, g_ * 512:g_ * 512 + 512])
        def pre_rwkv(bank):
            cur = uT[:, :, 1:129]
            if not samp:
                prev = uT[:, :, 0:128]
            else:
                p4 = prevS.rearrange('p a (s t) -> p a s t', t=8)
                c4 = cur.rearrange('p a (s t) -> p a s t', t=8)
                for c in range(14):
                    k.cp('pool', p4[:, c, :, 1:8], c4[:, c, :, 0:7])
                    k.cp('pool', p4[:, c, :, 0:1], ash_in[:, c, :].rearrange('p (s o) -> p s o', o=1))
                    k.cp('pool', ashs[:, c, :].rearrange('p (s o) -> p s o', o=1), c4[:, c, :, 7:8])
                prev = prevS
                k.dma('sp', d['o_sashift'], ashs)
            if i == NT - 2:
                k.cp('pool', pash, uT[:, :, 128:129])
                k.dma('sp', d['o_pashift'], pash)
            for c in range(14):
                k.tt('pool', xs[:, c, :], prev[:, c, :], cur[:, c, :], ALU.subtract)
                k.stt('dve' if c % 2 else 'pool', xs[:, c, :], xs[:, c, :], mu[:, c:c + 1], cur[:, c, :], ALU.mult, ALU.add)
            r_, kx, v_ = xs[:, 0:4, :], xs[:, 4:8, :], xs[:, 8:12, :]
            k.act(lorain[0:64, :], xs[0:64, 12, :], AF.Tanh)
            k.cp('dve', lorain[64:128, :], xs[64:128, 12, :])
            k.act(sgd, xs[:, 13, :], AF.Sigmoid)
            bw, ba_, bg = bank(), bank(), bank()
            for c in range(4):
                cs = slice(c * 128, (c + 1) * 128)
                k.mm(bw[:, cs], wlora[0:64, cs], lorain[0:64, :])
                k.mm(ba_[:, cs], wlora[64:128, cs], lorain[64:128, :])
                k.mm(bg[:, cs], wg2[:, cs], sgd)
            for c in range(4):
                cs = slice(c * 128, (c + 1) * 128)
                k.act(t_lw[:, c, :], bw[:, cs], AF.Exp, bias=nw0[:, c:c + 1], scale=-1.0)
                k.act(t_a[:, c, :], ba_[:, cs], AF.Sigmoid, bias=pv[:, 1, c:c + 1])
            k.cp('dve', t_g.rearrange('p a b -> p (a b)'), bg[:, :])
            fl = lambda a: a.rearrange('p a b -> p (a b)')
            k.act(fl(t_lw), fl(t_lw), AF.Ln, bias=1.0)
            k.act(fl(t_lw), fl(t_lw), AF.Exp, bias=-0.5, scale=-1.0)
            k.ts('dve', fl(t_lw), fl(t_lw), -1.0, ALU.mult)
            for c in range(4):
                k.ts('dve', t_kk[:, c, :], kx[:, c, :], pv[:, 2, c:c + 1], ALU.mult)
            k.tt('pool', t_tmp, t_kk, t_kk, ALU.mult)
            b = bank()
            for c in range(4):
                k.mm(b[:, c * 128:(c + 1) * 128], Bones, t_tmp[:, c, :])
            k.rsqrt(fl(t_tmp), b[:, :], L2_EPS)
            k.tt('dve', t_kk, t_kk, t_tmp, ALU.mult)
            for c in range(4):
                k.ts('dve', t_tmp[:, c, :], t_a[:, c, :], pv[:, 3, c:c + 1], ALU.mult, omka[:, c:c + 1], ALU.add)
            k.tt('dve', t_kp, kx, t_tmp, ALU.mult)
            k.tt('pool', t_b, t_kk, t_a, ALU.mult)
            k.P.op('dve', lambda e, rm=rmask[mi]: e.tensor_tensor_scan(out=fl(t_L), data0=rm, data1=fl(t_lw), initial=0.0, op0=ALU.mult, op1=ALU.add),
                 reads=[rmask[mi], fl(t_lw)], writes=[fl(t_L)])
            k.tt('dve', t_lw, t_L, t_lw, ALU.subtract)
            k.act(fl(t_eL), fl(t_L), AF.Exp)
            k.act(fl(t_lw), fl(t_lw), AF.Exp)
            k.act(fl(t_L), fl(t_L), AF.Exp, scale=-1.0)
            k.tt('dve', rt, r_, t_eL, ALU.mult)
            k.tt('dve', kkt, t_kk, t_lw, ALU.mult)
            k.tt('pool', bt, t_b, t_L, ALU.mult)
            k.tt('pool', kt, t_kp, t_L, ALU.mult)
            k.cp('pool', vb, v_)
            k.tt('dve', t_tmp, r_, t_kp, ALU.mult)
            for c in range(4):
                k.ts('dve', t_tmp[:, c, :], t_tmp[:, c, :], pv[:, 4, c:c + 1], ALU.mult)
            b = bank()
            for c in range(4):
                k.mm(b[:, c * 128:(c + 1) * 128], Bones, t_tmp[:, c, :])
            k.tt('dve', fl(t_bon), b[:, :], fl(v_), ALU.mult)
            for src, dst in ((kkt, KKtok), (bt, Btok), (kt, Ktok), (vb, Vtok)):
                b = bank()
                for c in range(4):
                    k.mm(b[:, c * 128:(c + 1) * 128], src[:, c, :], identB)
                k.cp('act', dst, b[:, :])
        def pre_gdn(bank):
            if samp:
                q4 = qkvp[:, :, 3:131].rearrange('p a (s t) -> p a s t', t=8)
                for j in range(3):
                    g4 = gsh[j].rearrange('p a (s t) -> p a s t', t=8)
                    for c in range(12):
                        k.cp('pool', g4[:, c, :, j + 1:8], q4[:, c, :, 0:7 - j])
                        k.cp('pool', g4[:, c, :, 0:j + 1], bcv_in[:, c, :, 2 - j:3])
                for c in range(12):
                    k.cp('pool', bcvs[:, c, :, :], q4[:, c, :, 5:8])
                k.dma('sp', d['o_sbconv'], bcvs)
            if i == NT - 2:
                k.dma('sp', d['o_pbconv'], qkvp[:, :, 128:131])
            for c in range(12):
                if samp:
                    X0, X1, X2, X3 = qkvp[:, c, 3:131], gsh[0][:, c, :], gsh[1][:, c, :], gsh[2][:, c, :]
                else:
                    X0, X1, X2, X3 = qkvp[:, c, 3:131], qkvp[:, c, 2:130], qkvp[:, c, 1:129], qkvp[:, c, 0:128]
                e = 'dve' if c % 2 else 'pool'
                k.ts(e, cv[:, c, :], X3, cwb[:, c, 0:1], ALU.mult)
                k.stt(e, cv[:, c, :], X2, cwb[:, c, 1:2], cv[:, c, :], ALU.mult, ALU.add)
                k.stt(e, cv[:, c, :], X1, cwb[:, c, 2:3], cv[:, c, :], ALU.mult, ALU.add)
                k.stt(e, cv[:, c, :], X0, cwb[:, c, 3:4], cv[:, c, :], ALU.mult, ALU.add)
            k.act(fl(cv), fl(cv), AF.Silu)
            k.tt('pool', tmp8, cv[:, 0:8, :], cv[:, 0:8, :], ALU.mult)
            b1, b2 = bank(), bank()
            for c in range(8):
                bb = b1 if c < 4 else b2
                k.mm(bb[:, (c % 4) * 128:(c % 4 + 1) * 128], onesF, tmp8[:, c, :])
            k.rsqrt(fl(tmp8[:, 0:4, :]), b1[:, :], L2_EPS)
            k.rsqrt(fl(tmp8[:, 4:8, :]), b2[:, :], L2_EPS)
            k.stt('dve', qh, cv[:, 0:4, :], 128.0 ** -0.5, tmp8[:, 0:4, :], ALU.mult, ALU.mult)
            k.cp('pool', qhb, qh)
            k.tt('dve', khb, cv[:, 4:8, :], tmp8[:, 4:8, :], ALU.mult)
            k.cp('pool', vgb, cv[:, 8:12, :])
            k.act(beta, batok[:, 0:4], AF.Sigmoid)
            k.tt('dve', gtk, batok[:, 4:8], dtb, ALU.add)
            k.act(gtk, gtk, AF.Exp)
            k.act(gtk, gtk, AF.Ln, bias=1.0)
            k.tt('dve', gtk, gtk, negA, ALU.mult)
            b = bank()
            k.mm(b[:, 0:4], MI[mi], gtk)
            k.mm(b[:, 4:8], MST[mi], gtk)
            k.cp('dve', Gtok, b[:, 0:4])
            k.act(eGtok, b[:, 0:4], AF.Exp)
            k.act(eGs, b[:, 4:8], AF.Exp)
            k.tt('dve', beG, beta, eGtok, ALU.mult)
            bk_, bv_ = bank(), bank()
            for h in range(4):
                k.mm(bk_[:, h * 128:(h + 1) * 128], khb[:, h, :], identB)
                k.mm(bv_[:, h * 128:(h + 1) * 128], vgb[:, h, :], identB)
            for h in range(4):
                hs = slice(h * 128, (h + 1) * 128)
                k.ts('dve', Win[:, h, :], bk_[:, hs], beG[:, h:h + 1], ALU.mult)
                k.ts('dve', Kdg[:, h, :], bk_[:, hs], eGs[:, h:h + 1], ALU.mult)
                k.ts('dve', Yin[:, h, :], bv_[:, hs], beta[:, h:h + 1], ALU.mult)

        if samp or cfg.get('nopre'):
            pre_rwkv(bank); pre_gdn(bank)
        else:
            lists = [record_calls(k, lambda: pre_rwkv(pbank[0])), record_calls(k, lambda: pre_gdn(pbank[1]))]
            replay_interleaved(k, lists)
        def vh_gen(vh, B, bank):
            Um, Lm, Ua, La, Ub, Lb, Qm, Qn = B['inv']
            TTb, ATb, A2Tb = B['TTb'], B['ATb'], B['A2Tb']
            AakT, XYin, nwv, uv = B['AakT'], B['XYin'], B['nwv'], B['uv']
            ZT, OcT, OT, PhiT, Psi, Idg = B['ZT'], B['OcT'], B['OT'], B['PhiT'], B['Psi'], B['Idg']
            gtmp = B['gtmp']
            gB, bB, tmpD, E1, E2 = B['gd']

            def invert(TTout):
                k.tt('dve', Qm, identF, Um, ALU.subtract)
                Uk, Lk, Un, Ln_ = Um, Lm, Ua, La
                Q, Q2 = Qm, Qn
                for lev in range(nlev):
                    b = bank()
                    k.mm(b[:, 0:128], Uk, Lk)
                    if lev < nlev - 1:
                        k.mm(b[:, 128:256], Lk, Uk)
                    yield
                    k.cp('act', Ln_, b[:, 0:128])
                    if lev < nlev - 1:
                        k.cp('act', Un, b[:, 128:256])
                    yield
                    b2 = bank()
                    k.mm(b2[:, 0:128], Ln_, Q)
                    yield
                    k.tt('dve', Q2, b2[:, 0:128], Q, ALU.add)
                    Q, Q2 = Q2, Q
                    if Un is Ua:
                        Uk, Lk, Un, Ln_ = Ua, La, Ub, Lb
                    else:
                        Uk, Lk, Un, Ln_ = Ub, Lb, Ua, La
                    yield
                k.cp('act', TTout, Q)

            rw = vh < 4
            if rw:
                hp = vh
                subs = []
                for h2 in range(2):
                    rs = slice(h2 * 64, (h2 + 1) * 64)
                    hc = slice(hp * 128 + h2 * 64, hp * 128 + (h2 + 1) * 64)
                    b = bank()
                    k.mm(b[:, 0:128], bt[rs, hp, :], kkt[rs, hp, :])
                    k.mm(b[:, 128:256], kkt[rs, hp, :], bt[rs, hp, :])
                    k.mm(b[:, 256:384], kt[rs, hp, :], kkt[rs, hp, :])
                    yield
                    k.tt('dve', Um, b[:, 0:128], MS[mi], ALU.mult)
                    k.tt('dve', Lm, b[:, 128:256], MST[mi], ALU.mult)
                    k.tt('dve', AakT, b[:, 256:384], MS[mi], ALU.mult)
                    b = bank()
                    k.mm(b[:, 0:128], bt[rs, hp, :], rt[rs, hp, :])
                    k.mm(b[:, 128:256], kt[rs, hp, :], rt[rs, hp, :])
                    yield
                    k.tt('dve', ATb[h2], b[:, 0:128], MI[mi], ALU.mult)
                    k.tt('dve', A2Tb[h2], b[:, 128:256], MI[mi], ALU.mult)
                    b = bank()
                    k.mm(b[:, 0:64], AakT, Vtok[:, hc])
                    k.cp('pool', XYin[:, 0:64], KKtok[:, hc])
                    yield
                    k.act(XYin[:, 64:128], b[:, 0:64], AF.Copy, scale=-1.0)
                    yield from invert(TTb[h2])
                    yield
                    b = bank()
                    k.mm(b[:, 0:128], TTb[h2], XYin[:, 0:128])
                    yield
                    k.act(nwv[:, rs], b[:, 0:64], AF.Copy, scale=-1.0)
                    k.cp('act', uv[:, rs], b[:, 64:128])
                    subs.append((h2, rs))
                    yield
                Kd = Btok[:, hp * 128:(hp + 1) * 128]
                QT = rt[:, hp, :]
                BD = Bones
            else:
                h = vh - 4
                k.ts('dve', gB, onesF, gtk[:, h:h + 1], ALU.mult)
                k.ts('dve', bB, onesF, beta[:, h:h + 1], ALU.mult)
                bG = bank()
                k.mm(bG[:, 0:128], gB, MI[mi])
                k.mm(bG[:, 128:256], bB, identF)
                bKK = bank()
                k.mm(bKK[:, 0:128], khb[:, h, :], khb[:, h, :])
                k.mm(bKK[:, 128:256], khb[:, h, :], qhb[:, h, :])
                yield
                k.act(eGbc[h], bG[:, 0:128], AF.Exp)
                k.ts('dve', tmpD, bG[:, 0:128], Gtok[:, h:h + 1], ALU.subtract)
                k.ts('dve', E1, tmpD, 0.0, ALU.min)
                k.ts('dve', E2, tmpD, -1.0, ALU.mult, 0.0, ALU.min)
                yield
                k.act(E1, E1, AF.Exp)
                k.act(E2, E2, AF.Exp)
                k.tt('dve', QTg[:, h, :], qh[:, h, :], eGbc[h], ALU.mult)
                yield
                k.tt('dve', Um, E1, MS[mi], ALU.mult)
                k.tt('dve', Um, Um, bG[:, 128:256], ALU.mult)
                k.tt('dve', Um, Um, bKK[:, 0:128], ALU.mult)
                yield
                k.tt('dve', tmpD, E1, MI[mi], ALU.mult)
                k.tt('dve', ATb[0], tmpD, bKK[:, 128:256], ALU.mult)
                k.stt('dve', Lm, E2, beta[:, h:h + 1], MST[mi], ALU.mult, ALU.mult)
                k.tt('dve', Lm, Lm, bKK[:, 0:128], ALU.mult)
                yield
                yield from invert(TTb[0])
                k.cp('pool', XYin[:, 0:128], Win[:, h, :])
                k.cp('pool', XYin[:, 128:256], Yin[:, h, :])
                yield
                b = bank()
                k.mm(b[:, 0:256], TTb[0], XYin[:, 0:256])
                yield
                k.act(nwv, b[:, 0:128], AF.Copy, scale=-1.0)
                k.cp('act', uv, b[:, 128:256])
                subs = [(0, slice(0, 128))]
                Kd = Kdg[:, h, :]
                QT = QTg[:, h, :]
                BD = None
                yield
            bZ, bO = bank(), bank()
            for (h2, rs) in subs:
                k.mm(bZ[rs, 0:128], nwv[:, rs], ATb[h2])
                k.mm(bO[rs, 0:128], uv[:, rs], ATb[h2], start=True, stop=(not rw))
                if rw:
                    hc = slice(vh * 128 + h2 * 64, vh * 128 + (h2 + 1) * 64)
                    k.mm(bO[rs, 0:128], Vtok[:, hc], A2Tb[h2], start=False, stop=True)
            yield
            k.tt('dve', ZT, bZ[:, 0:128], QT, ALU.add)
            k.cp('act', OcT, bO[:, 0:128])
            yield
            if samp:
                k.dma('sp', Sin, d['s_state'][vh])
                for j in range(16):
                    k.ts('dve', KdM[:, j, :], Kd, segm[:, j:j + 1], ALU.mult)
                    if rw:
                        k.ts('dve', KtM[:, j, :], Ktok[:, vh * 128:(vh + 1) * 128], segm[:, j:j + 1], ALU.mult)
            for sg in range(nseg):
                cols = slice(sg * blk, (sg + 1) * blk)
                last = sg * blk + blk - 1
                Scur = Sin[:, sg, :] if samp else S[vh]
                Snew = Sout[:, sg, :] if samp else S[vh]
                if rw:
                    cP = t_eL[:, vh, last:last + 1]
                    IdgX = identF
                else:
                    cP = onecol
                    k.ts('dve', Idg, identF, eGbc[vh - 4][:, last:last + 1], ALU.mult)
                    IdgX = Idg
                if samp:
                    nwS, KdS, uS = nwv, KdM[:, sg, :], uv
                else:
                    rows = slice(sg * 64, (sg + 1) * 64)
                    nwS, KdS, uS = nwv[rows, :], Kd[rows, :], uv[rows, :]
                bP = bank()
                k.mm(bP[:, 0:128], nwS, KdS)
                k.mm(bP[:, 128:256], KdS, uS, start=True, stop=(not rw))
                if rw:
                    if samp:
                        k.mm(bP[:, 128:256], KtM[:, sg, :], Vtok[:, vh * 128:(vh + 1) * 128], start=False, stop=True)
                    else:
                        k.mm(bP[:, 128:256], Ktok[rows, vh * 128:(vh + 1) * 128], Vtok[rows, vh * 128:(vh + 1) * 128], start=False, stop=True)
                yield
                if BD is not None:
                    k.tt('dve', PhiT, bP[:, 0:128], BD, ALU.mult)
                    k.tt('dve', PhiT, PhiT, IdgX, ALU.add)
                    k.stt('dve', Psi, bP[:, 128:256], cP, BD, ALU.mult, ALU.mult)
                else:
                    k.tt('dve', PhiT, bP[:, 0:128], IdgX, ALU.add)
                    k.cp('act', Psi, bP[:, 128:256])
                yield
                bS = bank()
                k.mm(bS[:, 0:blk], Scur, ZT[:, cols])
                k.mm(bS[:, 128:256], PhiT, Scur)
                yield
                k.tt('dve', OT[:, cols], bS[:, 0:blk], OcT[:, cols], ALU.add)
                k.stt('dve', Snew, bS[:, 128:256], cP, Psi, ALU.mult, ALU.add)
                yield
            if samp:
                k.dma('sp', d['o_sstate'][vh], Sout)
            if rw:
                k.tt('pool', gtmp[0], OT, OT, ALU.mult)
                b = bank()
                k.mm(b[:, 0:128], Bones, OT)
                k.mm(b[:, 128:256], Bones, gtmp[0])
                yield
                k.act(gtmp[1], b[:, 0:128], AF.Copy, scale=1.0 / 64)
                k.tt('dve', gtmp[2], gtmp[1], gtmp[1], ALU.mult)
                k.stt('dve', gtmp[2], b[:, 128:256], 1.0 / 64, gtmp[2], ALU.mult, ALU.subtract)
                yield
                k.rsqrt(gtmp[2], gtmp[2], GN_EPS)
                k.tt('dve', gtmp[1], OT, gtmp[1], ALU.subtract)
                yield
                k.tt('dve', gtmp[1], gtmp[1], gtmp[2], ALU.mult)
                k.ts('dve', gtmp[1], gtmp[1], pv[:, 5, vh:vh + 1], ALU.mult, pv[:, 6, vh:vh + 1], ALU.add)
                yield
                k.tt('pool', gtmp[1], gtmp[1], t_bon[:, vh, :], ALU.add)
                k.tt('pool', oT[:, vh, :], gtmp[1], t_g[:, vh, :], ALU.mult)
            else:
                h = vh - 4
                k.tt('pool', gtmp[0], OT, OT, ALU.mult)
                b = bank()
                k.mm(b[:, 0:128], onesF, gtmp[0])
                yield
                k.act(gtmp[2], b[:, 0:128], AF.Copy, scale=1.0 / 128)
                k.rsqrt(gtmp[2], gtmp[2], RMS_EPS)
                yield
                k.tt('dve', gtmp[1], OT, gtmp[2], ALU.mult)
                k.stt('dve', oT[:, vh, :], gtmp[1], ng[:, 0:1], zT[:, h, :], ALU.mult, ALU.mult)
            yield

        def lane(vhs, B, bankf):
            for vh in vhs:
                yield from vh_gen(vh, B, bankf)

        if samp or cfg.get('nolanes'):
            lanes = [lane(range(8), BS[0], bank)]
        else:
            lanes = [lane([l, 4 + l], BS[l], lbank[l]) for l in range(4)]
        while lanes:
            for g in list(lanes):
                try:
                    next(g)
                except StopIteration:
                    lanes.remove(g)
        b1, b2 = bank(), bank()
        for half, bb in ((0, b1), (1, b2)):
            for kc in range(8):
                k.mm(bb[:, :], oT[:, kc, :], wo[:, kc, half * 512:(half + 1) * 512], start=(kc == 0), stop=(kc == 7))
        k.stt('dve', z1[:, 0:512], xt[:, 0:512], DN_ALPHA, b1[:, :], ALU.mult, ALU.add)
        k.stt('dve', z1[:, 512:1024], xt[:, 512:1024], DN_ALPHA, b2[:, :], ALU.mult, ALU.add)
        k.rownorm_stats(z1, mv, stats)
        k.rsqrt(rstd, mv[:, 1:2], LN_EPS)
        k.ts('dve', z1, z1, mv[:, 0:1], ALU.subtract, rstd[:, 0:1], ALU.mult)
        k.tt('dve', z1, z1, l1g, ALU.mult)
        k.tt('dve', z1, z1, l1b, ALU.add)
        k.dma('sp', d['x1s'][r0:r0 + 128, :], z1)
        if cfg.get('dbg'):
            k.dma('sp', d['dbg_oT'][i], oT)
    for v in range(8):
        k.dma('sp', d['o_pstate'][v], S[v])


def consts():
    c = {}
    c['identF'] = np.eye(128, dtype=np.float32)
    MS = np.zeros((2, 128, 128), np.float32); MI = np.zeros((2, 128, 128), np.float32)
    rmask = np.ones((2, 128, 512), np.float32)
    for i, blk in enumerate((64, 8)):
        s = np.arange(128)[:, None]; t = np.arange(128)[None, :]
        same = (s // blk) == (t // blk)
        MS[i] = (same & (s < t)); MI[i] = (same & (s <= t))
        tt = np.arange(512)
        rmask[i][:, (tt % blk) == 0] = 0.0
    c['MS'] = MS; c['MI'] = MI; c['MST'] = np.ascontiguousarray(MS.transpose(0, 2, 1)); c['rmask'] = rmask
    p = np.arange(128)
    c['Bones'] = ((p[:, None] // 64) == (p[None, :] // 64)).astype(np.float32)
    c['onesF'] = np.ones((128, 128), np.float32)
    c['segm'] = ((p[:, None] // 8) == np.arange(16)[None, :]).astype(np.float32)
    return c


def fm(v, nch):
    return np.ascontiguousarray(v.reshape(nch, 128).T)


def weights(inp):
    w = {}
    w['w_in'] = inp['w_in'][0].reshape(8, 128, 3848)
    w['w_o'] = inp['w_o'][0].reshape(8, 128, 1024)
    w['w_up'] = inp['w_up'][0].reshape(8, 128, 5632)
    w['w_down'] = inp['w_down'][0].reshape(22, 128, 1024)
    w['w_ple_gate'] = inp['w_ple_gate'][0].reshape(8, 128, 1024)
    w['w_ple'] = inp['w_ple'][0].reshape(2, 128, 1024)
    w['f_conv_w'] = np.ascontiguousarray(inp['f_conv_w'][0].reshape(3, 22, 128).transpose(2, 1, 0))
    w['f_conv_b'] = fm(inp['f_conv_b'][0], 22)
    for n in ('ln1_g', 'ln1_b', 'ln2_g', 'ln2_b', 'ple_g'):
        w[n] = np.ascontiguousarray(inp[n][0])
    w['a_mu'] = fm(inp['a_mu'][0], 14)
    pv = np.stack([inp[n][0].reshape(4, 128) for n in ('a_w0', 'a_a0', 'a_k_k', 'a_k_a', 'a_r_k', 'a_gn_g', 'a_gn_b')], 0)
    w['a_pv'] = np.ascontiguousarray(pv.transpose(2, 0, 1))
    w['a_wlora'] = np.ascontiguousarray(np.concatenate([inp['a_w_w2'][0], inp['a_w_a2'][0]], 0))
    w['a_w_g2'] = np.ascontiguousarray(inp['a_w_g2'][0])
    w['b_conv_w'] = np.ascontiguousarray(inp['b_conv_w'][0].reshape(4, 12, 128).transpose(2, 1, 0))
    w['b_a_log'] = np.ascontiguousarray(inp['b_a_log'][0]); w['b_dt_bias'] = np.ascontiguousarray(inp['b_dt_bias'][0])
    w['b_norm_g'] = np.ascontiguousarray(inp['b_norm_g'][0].reshape(128, 1))
    return w


def percore(inp, c):
    m = {}
    sl = slice(16 * c, 16 * c + 16)
    x = np.concatenate([inp['x_prompt'][c], inp['x_sample'][sl].reshape(128, 1024)], 0)
    m['x'] = np.ascontiguousarray(x)
    m['xT'] = np.ascontiguousarray(x.T.reshape(8, 128, TOK))
    p = np.concatenate([inp['p_prompt'][0, c], inp['p_sample'][0, sl].reshape(128, 256)], 0)
    m['pT'] = np.ascontiguousarray(p.T.reshape(2, 128, TOK))
    m['s_fconv'] = np.ascontiguousarray(inp['state_ffn_conv'][0, sl].reshape(16, 2, 22, 128).transpose(3, 2, 0, 1))
    m['s_ashift'] = np.ascontiguousarray(inp['state_a_shift'][0, sl].reshape(16, 14, 128).transpose(2, 1, 0))
    m['s_bconv'] = np.ascontiguousarray(inp['state_b_conv'][0, sl].reshape(16, 3, 12, 128).transpose(3, 2, 0, 1))
    st = np.zeros((8, 128, 16, 128), np.float32)
    wkv = inp['state_a_wkv'][0, sl]
    for hp in range(4):
        for h2 in range(2):
            blkk = wkv[:, hp * 2 + h2].transpose(2, 0, 1)
            st[hp, h2 * 64:(h2 + 1) * 64, :, h2 * 64:(h2 + 1) * 64] = blkk
    ssm = inp['state_b_ssm'][0, sl]
    for h in range(4):
        st[4 + h] = ssm[:, h].transpose(1, 0, 2)
    m['s_state'] = st
    return m


def unpack(r):
    o = {}
    o['y_p'] = r['y'][:2048]; o['y_s'] = r['y'][2048:].reshape(16, 8, 1024)
    ps = r['o_pstate']
    o['pa_wkv'] = np.stack([ps[h // 2][(h % 2) * 64:(h % 2 + 1) * 64, (h % 2) * 64:(h % 2 + 1) * 64].T for h in range(8)], 0)
    o['pb_ssm'] = ps[4:8]
    o['pa_shift'] = r['o_pashift'].reshape(128, 14).T.reshape(1792)
    o['pb_conv'] = r['o_pbconv'].transpose(2, 1, 0).reshape(3, 1536)
    o['pf_conv'] = r['o_pfconv'].transpose(2, 1, 0).reshape(2, 2816)
    ss = r['o_sstate']
    o['sa_wkv'] = np.stack([ss[h // 2][(h % 2) * 64:(h % 2 + 1) * 64, :, (h % 2) * 64:(h % 2 + 1) * 64].transpose(1, 2, 0) for h in range(8)], 1)
    o['sb_ssm'] = ss[4:8].transpose(2, 0, 1, 3)
    o['sa_shift'] = r['o_sashift'].transpose(2, 1, 0).reshape(16, 1792)
    o['sb_conv'] = r['o_sbconv'].transpose(2, 3, 1, 0).reshape(16, 3, 1536)
    o['sf_conv'] = r['o_sfconv'].transpose(2, 3, 1, 0).reshape(16, 2, 2816)
    return o


from concourse.bass_utils import run_bass_kernel_spmd

_CACHE = {}


def build():
    nc = bass.Bass("TRN2", target_bir_lowering=False)
    CS = consts()
    with contextlib.ExitStack() as st:
        P = Prog(nc)
        k = KB(nc, P, st)
        k.din('x', [TOK, 1024]); k.din('xT', [8, 128, TOK]); k.din('pT', [2, 128, TOK])
        k.din('w_in', [8, 128, 3848]); k.din('w_o', [8, 128, 1024])
        k.din('w_up', [8, 128, 5632]); k.din('w_down', [22, 128, 1024]); k.din('w_ple_gate', [8, 128, 1024]); k.din('w_ple', [2, 128, 1024])
        for n, v in CS.items():
            k.din(n, v.shape)
        k.din('a_mu', [128, 14]); k.din('a_pv', [128, 7, 4]); k.din('a_wlora', [128, 512]); k.din('a_w_g2', [128, 512])
        k.din('b_conv_w', [128, 12, 4]); k.din('b_a_log', [4]); k.din('b_dt_bias', [4]); k.din('b_norm_g', [128, 1])
        k.din('ln1_g', [1024]); k.din('ln1_b', [1024]); k.din('ln2_g', [1024]); k.din('ln2_b', [1024]); k.din('ple_g', [1024])
        k.din('f_conv_w', [128, 22, 3]); k.din('f_conv_b', [128, 22])
        k.din('s_ashift', [128, 14, 16]); k.din('s_bconv', [128, 12, 16, 3]); k.din('s_state', [8, 128, 16, 128]); k.din('s_fconv', [128, 22, 16, 2])
        k.dint('x1s', [TOK, 1024])
        k.dint('w_in_bf', [8, 128, 3848], BF16)
        k.dout('y', [TOK, 1024])
        k.dout('o_pstate', [8, 128, 128]); k.dout('o_sstate', [8, 128, 16, 128])
        k.dout('o_pashift', [128, 14, 1]); k.dout('o_sashift', [128, 14, 16]); k.dout('o_pbconv', [128, 12, 3]); k.dout('o_sbconv', [128, 12, 16, 3])
        k.dout('o_pfconv', [128, 22, 2]); k.dout('o_sfconv', [128, 22, 16, 2])
        cfg = {}
        k.init_arena()
        idf = k.sb('identF_sb', [128, 128]); k.dma('sp', idf, k.d['identF']); cfg['identF'] = idf
        mark = k.acur
        pss = [k.ps('ps' + n, [128, 1024]) for n in 'ABCD']
        cfg['banks'] = [p[:, h * 512:(h + 1) * 512] for p in pss for h in range(2)]
        for n, p in zip('ABCD', pss):
            cfg['ps' + n] = p
        phase_A(k, cfg)
        P.barrier()
        k.arena_reset(mark)
        phase_B(k, cfg)
        P.final_wait_all_dma('sp')
        P.emit()
    return nc, CS


def kernel(**inputs):
    inp = {kk: np.asarray(v) for kk, v in inputs.items()}
    if 'nc' not in _CACHE:
        _CACHE['nc'] = build()
    nc, CS = _CACHE['nc']
    W = weights(inp)
    in_maps = []
    for c in range(8):
        m = {}
        m.update(CS)
        m.update(W)
        m.update(percore(inp, c))
        in_maps.append(m)
    res = run_bass_kernel_spmd(nc, in_maps, core_ids=list(range(8)))
    us = [unpack(r) for r in res.results]
    f = np.float32
    out = (
        np.stack([u['y_p'] for u in us], 0).astype(f),
        np.concatenate([u['y_s'] for u in us], 0).astype(f),
        np.stack([u['pa_wkv'] for u in us], 0)[None].astype(f),
        np.stack([u['pa_shift'] for u in us], 0)[None].astype(f),
        np.stack([u['pb_ssm'] for u in us], 0)[None].astype(f),
        np.stack([u['pb_conv'] for u in us], 0)[None].astype(f),
        np.stack([u['pf_conv'] for u in us], 0)[None].astype(f),
        np.concatenate([u['sa_wkv'] for u in us], 0)[None].astype(f),
        np.concatenate([u['sa_shift'] for u in us], 0)[None].astype(f),
        np.concatenate([u['sb_ssm'] for u in us], 0)[None].astype(f),
        np.concatenate([u['sb_conv'] for u in us], 0)[None].astype(f),
        np.concatenate([u['sf_conv'] for u in us], 0)[None].astype(f),
    )
    return tuple(np.ascontiguousarray(o) for o in out)
```
